# Optimizing a Trainium2 kernel written in Bass

```python
import jax
import jax.numpy as jnp
from jax import lax
import numpy as np

D_MODEL = 1024
BATCH = 16
SEQ = 2048
DEPTH = 2
DEC_BATCH = 4
DEC_SEQ = 4096
PAST_LEN = 128

GRID_W = 64
EPS = 1e-6

SSD_HEADS = 16
SSD_HEAD_DIM = 64
SSD_WIDTH = SSD_HEADS * SSD_HEAD_DIM
SSD_GROUPS = 4
SSD_STATE = 128
SSD_CONV = 5
SSD_CHUNK = 128
SSD_CONV_CH = SSD_WIDTH + 2 * SSD_GROUPS * SSD_STATE
DT_MIN = 0.001
DT_MAX = 0.1

GLA_HEADS = 4
GLA_DK = 64
GLA_DV = 128
GLA_QK_WIDTH = GLA_HEADS * GLA_DK
GLA_WIDTH = GLA_HEADS * GLA_DV
GLA_RANK = 16
GLA_TAU = 16.0
GLA_CHUNK = 64

ATT_HEADS = 8
ATT_KV_HEADS = 2
ATT_HEAD_DIM = 64
ATT_WIDTH = ATT_HEADS * ATT_HEAD_DIM
ATT_KV_WIDTH = ATT_KV_HEADS * ATT_HEAD_DIM
ROPE_THETA = 10000.0
ROPE_N_FREQ = ATT_HEAD_DIM // 4
Q_BLOCK = 128

N_BRANCH = 3
D_FF = -(-8 * D_MODEL // (3 * 256)) * 256

IN_SIZES = (SSD_WIDTH, SSD_CONV_CH, 2 * SSD_HEADS,
            GLA_QK_WIDTH, GLA_QK_WIDTH, GLA_WIDTH, GLA_WIDTH, 2 * GLA_RANK,
            ATT_WIDTH, ATT_KV_WIDTH, ATT_KV_WIDTH,
            N_BRANCH * D_MODEL)
IN_WIDTH = (SSD_WIDTH + SSD_CONV_CH + 2 * SSD_HEADS
            + 2 * GLA_QK_WIDTH + 2 * GLA_WIDTH + 2 * GLA_RANK
            + ATT_WIDTH + 2 * ATT_KV_WIDTH + N_BRANCH * D_MODEL)

kernel_name = 'hybrid_ssd_gla_axialgqa_encoder'


def _rmsnorm(x, w):
    xf = x.astype(jnp.float32)
    y = xf * lax.rsqrt(jnp.mean(xf * xf, axis=-1, keepdims=True) + EPS)
    return (y * w.astype(jnp.float32)).astype(x.dtype)


def _split(t, sizes):
    cuts = []
    acc = 0
    for s in sizes[:-1]:
        acc += s
        cuts.append(acc)
    return jnp.split(t, cuts, axis=-1)


def _rev(t):
    return jnp.flip(t, axis=1)


def _centred_depthwise_conv(x, w, b):
    y = lax.conv_general_dilated(
        x, w[:, None, :].astype(x.dtype), window_strides=(1,),
        padding=[(SSD_CONV // 2, SSD_CONV // 2)],
        dimension_numbers=('NWC', 'WIO', 'NWC'),
        feature_group_count=x.shape[-1])
    return y + b.astype(x.dtype)


def _segsum_exp(a):
    cs = jnp.cumsum(a, axis=-1)
    q = a.shape[-1]
    mask = jnp.tril(jnp.ones((q, q), dtype=bool))
    return jnp.exp(jnp.where(mask, cs[..., :, None] - cs[..., None, :], -jnp.inf))


def _ssd_chunked_scan(x, dt, a, b_mat, c_mat):
    bsz, length, n_heads, p = x.shape
    g, n = b_mat.shape[-2:]
    hg = n_heads // g
    nc = length // SSD_CHUNK
    xdt = (x * dt[..., None]).reshape(bsz, nc, SSD_CHUNK, g, hg, p)
    adt = jnp.moveaxis((dt * a).reshape(bsz, nc, SSD_CHUNK, g, hg), 2, -1)
    bc = b_mat.reshape(bsz, nc, SSD_CHUNK, g, n)
    cc = c_mat.reshape(bsz, nc, SSD_CHUNK, g, n)
    a_cs = jnp.cumsum(adt, axis=-1)
    decay_mat = _segsum_exp(adt)
    scores = jnp.einsum('bclgn,bcsgn->bcgls', cc, bc)
    w = scores[:, :, :, None] * decay_mat
    y_diag = jnp.einsum('bcghls,bcsghp->bclghp', w, xdt)
    decay_to_end = jnp.moveaxis(jnp.exp(a_cs[..., -1:] - a_cs), -1, 2)
    states = jnp.einsum('bclgn,bclghp->bcghpn', bc, xdt * decay_to_end[..., None])
    chunk_decay = jnp.exp(a_cs[..., -1])

    def step(s, inp):
        st, dec = inp
        return s * dec[..., None, None] + st, s

    init = jnp.zeros((bsz, g, hg, p, n), jnp.float32)
    _, prev = lax.scan(step, init, (jnp.moveaxis(states, 1, 0), jnp.moveaxis(chunk_decay, 1, 0)))
    prev = jnp.moveaxis(prev, 0, 1)
    y_off = jnp.einsum('bclgn,bcghpn->bclghp', cc, prev) * jnp.moveaxis(jnp.exp(a_cs), -1, 2)[..., None]
    return (y_diag + y_off).reshape(bsz, length, n_heads, p)


def _ssd_branch(z, xbc, dt_raw, conv_w, conv_b, a_log_f, a_log_b, dt_bias_f, dt_bias_b, d_skip, norm_w):
    f32 = jnp.float32
    bsz, length = z.shape[:2]
    xbc = jax.nn.silu(_centred_depthwise_conv(xbc, conv_w, conv_b)).astype(f32)
    xs, bm, cm = _split(xbc, (SSD_WIDTH, SSD_GROUPS * SSD_STATE, SSD_GROUPS * SSD_STATE))
    xs = xs.reshape(bsz, length, SSD_HEADS, SSD_HEAD_DIM)
    bm = bm.reshape(bsz, length, SSD_GROUPS, SSD_STATE)
    cm = cm.reshape(bsz, length, SSD_GROUPS, SSD_STATE)
    dt_raw = dt_raw.astype(f32)
    dt_f = jax.nn.softplus(dt_raw[..., :SSD_HEADS] + dt_bias_f.astype(f32))
    dt_b = jax.nn.softplus(dt_raw[..., SSD_HEADS:] + dt_bias_b.astype(f32))
    a_f = -jnp.exp(a_log_f.astype(f32))
    a_b = -jnp.exp(a_log_b.astype(f32))
    y_f = _ssd_chunked_scan(xs, dt_f, a_f, bm, cm)
    y_b = _rev(_ssd_chunked_scan(_rev(xs), _rev(dt_b), a_b, _rev(bm), _rev(cm)))
    y = y_f + y_b + xs * d_skip.astype(f32)[:, None]
    y = y.reshape(bsz, length, SSD_WIDTH) * jax.nn.silu(z.astype(f32))
    return _rmsnorm(y, norm_w).astype(z.dtype)


def _gla_chunked_scan(q, k, v, g):
    bsz, length, n_heads, dk = q.shape
    dv = v.shape[-1]
    nc = length // GLA_CHUNK
    q = q.reshape(bsz, nc, GLA_CHUNK, n_heads, dk)
    k = k.reshape(bsz, nc, GLA_CHUNK, n_heads, dk)
    v = v.reshape(bsz, nc, GLA_CHUNK, n_heads, dv)
    g = g.reshape(bsz, nc, GLA_CHUNK, n_heads, dk)
    bcum = jnp.cumsum(g, axis=2)
    ref = bcum[:, :, GLA_CHUNK // 2:GLA_CHUNK // 2 + 1]
    att = jnp.einsum('bcihk,bcjhk->bchij', q * jnp.exp(bcum - ref), k * jnp.exp(ref - bcum))
    mask = jnp.tril(jnp.ones((GLA_CHUNK, GLA_CHUNK), dtype=bool))
    o_intra = jnp.einsum('bchij,bcjhv->bcihv', jnp.where(mask, att, 0.0), v)
    last = bcum[:, :, -1:]
    s_chunk = jnp.einsum('bcjhk,bcjhv->bchkv', k * jnp.exp(last - bcum), v)
    chunk_decay = jnp.exp(last[:, :, 0])

    def step(s, inp):
        st, dec = inp
        return s * dec[..., None] + st, s

    init = jnp.zeros((bsz, n_heads, dk, dv), jnp.float32)
    _, prev = lax.scan(step, init, (jnp.moveaxis(s_chunk, 1, 0), jnp.moveaxis(chunk_decay, 1, 0)))
    prev = jnp.moveaxis(prev, 0, 1)
    o_inter = jnp.einsum('bcihk,bchkv->bcihv', q * jnp.exp(bcum), prev)
    return (o_intra + o_inter).reshape(bsz, length, n_heads, dv)


def _gla_branch(q, k, v, out_gate, lowrank, w2_f, b_f, w2_b, b_b, norm_w):
    f32 = jnp.float32
    bsz, length = q.shape[:2]
    q = q.astype(f32).reshape(bsz, length, GLA_HEADS, GLA_DK) * (GLA_DK ** -0.5)
    k = k.astype(f32).reshape(bsz, length, GLA_HEADS, GLA_DK)
    v = v.astype(f32).reshape(bsz, length, GLA_HEADS, GLA_DV)
    lowrank = lowrank.astype(f32)
    g_f = jax.nn.log_sigmoid(lowrank[..., :GLA_RANK] @ w2_f.astype(f32) + b_f.astype(f32)) / GLA_TAU
    g_b = jax.nn.log_sigmoid(lowrank[..., GLA_RANK:] @ w2_b.astype(f32) + b_b.astype(f32)) / GLA_TAU
    g_f = g_f.reshape(bsz, length, GLA_HEADS, GLA_DK)
    g_b = g_b.reshape(bsz, length, GLA_HEADS, GLA_DK)
    o = _gla_chunked_scan(q, k, v, g_f) + _rev(_gla_chunked_scan(_rev(q), _rev(k), _rev(v), _rev(g_b)))
    o = _rmsnorm(o, norm_w).reshape(bsz, length, GLA_WIDTH)
    return (o * jax.nn.silu(out_gate.astype(f32))).astype(out_gate.dtype)


def _axial_rope_tables(length):
    rows = length // GRID_W
    row = jnp.broadcast_to(jnp.arange(rows, dtype=jnp.float32)[:, None], (rows, GRID_W)).reshape(length)
    col = jnp.broadcast_to(jnp.arange(GRID_W, dtype=jnp.float32)[None, :], (rows, GRID_W)).reshape(length)
    inv = jnp.float32(ROPE_THETA) ** (-jnp.arange(ROPE_N_FREQ, dtype=jnp.float32) / ROPE_N_FREQ)
    ang = jnp.stack([row[:, None] * inv, col[:, None] * inv], axis=1)
    return jnp.cos(ang), jnp.sin(ang)


def _apply_axial_rope(x, cos, sin):
    bsz, length, n_heads, d = x.shape
    xr = x.astype(jnp.float32).reshape(bsz, length, n_heads, 2, 2, ROPE_N_FREQ)
    x1 = xr[..., 0, :]
    x2 = xr[..., 1, :]
    c = cos[None, :, None]
    s = sin[None, :, None]
    out = jnp.stack([x1 * c - x2 * s, x2 * c + x1 * s], axis=-2)
    return out.reshape(bsz, length, n_heads, d)


def _attn_branch(q, k, v, q_norm_w, k_norm_w):
    f32 = jnp.float32
    bsz, length = q.shape[:2]
    grp = ATT_HEADS // ATT_KV_HEADS
    q = _rmsnorm(q.astype(f32).reshape(bsz, length, ATT_HEADS, ATT_HEAD_DIM), q_norm_w)
    k = _rmsnorm(k.astype(f32).reshape(bsz, length, ATT_KV_HEADS, ATT_HEAD_DIM), k_norm_w)
    v_h = v.astype(f32).reshape(bsz, length, ATT_KV_HEADS, ATT_HEAD_DIM)
    cos, sin = _axial_rope_tables(length)
    q = _apply_axial_rope(q, cos, sin) * (ATT_HEAD_DIM ** -0.5)
    k = _apply_axial_rope(k, cos, sin)
    qb = q.reshape(bsz, length // Q_BLOCK, Q_BLOCK, ATT_KV_HEADS, grp, ATT_HEAD_DIM)
    qb = jnp.moveaxis(qb, 1, 0)

    def block(qi):
        s = jnp.einsum('bqkgd,bskd->bkgqs', qi, k)
        p = jax.nn.softmax(s, axis=-1)
        return jnp.einsum('bkgqs,bskd->bqkgd', p, v_h)

    o = lax.map(block, qb)
    o = jnp.moveaxis(o, 0, 1).reshape(bsz, length, ATT_WIDTH)
    return o.astype(v.dtype)


def _layer(x, w_in, conv_w, conv_b, ssd_a_log_f, ssd_a_log_b, ssd_dt_bias_f, ssd_dt_bias_b, ssd_d,
           ssd_norm_w, gla_w2_f, gla_b_f, gla_w2_b, gla_b_b, gla_norm_w, att_q_norm_w, att_k_norm_w,
           w_br_ssd, w_br_gla, w_br_attn, w_out, norm1_w, norm2_w, w_ffn_in, w_ffn_out):
    bsz, length, _ = x.shape
    h = _rmsnorm(x, norm1_w)
    proj = h @ w_in
    (z, xbc, dt_raw, gq, gk, gv, gog, glr, aq, ak, av, gates) = _split(proj, IN_SIZES)
    o_ssd = _ssd_branch(z, xbc, dt_raw, conv_w, conv_b, ssd_a_log_f, ssd_a_log_b,
                        ssd_dt_bias_f, ssd_dt_bias_b, ssd_d, ssd_norm_w)
    o_gla = _gla_branch(gq, gk, gv, gog, glr, gla_w2_f, gla_b_f, gla_w2_b, gla_b_b, gla_norm_w)
    o_att = _attn_branch(aq, ak, av, att_q_norm_w, att_k_norm_w)
    g = jax.nn.sigmoid(gates.astype(jnp.float32)).reshape(bsz, length, N_BRANCH, D_MODEL).astype(x.dtype)
    merged = (g[:, :, 0] * (o_ssd @ w_br_ssd)
              + g[:, :, 1] * (o_gla @ w_br_gla)
              + g[:, :, 2] * (o_att @ w_br_attn))
    x = x + merged @ w_out
    h2 = _rmsnorm(x, norm2_w)
    gate, up = _split(h2 @ w_ffn_in, (D_FF, D_FF))
    return x + (jax.nn.silu(gate) * up) @ w_ffn_out


def _trunk(x, w_in, conv_w, conv_b, ssd_a_log_f, ssd_a_log_b, ssd_dt_bias_f, ssd_dt_bias_b, ssd_d,
           ssd_norm_w, gla_w2_f, gla_b_f, gla_w2_b, gla_b_b, gla_norm_w, att_q_norm_w, att_k_norm_w,
           w_br_ssd, w_br_gla, w_br_attn, w_out, norm1_w, norm2_w, w_ffn_in, w_ffn_out, final_norm_w):
    for l in range(DEPTH):
        x = _layer(x, w_in[l], conv_w[l], conv_b[l], ssd_a_log_f[l], ssd_a_log_b[l], ssd_dt_bias_f[l],
                   ssd_dt_bias_b[l], ssd_d[l], ssd_norm_w[l], gla_w2_f[l], gla_b_f[l], gla_w2_b[l],
                   gla_b_b[l], gla_norm_w[l], att_q_norm_w[l], att_k_norm_w[l], w_br_ssd[l],
                   w_br_gla[l], w_br_attn[l], w_out[l], norm1_w[l], norm2_w[l], w_ffn_in[l], w_ffn_out[l])
    return _rmsnorm(x, final_norm_w)


def setup_inputs(seed: int = 0) -> dict:
    key = jax.random.key(seed)
    ks = jax.random.split(key, 32)
    f32 = jnp.float32

    def nrm(k, shape, fan_in):
        return jax.random.normal(k, shape, f32) * (fan_in ** -0.5)

    def gain(k, shape):
        return 1.0 + 0.02 * jax.random.normal(k, shape, f32)

    def dt_bias(k):
        dt0 = jnp.exp(jax.random.uniform(k, (DEPTH, SSD_HEADS), f32, jnp.log(DT_MIN), jnp.log(DT_MAX)))
        return dt0 + jnp.log(-jnp.expm1(-dt0))

    def a_log(k):
        return jnp.log(jax.random.uniform(k, (DEPTH, SSD_HEADS), f32, 1.0, 16.0))

    return {
        'x_prompt': jax.random.normal(ks[0], (BATCH, SEQ, D_MODEL), f32),
        'x_sample': jax.random.normal(ks[1], (DEC_BATCH, DEC_SEQ, D_MODEL), f32),
        'w_in': nrm(ks[2], (DEPTH, D_MODEL, IN_WIDTH), D_MODEL),
        'conv_w': nrm(ks[3], (DEPTH, SSD_CONV, SSD_CONV_CH), SSD_CONV),
        'conv_b': 0.01 * jax.random.normal(ks[4], (DEPTH, SSD_CONV_CH), f32),
        'ssd_a_log_f': a_log(ks[5]),
        'ssd_a_log_b': a_log(ks[6]),
        'ssd_dt_bias_f': dt_bias(ks[7]),
        'ssd_dt_bias_b': dt_bias(ks[8]),
        'ssd_d': 1.0 + 0.1 * jax.random.normal(ks[9], (DEPTH, SSD_HEADS), f32),
        'ssd_norm_w': gain(ks[10], (DEPTH, SSD_WIDTH)),
        'gla_w2_f': nrm(ks[11], (DEPTH, GLA_RANK, GLA_QK_WIDTH), GLA_RANK),
        'gla_b_f': 0.1 * jax.random.normal(ks[12], (DEPTH, GLA_QK_WIDTH), f32),
        'gla_w2_b': nrm(ks[13], (DEPTH, GLA_RANK, GLA_QK_WIDTH), GLA_RANK),
        'gla_b_b': 0.1 * jax.random.normal(ks[14], (DEPTH, GLA_QK_WIDTH), f32),
        'gla_norm_w': gain(ks[15], (DEPTH, GLA_DV)),
        'att_q_norm_w': gain(ks[16], (DEPTH, ATT_HEAD_DIM)),
        'att_k_norm_w': gain(ks[17], (DEPTH, ATT_HEAD_DIM)),
        'w_br_ssd': nrm(ks[18], (DEPTH, SSD_WIDTH, D_MODEL), SSD_WIDTH),
        'w_br_gla': nrm(ks[19], (DEPTH, GLA_WIDTH, D_MODEL), GLA_WIDTH),
        'w_br_attn': nrm(ks[20], (DEPTH, ATT_WIDTH, D_MODEL), ATT_WIDTH),
        'w_out': nrm(ks[21], (DEPTH, D_MODEL, D_MODEL), D_MODEL),
        'norm1_w': gain(ks[22], (DEPTH, D_MODEL)),
        'norm2_w': gain(ks[23], (DEPTH, D_MODEL)),
        'w_ffn_in': nrm(ks[24], (DEPTH, D_MODEL, 2 * D_FF), D_MODEL),
        'w_ffn_out': nrm(ks[25], (DEPTH, D_FF, D_MODEL), D_FF),
        'final_norm_w': gain(ks[26], (D_MODEL,)),
    }


def reference(x_prompt, x_sample, w_in, conv_w, conv_b, ssd_a_log_f, ssd_a_log_b, ssd_dt_bias_f,
              ssd_dt_bias_b, ssd_d, ssd_norm_w, gla_w2_f, gla_b_f, gla_w2_b, gla_b_b, gla_norm_w,
              att_q_norm_w, att_k_norm_w, w_br_ssd, w_br_gla, w_br_attn, w_out, norm1_w, norm2_w,
              w_ffn_in, w_ffn_out, final_norm_w):
    params = (w_in, conv_w, conv_b, ssd_a_log_f, ssd_a_log_b, ssd_dt_bias_f, ssd_dt_bias_b, ssd_d,
              ssd_norm_w, gla_w2_f, gla_b_f, gla_w2_b, gla_b_b, gla_norm_w, att_q_norm_w, att_k_norm_w,
              w_br_ssd, w_br_gla, w_br_attn, w_out, norm1_w, norm2_w, w_ffn_in, w_ffn_out, final_norm_w)
    y_prompt = _trunk(x_prompt, *params)
    y_sample = _trunk(x_sample, *params)
    return (y_prompt, y_sample)
```

```python
import os
import numpy as np
import ml_dtypes
import concourse.bass as bass
import concourse.mybir as mybir
from concourse.bass_utils import run_bass_kernel_spmd

F32 = mybir.dt.float32
BF16 = mybir.dt.bfloat16
AF = mybir.ActivationFunctionType
ALU = mybir.AluOpType
AX = mybir.AxisListType

NCORES = 8
D = 1024
NSEG = 3
SEGL = 2048
T = NSEG * SEGL
TB = 512
NB = T // TB
DFF = 2816
DEPTH = 2
EPS = 1e-6
IN_W = 8512

STAGE = int(os.environ.get("MK_STAGE", "9"))


class _Op:
    __slots__ = ("eng", "fn", "deps", "inc", "idx", "dma", "sem", "val", "cnt")


class MK:
    ENGS = ("pe", "act", "dve", "pool", "sp")
    EPOCH = 20000
    KDMA = 12

    def __init__(self, nc):
        self.nc = nc
        self.ops = {e: [] for e in self.ENGS}
        self.last_w = {}
        self.rd = {}
        self.ndma = {e: 0 for e in self.ENGS}
        self.dma_ops = {e: [] for e in self.ENGS}
        self.pending = {}

    def barrier(self, fn):
        deps = []
        for e in self.ENGS:
            cl = [o for o in self.ops[e] if not o.dma]
            if cl and e != "pool":
                deps.append(cl[-1])
            deps.extend(self.dma_ops[e][-self.KDMA:])
        o = self.op("pool", fn, _extra=deps)
        for e in self.ENGS:
            if e != "pool":
                self.pending[e] = o
        return o

    def op(self, eng, fn, r=(), w=(), dma=False, _extra=()):
        lst = self.ops[eng]
        o = _Op()
        o.eng = eng; o.fn = fn; o.idx = len(lst); o.dma = dma; o.inc = False
        o.sem = None; o.val = 0; o.cnt = 0
        deps = {}
        for k in r:
            lw = self.last_w.get(k)
            if lw is not None:
                deps[id(lw)] = lw
        for k in w:
            lw = self.last_w.get(k)
            if lw is not None:
                deps[id(lw)] = lw
            rdk = self.rd.get(k)
            if rdk:
                for d in rdk.values():
                    deps[id(d)] = d
        for d in _extra:
            deps[id(d)] = d
        pb = self.pending.pop(eng, None)
        if pb is not None:
            deps[id(pb)] = pb
        out = []
        for d in deps.values():
            if (not d.dma) and (not dma) and d.eng == eng:
                if eng == "pe" or d.idx < o.idx - 1:
                    continue
            out.append(d)
            d.inc = True
        if dma:
            j = self.ndma[eng]
            self.ndma[eng] = j + 1
            o.sem = j % self.KDMA
            o.val = 16 * (j // self.KDMA + 1)
            if j >= self.KDMA:
                out.append(self.dma_ops[eng][j - self.KDMA])
            self.dma_ops[eng].append(o)
        o.deps = out
        for k in r:
            self.rd.setdefault(k, {})[("d", eng, o.idx) if dma else eng] = o
        for k in w:
            self.last_w[k] = o
            self.rd[k] = {}
        lst.append(o)
        return o

    def emit(self, final_waits=()):
        nc = self.nc
        nep = {}
        for e in self.ENGS:
            c = 0
            for o in self.ops[e]:
                if o.dma:
                    continue
                if o.inc:
                    c += 1
                    o.cnt = c
            nep[e] = (c + self.EPOCH - 1) // self.EPOCH + 1
        esems = {e: [nc.alloc_semaphore(name=f"s_{e}_{i}") for i in range(nep[e])] for e in self.ENGS}
        dsems = {e: [nc.alloc_semaphore(name=f"d_{e}_{i}") for i in range(self.KDMA)] if self.ndma[e] else []
                 for e in self.ENGS}
        engobj = {"pe": "tensor", "act": "scalar", "dve": "vector", "pool": "gpsimd", "sp": "sync"}

        def tok(d):
            if d.dma:
                return dsems[d.eng][d.sem], d.val
            ep = (d.cnt - 1) // self.EPOCH
            return esems[d.eng][ep], (d.cnt - 1) % self.EPOCH + 1

        def body(e, eo):
            waited = {}
            for o in self.ops[e]:
                for d in o.deps:
                    s, v = tok(d)
                    key = id(s)
                    if waited.get(key, 0) >= v:
                        continue
                    waited[key] = v
                    eo.wait_ge(s, v)
                ins = o.fn(eo)
                if o.dma:
                    ins.then_inc(dsems[e][o.sem], 16)
                elif o.inc:
                    s, v = tok(o)
                    ins.then_inc(s, 1)
            if e == "sp":
                for d in final_waits:
                    s, v = tok(d)
                    eo.wait_ge(s, v)

        with nc.Block() as block:
            for e in self.ENGS:
                dec = getattr(block, engobj[e])
                dec(lambda eo, e=e: body(e, eo))


class Arena:
    def __init__(self, nc, nbytes):
        self.n = nbytes // 2
        self.t = nc.alloc_sbuf_tensor("arena", [128, self.n], BF16)
        self.off = 0

    def reset(self):
        self.off = 0

    def alloc(self, shape, dt):
        n = int(np.prod(shape[1:]))
        if dt == F32:
            if self.off % 2:
                self.off += 1
            nb = 2 * n
        else:
            nb = n
        assert self.off + nb <= self.n, ("arena overflow", self.off, nb, self.n)
        ap = self.t[0:shape[0], self.off:self.off + nb]
        self.off += nb
        if dt == F32:
            ap = ap.bitcast(F32)
        if len(shape) == 3:
            ap = ap.rearrange("p (a b) -> p a b", a=shape[1])
        elif len(shape) == 4:
            ap = ap.rearrange("p (a b c) -> p a b c", a=shape[1], b=shape[2])
        return ap


def build_program():
    nc = bass.Bass("TRN2", target_bir_lowering=False)
    mk = MK(nc)

    def din(name, shape, dt=F32):
        return nc.dram_tensor(name, list(shape), dt, kind="ExternalInput").ap()

    def dscr(name, shape, dt):
        return nc.dram_tensor(name, list(shape), dt, kind="Internal").ap()

    x_in = din("x", [T, D])
    w_ffn_in = din("w_ffn_in", [DEPTH, D, 2 * DFF])
    w_ffn_out = din("w_ffn_out", [DEPTH, DFF, D])
    w_in = din("w_in", [DEPTH, D, IN_W])
    w_br_attn = din("w_br_attn", [DEPTH, 512, D])
    w_br_gla = din("w_br_gla", [DEPTH, 512, D])
    w_br_ssd = din("w_br_ssd", [DEPTH, 1024, D])
    w_out = din("w_out", [DEPTH, D, D])
    gla_w2 = [din("gla_w2_f", [DEPTH, 16, 256]), din("gla_w2_b", [DEPTH, 16, 256])]
    gla_b = [din("gla_b_f", [DEPTH, 256]), din("gla_b_b", [DEPTH, 256])]
    gcst = din("gcst", [128, NGC])
    scst = din("scst", [128, NSC])
    srow = din("srow", [DEPTH, 128, NSR])
    ropec = din("ropec", [128, T])
    ropes = din("ropes", [128, T])
    pvec = din("pvec", [128, NPV])
    cst = din("cst", [128, NCST])
    y_out = nc.dram_tensor("y", [T, D], F32, kind="ExternalOutput").ap()
    dbg_out = nc.dram_tensor("dbg", [128, 512], F32, kind="ExternalOutput").ap()

    xT = dscr("xT_scr", [8, 128, T], F32)
    qS = dscr("qS_scr", [8, 64, T], BF16)
    kS = dscr("kS_scr", [2, 64, T], BF16)
    vS = dscr("vS_scr", [T, 128], BF16)
    oS = dscr("oS_scr", [8, 64, T], BF16)
    xbcS = dscr("xbcS_scr", [16, 128, T], BF16)
    zS = dscr("zS_scr", [T, 1024], BF16)
    dtS = dscr("dtS_scr", [T, 32], F32)
    xcS = dscr("xcS_scr", [8, 128, T], BF16)
    xtokS = dscr("xtokS_scr", [T, 1024], BF16)
    btokS = dscr("btokS_scr", [T, 512], BF16)
    yaccS = dscr("yaccS_scr", [T, 1024], F32)
    oSS = dscr("oSS_scr", [8, 128, T], BF16)
    gqS = dscr("gqS_scr", [4, 64, T], BF16)
    gkS = dscr("gkS_scr", [4, 64, T], BF16)
    gktS = dscr("gktS_scr", [T, 256], BF16)
    gvtS = dscr("gvtS_scr", [T, 512], BF16)
    gogS = dscr("gogS_scr", [4, 128, T], BF16)
    lrS = dscr("lrS_scr", [32, T], F32)
    goacc = dscr("goacc_scr", [4, 128, T], F32)
    oG = dscr("oG_scr", [4, 128, T], BF16)

    def sb(name, shape, dt):
        return nc.alloc_sbuf_tensor(name, list(shape), dt)

    pv = sb("pv", [128, NPV], F32)
    cs = sb("cs", [128, NCST], F32)
    ident = cs[:, 0:128]
    ones_bf = sb("ones_bf", [128, 128], BF16)
    perm_bf = sb("perm_bf", [128, 128], BF16)
    b64_bf = sb("b64_bf", [128, 128], BF16)
    ident_bf = sb("ident_bf", [128, 128], BF16)
    bscr = sb("bscr", [128, 4], F32)
    mk.op("sp", lambda e: e.dma_start(out=pv[:], in_=pvec), w=["pv"], dma=True)
    mk.op("sp", lambda e: e.dma_start(out=cs[:], in_=cst), w=["cs"], dma=True)
    mk.op("dve", lambda e: e.memset(ones_bf[:], 1.0), w=["ones_bf"])
    mk.op("dve", lambda e: e.tensor_copy(out=perm_bf[:], in_=cs[:, 136:264]), r=["cs"], w=["perm_bf"])
    mk.op("dve", lambda e: e.tensor_copy(out=b64_bf[:], in_=cs[:, 264:392]), r=["cs"], w=["b64_bf"])
    mk.op("dve", lambda e: e.tensor_copy(out=ident_bf[:], in_=cs[:, 0:128]), r=["cs"], w=["ident_bf"])
    psw = [nc.alloc_psum_tensor(f"psw{i}", [128, 1024], F32) for i in range(4)]
    psb = [psw[i // 2][:, (i % 2) * 512:(i % 2 + 1) * 512] for i in range(8)]
    A = Arena(nc, nc.sbuf_bytes_remaining - 256)

    def phase():
        mk.barrier(lambda e: e.memset(bscr[:, 0:1], 0.0))
        A.reset()

    PIECE = 1024
    cast_rr = [0]

    def load_weight(dst, src, kc, ncols, stg, scale_col0=None, tag=""):
        srcv = src.rearrange("(c p) n -> p c n", p=128)
        for c in range(kc):
            for n0 in range(0, ncols, PIECE):
                n1 = min(ncols, n0 + PIECE)
                i = cast_rr[0] % len(stg)
                cast_rr[0] += 1
                st = stg[i]
                mk.op("sp", lambda e, st=st, c=c, n0=n0, n1=n1: e.dma_start(out=st[:, 0:n1 - n0], in_=srcv[:, c, n0:n1]),
                      w=[("stg", i)], dma=True)
                eng = ("act", "dve")[cast_rr[0] % 2]
                if scale_col0 is None:
                    if eng == "act":
                        fn = lambda e, st=st, c=c, n0=n0, n1=n1: e.copy(out=dst[:, c, n0:n1], in_=st[:, 0:n1 - n0])
                    else:
                        fn = lambda e, st=st, c=c, n0=n0, n1=n1: e.tensor_copy(out=dst[:, c, n0:n1], in_=st[:, 0:n1 - n0])
                    rk = [("stg", i)]
                else:
                    sc = scale_col0 + c
                    if eng == "act":
                        fn = lambda e, st=st, c=c, n0=n0, n1=n1, sc=sc: e.activation(
                            out=dst[:, c, n0:n1], in_=st[:, 0:n1 - n0], func=AF.Copy, scale=pv[:, sc:sc + 1])
                    else:
                        fn = lambda e, st=st, c=c, n0=n0, n1=n1, sc=sc: e.tensor_scalar(
                            out=dst[:, c, n0:n1], in0=st[:, 0:n1 - n0], scalar1=pv[:, sc:sc + 1], scalar2=None, op0=ALU.mult)
                    rk = [("stg", i), "pv"]
                mk.op(eng, fn, r=rk, w=[("W", tag, c, n0 // PIECE)])

    def evac(eng, out, in_, r, w):
        if eng == "act":
            mk.op("act", lambda e: e.copy(out=out, in_=in_), r=r, w=w)
        else:
            mk.op(eng, lambda e: e.tensor_copy(out=out, in_=in_), r=r, w=w)

    xin_t = [A.alloc([128, D], F32) for i in range(4)]
    xtt = [A.alloc([128, 8, 128], F32) for i in range(4)]
    dbgt = A.alloc([128, 512], F32)
    mk.op("dve", lambda e: e.memset(dbgt, 0.0), w=["dbgt"])
    mk.op("pool", lambda e: e.tensor_copy(out=dbgt[0:64, 0:128], in_=cs[64:128, 0:128]), r=["cs", "dbgt"], w=["dbgt1"])
    mk.op("dve", lambda e: e.tensor_copy(out=dbgt[0:64, 128:256], in_=cs[64:128, 0:128]), r=["cs", "dbgt"], w=["dbgt2"])
    mk.op("act", lambda e: e.copy(out=dbgt[64:128, 256:384], in_=cs[0:64, 0:128]), r=["cs", "dbgt"], w=["dbgt3"])
    dbg_o = mk.op("sp", lambda e: e.dma_start(out=dbg_out, in_=dbgt), r=["dbgt1", "dbgt2", "dbgt3"], w=["dbgo"], dma=True)

    NT0 = T // 128

    def p0_load(t):
        i = t % 4
        mk.op("sp", lambda e: e.dma_start(out=xin_t[i], in_=x_in[t * 128:(t + 1) * 128, :]),
              w=[("xin", i)], dma=True)

    def p0_tile(t):
        i = t % 4
        for half in range(2):
            pb = (2 * t + half) % 8
            for c4 in range(4):
                c = half * 4 + c4
                mk.op("pe", lambda e, c=c, c4=c4, pb=pb: e.transpose(
                    out=psb[pb][:, c4 * 128:(c4 + 1) * 128], in_=xin_t[i][:, c * 128:(c + 1) * 128], identity=ident),
                    r=[("xin", i), "cs"], w=[("ps", pb)])
            evac(("act", "dve")[half], xtt[i][:, half * 4:(half + 1) * 4, :], psb[pb][:].rearrange("p (c t) -> p c t", c=4),
                 r=[("ps", pb)], w=[("xtt", i, half)])
        mk.op("sp", lambda e: e.dma_start(
            out=xT[:, :, t * 128:(t + 1) * 128].rearrange("c p t -> p c t"), in_=xtt[i]),
            r=[("xtt", i, 0), ("xtt", i, 1)], w=[("xT", t // 4)], dma=True)

    for t in range(-3, NT0):
        if t + 3 < NT0:
            p0_load(t + 3)
        if t >= 0:
            p0_tile(t)

    def load_xblock(xb, b, par=0):
        mk.op("sp", lambda e, b=b: e.dma_start(out=xb, in_=xT[:, :, b * TB:(b + 1) * TB].rearrange("c p t -> p c t")),
              r=[("xT", b)], w=[("xb", par, c) for c in range(8)], dma=True)

    def rms_block(xb, hb, sq, rstd, pbank, par=0, xpar=None):
        xk = par if xpar is None else xpar
        for c in range(8):
            mk.op("act", lambda e, c=c, s=sq[c % 2]: e.activation(out=s, in_=xb[:, c, :], func=AF.Square),
                  r=[("xb", xk, c)], w=[("sq", c % 2)])
            mk.op("pe", lambda e, c=c, s=sq[c % 2]: e.matmul(psb[pbank][:], lhsT=ones_bf[:], rhs=s, start=(c == 0), stop=(c == 7)),
                  r=[("sq", c % 2), "ones_bf"], w=[("ps", pbank)])
        mk.op("act", lambda e: e.activation(out=rstd, in_=psb[pbank][:], func=AF.Sqrt, scale=1.0 / D, bias=cs[:, 128:129]),
              r=[("ps", pbank), "cs"], w=[("rstd0", xk)])
        mk.op("dve", lambda e: e.reciprocal(out=rstd, in_=rstd), r=[("rstd0", xk)], w=[("rstd", xk)])
        if hb is not None:
            for c in range(8):
                mk.op("dve", lambda e, c=c: e.tensor_tensor(out=hb[:, c, :], in0=xb[:, c, :], in1=rstd, op=ALU.mult),
                      r=[("xb", xk, c), ("rstd", xk)], w=[("hb", par, c)])

    C_AQ, C_AK, C_AV, C_G = 4672, 5184, 5312, 5440

    def attn_inproj_setup(l, stg):
        watt = A.alloc([128, 8, 768], BF16)
        qraw = [A.alloc([128, TB], F32) for i in range(2)]
        sqq = [A.alloc([128, TB], BF16) for i in range(2)]
        rq = [A.alloc([128, TB], F32) for i in range(2)]
        qn = [A.alloc([128, TB], BF16) for i in range(2)]
        cosb = A.alloc([128, TB], F32)
        sinb = A.alloc([128, TB], F32)
        t1 = [A.alloc([128, TB], F32) for i in range(2)]
        t2 = [A.alloc([128, TB], F32) for i in range(2)]
        qo = [A.alloc([128, TB], BF16) for i in range(2)]
        vt = [A.alloc([128, 4, 128], BF16) for i in range(2)]
        load_weight(watt, w_in[l][:, C_AQ:C_AQ + 768], 8, 768, stg, scale_col0=PV_NORM1 + 8 * l, tag="watt")
        itc = [0]

        def body(b, hb, par):
            mk.op("sp", lambda e, b=b: e.dma_start(out=cosb, in_=ropec[:, b * TB:(b + 1) * TB]), w=["cosb"], dma=True)
            mk.op("sp", lambda e, b=b: e.dma_start(out=sinb, in_=ropes[:, b * TB:(b + 1) * TB]), w=["sinb"], dma=True)
            for ch in range(5):
                i = itc[0] % 2
                itc[0] += 1
                pa, pS, pR = 1 + i, 3 + i, 5 + i
                for c in range(8):
                    mk.op("pe", lambda e, c=c, ch=ch, pa=pa: e.matmul(psb[pa][:], lhsT=watt[:, c, ch * 128:(ch + 1) * 128],
                                                                  rhs=hb[:, c, :], start=(c == 0), stop=(c == 7)),
                          r=[("hb", par, c), ("W", "watt", c, 0)], w=[("ps", pa)])
                mk.op("act", lambda e, i=i, pa=pa: e.copy(out=qraw[i], in_=psb[pa][:]), r=[("ps", pa)], w=[("qraw", i)])
                mk.op("act", lambda e, i=i: e.activation(out=sqq[i], in_=qraw[i], func=AF.Square),
                      r=[("qraw", i)], w=[("sqq", i)])
                mk.op("pe", lambda e, i=i, pS=pS: e.matmul(psb[pS][:], lhsT=b64_bf[:], rhs=sqq[i], start=True, stop=True),
                      r=[("sqq", i), "b64_bf"], w=[("ps", pS)])
                if ch < 4:
                    mk.op("act", lambda e, i=i, pS=pS: e.activation(out=rq[i], in_=psb[pS][:], func=AF.Sqrt, scale=1.0, bias=cs[:, 129:130]),
                          r=[("ps", pS), "cs"], w=[("rq0", i)])
                    wc = PV_QW + l
                else:
                    mk.op("act", lambda e, i=i, pS=pS: e.activation(out=rq[i], in_=psb[pS][:], func=AF.Sqrt, scale=1.0 / 64, bias=cs[:, 128:129]),
                          r=[("ps", pS), "cs"], w=[("rq0", i)])
                    wc = PV_KW + l
                mk.op("dve", lambda e, i=i: e.reciprocal(out=rq[i], in_=rq[i]), r=[("rq0", i)], w=[("rq", i)])
                mk.op("dve", lambda e, i=i, wc=wc: e.scalar_tensor_tensor(out=qn[i], in0=qraw[i], scalar=pv[:, wc:wc + 1], in1=rq[i],
                                                                     op0=ALU.mult, op1=ALU.mult),
                      r=[("qraw", i), ("rq", i), "pv"], w=[("qn", i)])
                mk.op("pe", lambda e, i=i, pR=pR: e.matmul(psb[pR][:], lhsT=perm_bf[:], rhs=qn[i], start=True, stop=True),
                      r=[("qn", i), "perm_bf"], w=[("ps", pR)])
                mk.op("dve", lambda e, i=i: e.tensor_tensor(out=t1[i], in0=qn[i], in1=cosb, op=ALU.mult),
                      r=[("qn", i), "cosb"], w=[("t1", i)])
                mk.op("dve", lambda e, i=i, pR=pR: e.tensor_tensor(out=t2[i], in0=psb[pR][:], in1=sinb, op=ALU.mult),
                      r=[("ps", pR), "sinb"], w=[("t2", i)])
                mk.op("dve", lambda e, i=i: e.tensor_tensor(out=qo[i], in0=t1[i], in1=t2[i], op=ALU.add),
                      r=[("t1", i), ("t2", i)], w=[("qo", i)])
                if ch < 4:
                    dst = qS[2 * ch:2 * ch + 2, :, b * TB:(b + 1) * TB].rearrange("h d t -> (h d) t")
                else:
                    dst = kS[:, :, b * TB:(b + 1) * TB].rearrange("h d t -> (h d) t")
                mk.op("sp", lambda e, i=i, dst=dst: e.dma_start(out=dst, in_=qo[i]), r=[("qo", i)], w=[("qkS", b, ch)], dma=True)
            vi = b % 2
            for tt in range(4):
                for c in range(8):
                    mk.op("pe", lambda e, c=c, tt=tt: e.matmul(psb[7][:, tt * 128:(tt + 1) * 128], lhsT=hb[:, c, tt * 128:(tt + 1) * 128],
                                                             rhs=watt[:, c, 640:768], start=(c == 0), stop=(c == 7)),
                          r=[("hb", par, c), ("W", "watt", c, 0)], w=[("ps", 7)])
            mk.op("act", lambda e, vi=vi: e.copy(out=vt[vi], in_=psb[7][:].rearrange("p (a b) -> p a b", a=4)),
                  r=[("ps", 7)], w=[("vt", vi)])
            mk.op("sp", lambda e, vi=vi, b=b: e.dma_start(out=vS[b * TB:(b + 1) * TB, :].rearrange("(a p) n -> p a n", p=128), in_=vt[vi]),
                  r=[("vt", vi)], w=[("vS", b)], dma=True)
        return body

    def inproj_phase(l):
        phase()
        stg = [A.alloc([128, PIECE], F32) for i in range(2)]
        xb2 = [A.alloc([128, 8, TB], F32) for i in range(2)]
        hb2 = [A.alloc([128, 8, TB], BF16) for i in range(2)]
        sq = [A.alloc([128, TB], BF16) for i in range(2)]
        rstd2 = [A.alloc([128, TB], F32) for i in range(2)]
        bodies = [attn_inproj_setup(l, stg)]
        if STAGE >= 2:
            bodies.append(gla_inproj_setup(l, stg))
        if STAGE >= 3:
            bodies.append(ssd_inproj_setup(l, stg))
        load_xblock(xb2[0], 0, 0)
        rms_block(xb2[0], hb2[0], sq, rstd2[0], 0, 0)
        for b in range(NB):
            par = b % 2
            if b + 1 < NB:
                load_xblock(xb2[1 - par], b + 1, 1 - par)
            for k, body in enumerate(bodies):
                if k == len(bodies) - 1 and b + 1 < NB:
                    rms_block(xb2[1 - par], hb2[1 - par], sq, rstd2[1 - par], 0, 1 - par)
                body(b, hb2[par], par)

    def attention_phase(l):
        phase()
        NKT = 32
        kt_sb = A.alloc([128, NKT * 128], BF16)
        v1 = A.alloc([128, NKT, 2, 128], BF16)
        v2 = A.alloc([128, NKT, 2, 128], BF16)
        qa = [A.alloc([128, 4, TB], BF16) for i in range(2)]
        pt = [A.alloc([128, 2 * TB], BF16) for i in range(3)]
        rs = [A.alloc([128, TB], F32) for i in range(2)]
        r0 = [A.alloc([128, TB], F32) for i in range(2)]
        ob = [A.alloc([128, TB], BF16) for i in range(2)]
        SB = [1, 2, 3, 4]
        OB = [5, 6]
        cnt = {"s": 0, "o": 0, "p": 0, "q": 0, "r": 0}
        for grp, (tok0, ntok) in enumerate([(0, 2 * SEGL), (2 * SEGL, SEGL)]):
            nkt = ntok // 128
            gk = ("grp", l, grp)
            mk.op("sp", lambda e, tok0=tok0, ntok=ntok: e.dma_start(
                out=kt_sb[:, 0:ntok], in_=kS[:, :, tok0:tok0 + ntok].rearrange("h d t -> (h d) t")),
                r=[("qkS", b, 4) for b in range(tok0 // TB, (tok0 + ntok) // TB)], w=["kt_sb"], dma=True)
            mk.op("pool", lambda e, nkt=nkt: e.memset(v1[:, 0:nkt, :, 64:128], 1.0), w=["v1ones"])
            for kv in range(2):
                mk.op("sp", lambda e, kv=kv, tok0=tok0, ntok=ntok, nkt=nkt: e.dma_start(
                    out=v1[:, 0:nkt, kv, 0:64],
                    in_=vS[tok0:tok0 + ntok, kv * 64:(kv + 1) * 64].rearrange("(a p) n -> p a n", p=128)),
                    r=[("vS", b) for b in range(tok0 // TB, (tok0 + ntok) // TB)], w=[("v1", kv)], dma=True)
            if grp == 0:
                mk.op("dve", lambda e, nkt=nkt: e.tensor_scalar(out=v2[:, 0:nkt].rearrange("p a b c -> p (a b c)"),
                                                               in0=v1[:, 0:nkt].rearrange("p a b c -> p (a b c)"),
                                                               scalar1=pv[:, PV_LINK:PV_LINK + 1], scalar2=None, op0=ALU.mult),
                      r=[("v1", 0), ("v1", 1), "v1ones", "pv"], w=["v2"])
            for qb in range(ntok // TB):
                b = tok0 // TB + qb
                qi = cnt["q"] % 2
                cnt["q"] += 1
                for half in range(2):
                    mk.op("sp", lambda e, half=half, b=b, qi=qi: e.dma_start(
                        out=qa[qi][half * 64:(half + 1) * 64, :, :],
                        in_=qS[half * 4:half * 4 + 4, :, b * TB:(b + 1) * TB].rearrange("h d t -> d h t")),
                        r=[("qkS", b, ch) for ch in range(4)], w=[("qa", qi, half)], dma=True)
                qseg = (tok0 + qb * TB) // SEGL
                for j in range(4):
                    heads = (j, 4 + j)
                    oi = cnt["o"] % 2
                    cnt["o"] += 1
                    pos = (4 + 2 * oi, 5 + 2 * oi)
                    pend = []

                    def issue_s(kt, j=j):
                        si = cnt["s"] % 2
                        cnt["s"] += 1
                        for kv in range(2):
                            mk.op("pe", lambda e, kt=kt, si=si, kv=kv, j=j, qi=qi: e.matmul(
                                psb[2 * si + kv][:], lhsT=kt_sb[kv * 64:(kv + 1) * 64, kt * 128:(kt + 1) * 128],
                                rhs=qa[qi][kv * 64:(kv + 1) * 64, j, :], start=True, stop=True),
                                r=["kt_sb", ("qa", qi, kv)], w=[("ps", 2 * si + kv)])
                        pi = cnt["p"] % 3
                        cnt["p"] += 1
                        mk.op("act", lambda e, si=si, pi=pi: e.activation(out=pt[pi], in_=psw[si][:], func=AF.Exp),
                              r=[("ps", 2 * si), ("ps", 2 * si + 1)], w=[("pt", pi)])
                        return pi

                    def issue_o(kt, pi, pos=pos):
                        kseg = (tok0 + kt * 128) // SEGL
                        vv = v1 if kseg == qseg else v2
                        for kv in range(2):
                            vk = [("v1", kv), "v1ones"] if kseg == qseg else ["v2"]
                            mk.op("pe", lambda e, kt=kt, pi=pi, vv=vv, kv=kv, po=pos[kv]: e.matmul(
                                psb[po][:], lhsT=vv[:, kt, kv, :], rhs=pt[pi][:, kv * 512:(kv + 1) * 512], start=(kt == 0), stop=(kt == nkt - 1)),
                                r=[("pt", pi)] + vk, w=[("ps", pos[kv])])

                    LA = 1
                    for kt in range(nkt):
                        pend.append((kt, issue_s(kt)))
                        if len(pend) > LA:
                            issue_o(*pend.pop(0))
                    while pend:
                        issue_o(*pend.pop(0))
                    for kv in range(2):
                        h = heads[kv]
                        po = pos[kv]
                        ri = cnt["r"] % 2
                        cnt["r"] += 1
                        mk.op("dve", lambda e, ri=ri, po=po: e.reciprocal(out=rs[ri][64:128, :], in_=psb[po][64:128, :]),
                              r=[("ps", po)], w=[("rs", ri)])
                        mk.op("pool", lambda e, ri=ri: e.tensor_copy(out=r0[ri][0:64, :], in_=rs[ri][64:128, :]),
                              r=[("rs", ri)], w=[("r0", ri)])
                        mk.op("dve", lambda e, ri=ri, po=po: e.tensor_tensor(out=ob[ri][0:64, :], in0=psb[po][0:64, :], in1=r0[ri][0:64, :], op=ALU.mult),
                              r=[("ps", po), ("r0", ri)], w=[("ob", ri)])
                        mk.op("sp", lambda e, ri=ri, h=h, b=b: e.dma_start(out=oS[h, :, b * TB:(b + 1) * TB], in_=ob[ri][0:64, :]),
                              r=[("ob", ri)], w=[("oS", b, h)], dma=True)

    C_GQ, C_GK, C_GV, C_GOG, C_GLR = 3104, 3360, 3616, 4128, 4640

    def gla_inproj_setup(l, stg):
        NW = 1568
        wg_ = A.alloc([128, 8, NW], BF16)
        ob_ = [A.alloc([128, TB], BF16) for i in range(2)]
        lrt = A.alloc([32, TB], F32)
        kt_ = [A.alloc([128, 256], BF16) for i in range(2)]
        vt_ = [A.alloc([128, 512], BF16) for i in range(2)]
        for c in range(8):
            for (n0, n1) in ((0, 784), (784, NW)):
                pass
        load_weight(wg_[:, :, 0:1024], w_in[l][:, C_GQ:C_GQ + 1024], 8, 1024, stg, scale_col0=PV_NORM1 + 8 * l, tag="wgl0")
        load_weight(wg_[:, :, 1024:NW], w_in[l][:, C_GQ + 1024:C_GQ + NW], 8, NW - 1024, stg, scale_col0=PV_NORM1 + 8 * l, tag="wgl1")

        def wkey(c, col):
            return ("W", "wgl0" if col < 1024 else "wgl1", c, 0)
        itc = [0]

        def body(b, hb, par):
            specs = [(0, gqS, 0), (128, gqS, 1), (256, gkS, 0), (384, gkS, 1)] + [(1024 + 128 * j, gogS, j) for j in range(4)]
            for (col, dstS, j) in specs:
                i = itc[0] % 2
                itc[0] += 1
                pa = 1 + i
                for c in range(8):
                    mk.op("pe", lambda e, c=c, col=col, pa=pa: e.matmul(psb[pa][:], lhsT=wg_[:, c, col:col + 128], rhs=hb[:, c, :],
                                                                    start=(c == 0), stop=(c == 7)),
                          r=[("hb", par, c), wkey(c, col)], w=[("ps", pa)])
                evac(("act", "dve")[i], ob_[i], psb[pa][:], r=[("ps", pa)], w=[("ob_", i)])
                if dstS is gogS:
                    dst = gogS[j, :, b * TB:(b + 1) * TB]
                else:
                    dst = dstS[2 * j:2 * j + 2, :, b * TB:(b + 1) * TB].rearrange("h d t -> (h d) t")
                mk.op("sp", lambda e, i=i, dst=dst: e.dma_start(out=dst, in_=ob_[i]), r=[("ob_", i)], w=[("gfm", b, col)], dma=True)
            for c in range(8):
                mk.op("pe", lambda e, c=c: e.matmul(psb[3][0:32, :], lhsT=wg_[:, c, 1536:1568], rhs=hb[:, c, :], start=(c == 0), stop=(c == 7)),
                      r=[("hb", par, c), wkey(c, 1536)], w=[("ps", 3)])
            evac("act", lrt, psb[3][0:32, :], r=[("ps", 3)], w=["lrt"])
            mk.op("sp", lambda e, b=b: e.dma_start(out=lrS[:, b * TB:(b + 1) * TB], in_=lrt), r=["lrt"], w=[("lrS", b)], dma=True)
            for tt in range(4):
                i = tt % 2
                for c in range(8):
                    mk.op("pe", lambda e, c=c, tt=tt, i=i: e.matmul(psb[4 + i][:, 0:256], lhsT=hb[:, c, tt * 128:(tt + 1) * 128],
                                                                 rhs=wg_[:, c, 256:512], start=(c == 0), stop=(c == 7)),
                          r=[("hb", par, c), wkey(c, 256)], w=[("ps", 4 + i)])
                for c in range(8):
                    mk.op("pe", lambda e, c=c, tt=tt, i=i: e.matmul(psb[6 + i][:], lhsT=hb[:, c, tt * 128:(tt + 1) * 128],
                                                                 rhs=wg_[:, c, 512:1024], start=(c == 0), stop=(c == 7)),
                          r=[("hb", par, c), wkey(c, 512)], w=[("ps", 6 + i)])
                evac("act", kt_[i], psb[4 + i][:, 0:256], r=[("ps", 4 + i)], w=[("kt_", i)])
                evac("dve", vt_[i], psb[6 + i][:], r=[("ps", 6 + i)], w=[("vt_", i)])
                t0 = b * TB + tt * 128
                mk.op("sp", lambda e, i=i, t0=t0: e.dma_start(out=gktS[t0:t0 + 128, :], in_=kt_[i]), r=[("kt_", i)], w=[("gtm", b, tt, 0)], dma=True)
                mk.op("sp", lambda e, i=i, t0=t0: e.dma_start(out=gvtS[t0:t0 + 128, :], in_=vt_[i]), r=[("vt_", i)], w=[("gtm", b, tt, 1)], dma=True)
        return body

    def gla_pass(l, dr):
        phase()
        gc = A.alloc([128, NGC], F32)
        mk.op("sp", lambda e: e.dma_start(out=gc, in_=gcst), w=["gc"], dma=True)
        o0 = dr * 896
        M12 = gc[:, o0:o0 + 256]
        M3 = gc[:, o0 + 256:o0 + 384]
        mask4f = gc[:, o0 + 384:o0 + 896]
        w2a = A.alloc([17, 256], F32)
        mk.op("sp", lambda e: e.dma_start(out=w2a[0:16, :], in_=gla_w2[dr][l]), w=["w2a0"], dma=True)
        mk.op("sp", lambda e: e.dma_start(out=w2a[16:17, :], in_=gla_b[dr][l:l + 1, :]), w=["w2a1"], dma=True)
        ones_f = A.alloc([128, 128], F32)
        mk.op("dve", lambda e: e.memset(ones_f, 1.0), w=["ones_f"])
        lrt = [A.alloc([17, 128], F32) for i in range(2)]
        for i in range(2):
            mk.op("dve", lambda e, i=i: e.memset(lrt[i], 1.0), w=[("lrt", i)])
        q_sb = [A.alloc([64, 4, 128], BF16) for i in range(2)]
        k_sb = [A.alloc([64, 4, 128], BF16) for i in range(2)]
        kt_sb = [A.alloc([128, 256], BF16) for i in range(2)]
        vt_sb = [A.alloc([128, 512], BF16) for i in range(2)]
        e_ = A.alloc([128, 256], F32)
        sp_ = A.alloc([128, 256], F32)
        E1 = A.alloc([64, 4, 128], F32)
        E1i = A.alloc([64, 4, 128], F32)
        EB2 = [A.alloc([64, 4, 128], F32) for _ in range(2)]
        Ed = A.alloc([128, 256], F32)
        qe = A.alloc([64, 4, 128], BF16)
        ke = A.alloc([64, 4, 128], BF16)
        qb2 = [A.alloc([64, 4, 128], BF16) for _ in range(2)]
        kd2 = [A.alloc([128, 256], BF16) for _ in range(2)]
        attm2 = [A.alloc([128, 4, 128], BF16) for _ in range(2)]
        S = A.alloc([64, 4, 128], F32)
        Sb = [A.alloc([64, 4, 128], BF16) for i in range(2)]
        osb = A.alloc([128, 512], F32)
        if dr == 1:
            oprev2 = [A.alloc([128, 4, 128], F32) for _ in range(2)]
            gog2 = [A.alloc([128, 4, 128], BF16) for _ in range(2)]
            sqo = A.alloc([128, 512], BF16)
            rr = A.alloc([128, 512], F32)
            sg_ = A.alloc([128, 512], F32)
            oo = A.alloc([128, 512], BF16)
        NT = T // 128
        order = list(range(NT)) if dr == 0 else list(range(NT - 1, -1, -1))
        tiles_per_seg = SEGL // 128
        def tileA(n, t):
            i = n % 2
            t0 = t * 128
            seg = t // tiles_per_seg
            mk.op("sp", lambda e, i=i, t0=t0: e.dma_start(out=lrt[i][0:16, :], in_=lrS[dr * 16:(dr + 1) * 16, t0:t0 + 128]),
                  r=[("lrS", t0 // TB)], w=[("lrt", i)], dma=True)
            mk.op("sp", lambda e, i=i, t0=t0: e.dma_start(out=q_sb[i], in_=gqS[:, :, t0:t0 + 128].rearrange("h d t -> d h t")),
                  r=[("gfm", t0 // TB, 0), ("gfm", t0 // TB, 128)], w=[("q_sb", i)], dma=True)
            mk.op("sp", lambda e, i=i, t0=t0: e.dma_start(out=k_sb[i], in_=gkS[:, :, t0:t0 + 128].rearrange("h d t -> d h t")),
                  r=[("gfm", t0 // TB, 256), ("gfm", t0 // TB, 384)], w=[("k_sb", i)], dma=True)
            mk.op("sp", lambda e, i=i, t0=t0: e.dma_start(out=kt_sb[i], in_=gktS[t0:t0 + 128, :]),
                  r=[("gtm", t0 // TB, (t0 % TB) // 128, 0)], w=[("kt_sb", i)], dma=True)
            mk.op("sp", lambda e, i=i, t0=t0: e.dma_start(out=vt_sb[i], in_=gvtS[t0:t0 + 128, :]),
                  r=[("gtm", t0 // TB, (t0 % TB) // 128, 1)], w=[("vt_sb", i)], dma=True)
            if dr == 1:
                mk.op("sp", lambda e, t0=t0: e.dma_start(out=oprev2[i], in_=goacc[:, :, t0:t0 + 128].rearrange("h v t -> v h t")),
                      r=[("goacc", t0 // 128)], w=[("oprev", i)], dma=True)
                mk.op("sp", lambda e, t0=t0: e.dma_start(out=gog2[i], in_=gogS[:, :, t0:t0 + 128].rearrange("h v t -> v h t")),
                      r=[("gfm", t0 // TB, 1024 + 128 * j) for j in range(4)], w=[("gog", i)], dma=True)
            mk.op("pe", lambda e, i=i: e.matmul(psb[0][:, 0:256], lhsT=lrt[i], rhs=w2a, start=True, stop=True),
                  r=[("lrt", i), "w2a0", "w2a1"], w=[("ps", 0)])
            mk.op("act", lambda e: e.activation(out=e_, in_=psb[0][:, 0:256], func=AF.Exp, scale=-1.0), r=[("ps", 0)], w=["e_"])
            mk.op("act", lambda e: e.activation(out=sp_, in_=e_, func=AF.Ln, bias=cs[:, 130:131], scale=1.0), r=["e_", "cs"], w=["sp_"])
            yield
            for h in range(4):
                pbk = 1 + h // 2
                mk.op("pe", lambda e, h=h, pbk=pbk: e.matmul(psb[pbk][0:64, (h % 2) * 256:(h % 2 + 1) * 256], lhsT=sp_[:, h * 64:(h + 1) * 64],
                                                           rhs=M12, start=True, stop=True),
                      r=["sp_", "gc"], w=[("ps", pbk)])
            mk.op("pe", lambda e: e.matmul(psb[3][:, 0:256], lhsT=M3, rhs=sp_, start=True, stop=True), r=["sp_", "gc"], w=[("ps", 3)])
            yield
            for hp in range(2):
                v = psb[1 + hp][0:64, :].rearrange("p (h a l) -> p h a l", h=2, a=2)
                mk.op("act", lambda e, hp=hp, v=v: e.activation(out=E1[:, 2 * hp:2 * hp + 2, :], in_=v[:, :, 0, :], func=AF.Exp),
                      r=[("ps", 1 + hp)], w=[("E1", hp)])
                mk.op("act", lambda e, hp=hp, v=v: e.activation(out=E1i[:, 2 * hp:2 * hp + 2, :], in_=v[:, :, 0, :], func=AF.Exp, scale=-1.0),
                      r=[("ps", 1 + hp)], w=[("E1i", hp)])
                mk.op("act", lambda e, hp=hp, v=v: e.activation(out=EB2[i][:, 2 * hp:2 * hp + 2, :], in_=v[:, :, 1, :], func=AF.Exp),
                      r=[("ps", 1 + hp)], w=[("EB", i, hp)])
            mk.op("act", lambda e: e.activation(out=Ed, in_=psb[3][:, 0:256], func=AF.Exp), r=[("ps", 3)], w=["Ed"])
            yield
            ek = [("E1", 0), ("E1", 1)]
            mk.op("dve", lambda e, i=i: e.scalar_tensor_tensor(out=qe, in0=q_sb[i], scalar=0.125, in1=E1, op0=ALU.mult, op1=ALU.mult),
                  r=[("q_sb", i)] + ek, w=["qe"])
            mk.op("dve", lambda e, i=i: e.tensor_tensor(out=ke, in0=k_sb[i], in1=E1i, op=ALU.mult),
                  r=[("k_sb", i), ("E1i", 0), ("E1i", 1)], w=["ke"])
            mk.op("dve", lambda e, i=i: e.scalar_tensor_tensor(out=qb2[i], in0=q_sb[i], scalar=0.125, in1=EB2[i], op0=ALU.mult, op1=ALU.mult),
                  r=[("q_sb", i), ("EB", i, 0), ("EB", i, 1)], w=[("qb", i)])
            mk.op("dve", lambda e, i=i: e.tensor_tensor(out=kd2[i], in0=kt_sb[i], in1=Ed, op=ALU.mult), r=[("kt_sb", i), "Ed"], w=[("kd", i)])
            yield
            for h in range(4):
                mk.op("pe", lambda e, h=h: e.matmul(psb[4][:, h * 128:(h + 1) * 128], lhsT=ke[:, h, :], rhs=qe[:, h, :], start=True, stop=True),
                      r=["ke", "qe"], w=[("ps", 4)])
            mk.op("dve", lambda e: e.tensor_tensor(out=attm2[i].rearrange("p h l -> p (h l)"), in0=psb[4][:], in1=mask4f, op=ALU.mult),
                  r=[("ps", 4), "gc"], w=[("attm", i)])
            for ci, cc in enumerate((0, 1) if dr == 0 else (1, 0)):
                sbk = 5 if ci == 0 else 7
                for h in range(4):
                    mk.op("pe", lambda e, h=h, cc=cc, sbk=sbk: e.matmul(psb[sbk][0:64, h * 128:(h + 1) * 128],
                                                                    lhsT=kd2[i][cc * 64:(cc + 1) * 64, h * 64:(h + 1) * 64],
                                                                    rhs=vt_sb[i][cc * 64:(cc + 1) * 64, h * 128:(h + 1) * 128], start=True, stop=True),
                          r=[("kd", i), ("vt_sb", i)], w=[("ps", sbk)])
            yield

        def tileB(n, t):
            i = n % 2
            t0 = t * 128
            seg = t // tiles_per_seg
            first_in_seg = (t % tiles_per_seg == 0) if dr == 0 else (t % tiles_per_seg == tiles_per_seg - 1)
            if first_in_seg:
                linked = (seg == 1) if dr == 0 else (seg == 0)
                if linked:
                    mk.op("dve", lambda e: e.tensor_scalar(out=S, in0=S, scalar1=pv[0:64, PV_LINK:PV_LINK + 1], scalar2=None, op0=ALU.mult),
                          r=["S", "pv"], w=["S"])
                else:
                    mk.op("dve", lambda e: e.memset(S, 0.0), r=["S"], w=["S"])
            corder = (0, 1) if dr == 0 else (1, 0)
            mk.op("act", lambda e: e.copy(out=Sb[0], in_=S), r=["S"], w=[("Sb", 0)])
            yield
            for ci, cc in enumerate(corder):
                sbk = 5 if ci == 0 else 7
                col = cc * 64 + (63 if dr == 0 else 0)
                mk.op("dve", lambda e, col=col: e.tensor_tensor(out=S, in0=S, in1=EB2[i][:, :, col:col + 1].broadcast_to([64, 4, 128]), op=ALU.mult),
                      r=[("EB", i, 0), ("EB", i, 1), "S"] + ([("Sb", 0)] if ci == 0 else []), w=["S"])
                mk.op("dve", lambda e, sbk=sbk: e.tensor_tensor(out=S.rearrange("p h v -> p (h v)"), in0=psb[sbk][0:64, :], in1=S.rearrange("p h v -> p (h v)"), op=ALU.add),
                      r=[("ps", sbk), "S"], w=["S"])
                if ci == 0:
                    mk.op("act", lambda e: e.copy(out=Sb[1], in_=S), r=["S"], w=[("Sb", 1)])
                yield
            for h in range(4):
                mk.op("pe", lambda e, h=h, i=i: e.matmul(psb[6][:, h * 128:(h + 1) * 128], lhsT=vt_sb[i][:, h * 128:(h + 1) * 128], rhs=attm2[i][:, h, :],
                                                      start=True, stop=False),
                      r=[("vt_sb", i), ("attm", i)], w=[("ps", 6)])
                for ci, cc in enumerate(corder):
                    mk.op("pe", lambda e, h=h, ci=ci, cc=cc: e.matmul(psb[6][:, h * 128 + cc * 64:h * 128 + (cc + 1) * 64], lhsT=Sb[ci][:, h, :],
                                                                   rhs=qb2[i][:, h, cc * 64:(cc + 1) * 64], start=False, stop=(ci == 1),
                                                                   skip_group_check=True),
                          r=[("Sb", ci), ("qb", i)], w=[("ps", 6)])
            if dr == 0:
                mk.op("act", lambda e: e.copy(out=osb, in_=psb[6][:]), r=[("ps", 6)], w=["osb"])
                mk.op("sp", lambda e, t0=t0: e.dma_start(out=goacc[:, :, t0:t0 + 128].rearrange("h v t -> v h t"),
                                                       in_=osb.rearrange("p (h l) -> p h l", h=4)),
                      r=["osb"], w=[("goacc", t0 // 128)], dma=True)
            else:
                mk.op("dve", lambda e: e.tensor_tensor(out=osb, in0=psb[6][:], in1=oprev2[i].rearrange("p h l -> p (h l)"), op=ALU.add),
                      r=[("ps", 6), ("oprev", i)], w=["osb"])
                mk.op("act", lambda e: e.activation(out=sqo, in_=osb, func=AF.Square), r=["osb"], w=["sqo"])
                mk.op("pe", lambda e: e.matmul(psb[7][:], lhsT=ones_bf[:], rhs=sqo, start=True, stop=True), r=["sqo", "ones_bf"], w=[("ps", 7)])
                mk.op("act", lambda e: e.activation(out=rr, in_=psb[7][:], func=AF.Ln, scale=1.0 / 128, bias=cs[:, 128:129]),
                      r=[("ps", 7), "cs"], w=["rr0"])
                mk.op("act", lambda e: e.activation(out=rr, in_=rr, func=AF.Exp, scale=-0.5), r=["rr0"], w=["rr"])
                mk.op("act", lambda e: e.activation(out=sg_, in_=gog2[i].rearrange("p h l -> p (h l)"), func=AF.Exp, scale=-1.0), r=[("gog", i)], w=["sg0"])
                mk.op("dve", lambda e: e.tensor_scalar(out=sg_, in0=sg_, scalar1=1.0, scalar2=None, op0=ALU.add), r=["sg0"], w=["sg1"])
                mk.op("dve", lambda e: e.reciprocal(out=sg_, in_=sg_), r=["sg1"], w=["sg2"])
                mk.op("dve", lambda e: e.tensor_tensor(out=sg_, in0=sg_, in1=gog2[i].rearrange("p h l -> p (h l)"), op=ALU.mult), r=["sg2", ("gog", i)], w=["sg3"])
                mk.op("dve", lambda e: e.scalar_tensor_tensor(out=rr, in0=rr, scalar=pv[:, PV_GLAW + l:PV_GLAW + l + 1], in1=sg_,
                                                            op0=ALU.mult, op1=ALU.mult), r=["rr", "sg3", "pv"], w=["rr2"])
                mk.op("dve", lambda e: e.tensor_tensor(out=oo, in0=osb, in1=rr, op=ALU.mult), r=["osb", "rr2"], w=["oo"])
                mk.op("sp", lambda e, t0=t0: e.dma_start(out=oG[:, :, t0:t0 + 128].rearrange("h v t -> v h t"),
                                                       in_=oo.rearrange("p (h l) -> p h l", h=4)),
                      r=["oo"], w=[("oG", t0 // TB)], dma=True)
            yield

        def drive(*gens):
            gens = [x for x in gens if x is not None]
            while gens:
                for x in list(gens):
                    try:
                        next(x)
                    except StopIteration:
                        gens.remove(x)

        drive(tileA(0, order[0]))
        for n, t in enumerate(order):
            drive(tileA(n + 1, order[n + 1]) if n + 1 < NT else None, tileB(n, t))

    def ssd_inproj_setup(l, stg):
        NW = 3104
        wz = A.alloc([128, 8, NW], BF16)
        ob_ = [A.alloc([128, TB], BF16) for i in range(2)]
        zt = [A.alloc([128, 1024], BF16) for i in range(2)]
        dtt = [A.alloc([128, 32], F32) for i in range(2)]
        for k, n0 in enumerate(range(0, NW, 1024)):
            n1 = min(NW, n0 + 1024)
            load_weight(wz[:, :, n0:n1], w_in[l][:, n0:n1], 8, n1 - n0, stg, scale_col0=PV_NORM1 + 8 * l, tag="wz%d" % k)

        def wkey(c, col):
            return ("W", "wz%d" % (col // 1024), c, 0)
        itc = [0]

        def body(b, hb, par):
            for j in range(16):
                i = itc[0] % 2
                itc[0] += 1
                pa = 1 + i
                col = 1024 + 128 * j
                for c in range(8):
                    mk.op("pe", lambda e, c=c, col=col, pa=pa: e.matmul(psb[pa][:], lhsT=wz[:, c, col:col + 128], rhs=hb[:, c, :],
                                                                    start=(c == 0), stop=(c == 7)),
                          r=[("hb", par, c), wkey(c, col)], w=[("ps", pa)])
                evac(("act", "dve")[i], ob_[i], psb[pa][:], r=[("ps", pa)], w=[("ob_", i)])
                mk.op("sp", lambda e, i=i, j=j, b=b: e.dma_start(out=xbcS[j, :, b * TB:(b + 1) * TB], in_=ob_[i]),
                      r=[("ob_", i)], w=[("xbcS", b)], dma=True)
            for tt in range(4):
                i = tt % 2
                for hf in range(2):
                    for c in range(8):
                        mk.op("pe", lambda e, c=c, tt=tt, hf=hf, i=i: e.matmul(psb[3 + 2 * i + hf][:], lhsT=hb[:, c, tt * 128:(tt + 1) * 128],
                                                                            rhs=wz[:, c, hf * 512:(hf + 1) * 512], start=(c == 0), stop=(c == 7)),
                              r=[("hb", par, c), wkey(c, 0)], w=[("ps", 3 + 2 * i + hf)])
                    evac(("act", "dve")[hf], zt[i][:, hf * 512:(hf + 1) * 512], psb[3 + 2 * i + hf][:], r=[("ps", 3 + 2 * i + hf)], w=[("zt", i, hf)])
                for c in range(8):
                    mk.op("pe", lambda e, c=c, tt=tt: e.matmul(psb[7][:, 0:32], lhsT=hb[:, c, tt * 128:(tt + 1) * 128],
                                                             rhs=wz[:, c, 3072:3104], start=(c == 0), stop=(c == 7)),
                          r=[("hb", par, c), wkey(c, 3072)], w=[("ps", 7)])
                evac("act", dtt[i], psb[7][:, 0:32], r=[("ps", 7)], w=[("dtt", i)])
                t0 = b * TB + tt * 128
                mk.op("sp", lambda e, i=i, t0=t0: e.dma_start(out=zS[t0:t0 + 128, :], in_=zt[i]), r=[("zt", i, 0), ("zt", i, 1)], w=[("zS", t0 // 128)], dma=True)
                mk.op("sp", lambda e, i=i, t0=t0: e.dma_start(out=dtS[t0:t0 + 128, :], in_=dtt[i]), r=[("dtt", i)], w=[("dtS", t0 // 128)], dma=True)
        return body

    def ssd_pass(l, dr):
        phase()
        sc = A.alloc([128, NSC], F32)
        mk.op("sp", lambda e: e.dma_start(out=sc, in_=scst), w=["sc"], dma=True)
        sr = A.alloc([128, NSR], F32)
        mk.op("sp", lambda e: e.dma_start(out=sr, in_=srow[l]), w=["sr"], dma=True)
        U = sc[:, dr * 256:dr * 256 + 128]
        SL = sc[:, dr * 256 + 128:dr * 256 + 256]
        mask4 = sc[:, 512 + dr * 512:512 + (dr + 1) * 512]
        ones_f = A.alloc([128, 128], F32)
        mk.op("dve", lambda e: e.memset(ones_f, 1.0), w=["ones_f"])
        arow = A.alloc([128, 16], F32)
        mk.op("act", lambda e: e.activation(out=arow, in_=sr[:, 32 + dr * 16:48 + dr * 16], func=AF.Exp), r=["sr"], w=["arow0"])
        mk.op("dve", lambda e: e.tensor_scalar(out=arow, in0=arow, scalar1=-1.0, scalar2=None, op0=ALU.mult), r=["arow0"], w=["arow"])
        S = A.alloc([128, 1024], F32)
        Sbf = A.alloc([128, 1024], BF16)
        Wt = A.alloc([128, 16, 128], BF16)
        xw = A.alloc([128, 1024], BF16)
        xdt = A.alloc([128, 1024], BF16)
        yd = A.alloc([128, 1024], F32)
        y = A.alloc([128, 1024], F32)
        adtU = A.alloc([128, 16, 128], F32)
        D2 = lambda shape, dt_: [A.alloc(shape, dt_) for _ in range(2)]
        D3 = lambda shape, dt_: [A.alloc(shape, dt_) for _ in range(3)]
        dtr = D3([128, 32], F32)
        dt = D2([128, 16], F32)
        ldt = D2([128, 16], F32)
        adt = D2([128, 16], F32)
        ex = D2([128, 48], F32)
        wdec = D2([128, 16], F32)
        Lm = D2([128, 16, 128], BF16)
        SM = D2([128, 4, 128], BF16)
        xtok = D2([128, 1024], BF16)
        btok = D2([128, 512], BF16)
        if dr == 0:
            dg = A.alloc([128, 80, 128], BF16)
            for j in range(16):
                for k in range(5):
                    cb = PV_CONV + (l * 16 + j) * 6 + k
                    eng = ("pool", "dve", "act")[(j * 5 + k) % 3]
                    if eng == "act":
                        mk.op("act", lambda e, j=j, k=k, cb=cb: e.activation(out=dg[:, j * 5 + k, :], in_=ident, func=AF.Copy, scale=pv[:, cb:cb + 1]),
                              r=["cs", "pv"], w=[("dg", j)])
                    else:
                        mk.op(eng, lambda e, j=j, k=k, cb=cb: e.tensor_scalar(out=dg[:, j * 5 + k, :], in0=ident, scalar1=pv[:, cb:cb + 1],
                                                                          scalar2=None, op0=ALU.mult), r=["cs", "pv"], w=[("dg", j)])
            xh = D3([128, 16, 132], BF16)
            xc = D2([128, 16, 128], BF16)
        else:
            bc = D3([128, 8, 128], BF16)
            yprev = D3([128, 1024], F32)
            zt = D3([128, 1024], BF16)
            xtok3 = D3([128, 1024], BF16)
            btok3 = D3([128, 512], BF16)
            zs = A.alloc([128, 1024], F32)
            tmpd = A.alloc([128, 1024], F32)
            ss = A.alloc([128, 2], F32)
            yn = A.alloc([128, 1024], BF16)
            yT = A.alloc([128, 8, 128], BF16)
        NT = T // 128
        tps = SEGL // 128
        order = list(range(NT)) if dr == 0 else list(range(NT - 1, -1, -1))

        def stage0(n, t):
            q3 = n % 3
            t0 = t * 128
            seg = t // tps
            K3 = lambda nm: (nm, "q", q3)
            mk.op("sp", lambda e: e.dma_start(out=dtr[q3], in_=dtS[t0:t0 + 128, :]), r=[("dtS", t)], w=[K3("dtr")], dma=True)
            if dr == 0:
                xh_ = xh[q3]
                mk.op("sp", lambda e: e.dma_start(out=xh_[:, :, 2:130], in_=xbcS[:, :, t0:t0 + 128].rearrange("c p t -> p c t")),
                      r=[("xbcS", t0 // TB)], w=[K3("xh")], dma=True)
                for side in range(2):
                    at_edge = (t % tps == 0) if side == 0 else (t % tps == tps - 1)
                    lk = at_edge and ((seg == 1 and side == 0) or (seg == 0 and side == 1))
                    dcols = slice(0, 2) if side == 0 else slice(130, 132)
                    src0 = t0 - 2 if side == 0 else t0 + 128
                    if at_edge and not lk:
                        mk.op("pool", lambda e, dcols=dcols: e.memset(xh_[:, :, dcols], 0.0), w=[("xhh", q3, side)])
                    else:
                        nb_ = (src0 // TB)
                        mk.op("sp", lambda e, dcols=dcols, src0=src0: e.dma_start(
                            out=xh_[:, :, dcols], in_=xbcS[:, :, src0:src0 + 2].rearrange("c p t -> p c t")),
                            r=[("xbcS", nb_)], w=[("xhh", q3, side)], dma=True)
                        if lk:
                            mk.op("pool", lambda e, dcols=dcols: e.tensor_scalar(out=xh_[:, :, dcols], in0=xh_[:, :, dcols],
                                                                              scalar1=pv[:, PV_LINK:PV_LINK + 1], scalar2=None, op0=ALU.mult),
                                  r=[("xhh", q3, side), "pv"], w=[("xhh", q3, side)])
            else:
                mk.op("sp", lambda e: e.dma_start(out=bc[q3], in_=xcS[:, :, t0:t0 + 128].rearrange("c p t -> p c t")),
                      r=[("xcS", t)], w=[K3("bc")], dma=True)
                mk.op("sp", lambda e: e.dma_start(out=xtok3[q3], in_=xtokS[t0:t0 + 128, :]), r=[("xtokS", t)], w=[K3("xtok")], dma=True)
                mk.op("sp", lambda e: e.dma_start(out=btok3[q3], in_=btokS[t0:t0 + 128, :]), r=[("btokS", t)], w=[K3("btok")], dma=True)
                mk.op("sp", lambda e: e.dma_start(out=yprev[q3], in_=yaccS[t0:t0 + 128, :]), r=[("yaccS", t)], w=[K3("yprev")], dma=True)
                mk.op("sp", lambda e: e.dma_start(out=zt[q3], in_=zS[t0:t0 + 128, :]), r=[("zS", t)], w=[K3("zt")], dma=True)

        def stage1(n, t):
            p = n % 2
            q3 = n % 3
            t0 = t * 128
            seg = t // tps
            K = lambda nm: (nm, p)
            K3 = lambda nm: (nm, "q", q3)
            if dr == 0:
                xh_, xc_ = xh[q3], xc[p]
                xk = [K3("xh"), ("xhh", q3, 0), ("xhh", q3, 1)]
                for j in range(16):
                    bk = j % 4
                    for k in range(5):
                        mk.op("pe", lambda e, j=j, k=k, bk=bk: e.matmul(psb[bk][:, (j // 4) * 128:(j // 4 + 1) * 128], lhsT=dg[:, j * 5 + k, :],
                                                                    rhs=xh_[:, j, k:k + 128], start=(k == 0), stop=(k == 4)),
                              r=xk + [("dg", j)], w=[("ps", bk)])
                    cb = PV_CONV + (l * 16 + j) * 6 + 5
                    mk.op("act", lambda e, j=j, bk=bk, cb=cb: e.activation(out=xc_[:, j, :], in_=psb[bk][:, (j // 4) * 128:(j // 4 + 1) * 128],
                                                                       func=AF.Silu, bias=pv[:, cb:cb + 1]),
                          r=[("ps", bk), "pv"], w=[("xc", p, j // 8)])
                    yield
                mk.op("sp", lambda e: e.dma_start(out=xcS[:, :, t0:t0 + 128].rearrange("c p t -> p c t"), in_=xc_[:, 8:16, :]),
                      r=[("xc", p, 1)], w=[("xcS", t)], dma=True)
                pv_ = psb[0][:].bitcast(BF16)
                for j in range(8):
                    mk.op("pe", lambda e, j=j: e.transpose(out=pv_[:, j * 128:(j + 1) * 128], in_=xc_[:, j, :], identity=ident_bf[:]),
                          r=[("xc", p, 0), "ident_bf"], w=[("ps", 0)])
                mk.op("act", lambda e: e.copy(out=xtok[p], in_=pv_), r=[("ps", 0)], w=[K("xtok")])
                yield
                pb_ = psb[1][:].bitcast(BF16)
                for j in range(4):
                    mk.op("pe", lambda e, j=j: e.transpose(out=pb_[:, j * 128:(j + 1) * 128], in_=xc_[:, 8 + j, :], identity=ident_bf[:]),
                          r=[("xc", p, 1), "ident_bf"], w=[("ps", 1)])
                mk.op("dve", lambda e: e.tensor_copy(out=btok[p], in_=pb_[:, 0:512]), r=[("ps", 1)], w=[K("btok")])
                yield
                mk.op("sp", lambda e: e.dma_start(out=xtokS[t0:t0 + 128, :], in_=xtok[p]), r=[K("xtok")], w=[("xtokS", t)], dma=True)
                mk.op("sp", lambda e: e.dma_start(out=btokS[t0:t0 + 128, :], in_=btok[p]), r=[K("btok")], w=[("btokS", t)], dma=True)
                BC = xc_[:, 8:16, :]
                bck = [("xc", p, 1)]
            else:
                BC = bc[q3]
                bck = [K3("bc")]
            dt_, adt_, ex_, ldt_ = dt[p], adt[p], ex[p], ldt[p]
            mk.op("dve", lambda e: e.tensor_tensor(out=dt_, in0=dtr[q3][:, dr * 16:(dr + 1) * 16], in1=sr[:, dr * 16:(dr + 1) * 16], op=ALU.add),
                  r=[K3("dtr"), "sr"], w=[K("dt0")])
            mk.op("act", lambda e: e.activation(out=dt_, in_=dt_, func=AF.Exp), r=[K("dt0")], w=[K("dt1")])
            mk.op("act", lambda e: e.activation(out=dt_, in_=dt_, func=AF.Ln, bias=cs[:, 130:131], scale=1.0), r=[K("dt1"), "cs"], w=[K("dt")])
            mk.op("dve", lambda e: e.tensor_tensor(out=adt_, in0=dt_, in1=arow, op=ALU.mult), r=[K("dt"), "arow"], w=[K("adt")])
            yield
            mk.op("pe", lambda e: e.matmul(psb[2][:, 0:16], lhsT=U, rhs=adt_, start=True, stop=True), r=[K("adt"), "sc"], w=[("ps", 2)])
            mk.op("pe", lambda e: e.matmul(psb[2][:, 16:32], lhsT=SL, rhs=adt_, start=True, stop=True), r=[K("adt"), "sc"], w=[("ps", 2)])
            mk.op("pe", lambda e: e.matmul(psb[2][:, 32:48], lhsT=ones_f, rhs=adt_, start=True, stop=True), r=[K("adt"), "ones_f"], w=[("ps", 2)])
            mk.op("act", lambda e: e.activation(out=ex_, in_=psb[2][:, 0:48], func=AF.Exp), r=[("ps", 2)], w=[K("ex")])
            mk.op("dve", lambda e: e.tensor_tensor(out=wdec[p], in0=dt_, in1=ex_[:, 16:32], op=ALU.mult), r=[K("dt"), K("ex")], w=[K("wdec")])
            yield
            mk.op("dve", lambda e: e.tensor_tensor(out=adtU, in0=U.unsqueeze(1).broadcast_to([128, 16, 128]),
                                                   in1=adt_.unsqueeze(2).broadcast_to([128, 16, 128]), op=ALU.mult),
                  r=[K("adt"), "sc"], w=[("adtU", q) for q in range(4)])
            yield
            for q in range(4):
                mk.op("pe", lambda e, q=q: e.matmul(psb[q][:], lhsT=SL, rhs=adtU[:, 4 * q:4 * q + 4, :].rearrange("p h l -> p (h l)"),
                                                  start=True, stop=True), r=[("adtU", q), "sc"], w=[("ps", q)])
                mk.op("act", lambda e, q=q: e.activation(out=Lm[p][:, 4 * q:4 * q + 4, :].rearrange("p h l -> p (h l)"), in_=psb[q][:], func=AF.Exp),
                      r=[("ps", q)], w=[("Lm", p, q)])
                yield
            for g in range(4):
                mk.op("pe", lambda e, g=g: e.matmul(psb[3][:, g * 128:(g + 1) * 128], lhsT=BC[:, g, :], rhs=BC[:, 4 + g, :], start=True, stop=True),
                      r=bck, w=[("ps", 3)])
            mk.op("dve", lambda e: e.tensor_tensor(out=SM[p].rearrange("p g l -> p (g l)"), in0=psb[3][:], in1=mask4, op=ALU.mult),
                  r=[("ps", 3), "sc"], w=[K("SM")])
            yield
            return

        def stage2(n, t):
            p = n % 2
            t0 = t * 128
            seg = t // tps
            K = lambda nm: (nm, p)
            if dr == 0:
                BC = xc[p][:, 8:16, :]
                bck = [("xc", p, 1)]
            else:
                BC = bc[n % 3]
                bck = [("bc", "q", n % 3)]
            first_in_seg = (t % tps == 0) if dr == 0 else (t % tps == tps - 1)
            if first_in_seg:
                linked = (seg == 1) if dr == 0 else (seg == 0)
                if linked:
                    mk.op("dve", lambda e: e.tensor_scalar(out=S, in0=S, scalar1=pv[:, PV_LINK:PV_LINK + 1], scalar2=None, op0=ALU.mult),
                          r=["S", "pv"], w=["S"])
                else:
                    mk.op("dve", lambda e: e.memset(S, 0.0), r=["S"], w=["S"])
            q3 = n % 3
            K3 = lambda nm: (nm, "q", q3)
            if dr == 0:
                ex_, xtok_, btok_ = ex[p], xtok[p], btok[p]
                kx, kb = ("xtok", p), ("btok", p)
            else:
                ex_, xtok_, btok_ = ex[p], xtok3[q3], btok3[q3]
                kx, kb = K3("xtok"), K3("btok")
            mk.op("dve", lambda e: e.tensor_tensor(out=Wt.rearrange("p (g a) l -> p g a l", g=4), in0=Lm[p].rearrange("p (g a) l -> p g a l", g=4),
                                                   in1=SM[p].unsqueeze(2).broadcast_to([128, 4, 4, 128]), op=ALU.mult),
                  r=[("Lm", p, q) for q in range(4)] + [K("SM")], w=["Wt"])
            mk.op("dve", lambda e: e.tensor_tensor(out=xdt.rearrange("p (h d) -> p h d", h=16), in0=xtok_.rearrange("p (h d) -> p h d", h=16),
                                                   in1=dt[p].unsqueeze(2).broadcast_to([128, 16, 64]), op=ALU.mult),
                  r=[kx, K("dt")], w=["xdt"])
            yield
            for h in range(16):
                mk.op("pe", lambda e, h=h: e.matmul(psb[4 + h // 8][:, (h % 8) * 64:(h % 8 + 1) * 64], lhsT=Wt[:, h, :], rhs=xdt[:, h * 64:(h + 1) * 64],
                                                  start=True, stop=True), r=["Wt", "xdt"], w=[("ps", 4 + h // 8)])
                if h % 4 == 3:
                    yield
            mk.op("act", lambda e: e.copy(out=Sbf, in_=S), r=["S"], w=["Sbf"])
            for g in range(4):
                mk.op("pe", lambda e, g=g: e.matmul(psb[6 + g // 2][:, (g % 2) * 256:(g % 2 + 1) * 256], lhsT=BC[:, 4 + g, :],
                                                  rhs=Sbf[:, g * 256:(g + 1) * 256], start=True, stop=True),
                      r=bck + ["Sbf"], w=[("ps", 6 + g // 2)])
            yield
            mk.op("dve", lambda e: e.tensor_tensor(out=y.rearrange("p (h d) -> p h d", h=16), in0=psw[3][:].rearrange("p (h d) -> p h d", h=16),
                                                   in1=ex_[:, 0:16].unsqueeze(2).broadcast_to([128, 16, 64]), op=ALU.mult),
                  r=[("ps", 6), ("ps", 7), K("ex")], w=["y0"])
            mk.op("dve", lambda e: e.tensor_tensor(out=y, in0=psw[2][:], in1=y, op=ALU.add), r=[("ps", 4), ("ps", 5), "y0"], w=["yfull"])
            mk.op("dve", lambda e: e.tensor_tensor(out=xw.rearrange("p (h d) -> p h d", h=16), in0=xtok_.rearrange("p (h d) -> p h d", h=16),
                                                   in1=wdec[p].unsqueeze(2).broadcast_to([128, 16, 64]), op=ALU.mult),
                  r=[kx, K("wdec")], w=["xw"])
            yield
            for g in range(4):
                mk.op("pe", lambda e, g=g: e.matmul(psb[4 + g // 2][:, (g % 2) * 256:(g % 2 + 1) * 256], lhsT=btok_[:, g * 128:(g + 1) * 128],
                                                  rhs=xw[:, g * 256:(g + 1) * 256], start=True, stop=True),
                      r=[kb, "xw"], w=[("ps", 4 + g // 2)])
            mk.op("dve", lambda e: e.tensor_tensor(out=S.rearrange("p (h d) -> p h d", h=16), in0=S.rearrange("p (h d) -> p h d", h=16),
                                                   in1=ex_[:, 32:48].unsqueeze(2).broadcast_to([128, 16, 64]), op=ALU.mult),
                  r=[K("ex"), "S", "Sbf"], w=["S"])
            mk.op("dve", lambda e: e.tensor_tensor(out=S, in0=psw[2][:], in1=S, op=ALU.add), r=[("ps", 4), ("ps", 5), "S"], w=["S"])
            yield
            yk = ["yfull"]
            if dr == 0:
                mk.op("sp", lambda e: e.dma_start(out=yaccS[t0:t0 + 128, :], in_=y), r=yk, w=[("yaccS", t)], dma=True)
            else:
                mk.op("dve", lambda e: e.tensor_tensor(out=tmpd, in0=xtok_, in1=sr[:, 1104:2128], op=ALU.mult), r=[kx, "sr"], w=["tmpd"])
                mk.op("dve", lambda e: e.tensor_tensor(out=tmpd, in0=tmpd, in1=yprev[q3], op=ALU.add), r=["tmpd", K3("yprev")], w=["tmpd2"])
                mk.op("dve", lambda e: e.tensor_tensor(out=y, in0=y, in1=tmpd, op=ALU.add), r=yk + ["tmpd2"], w=["y3"])
                mk.op("act", lambda e: e.activation(out=zs, in_=zt[q3], func=AF.Silu), r=[K3("zt")], w=["zs"])
                mk.op("dve", lambda e: e.tensor_tensor(out=y, in0=y, in1=zs, op=ALU.mult), r=["y3", "zs"], w=["y4"])
                yield
                mk.op("act", lambda e: e.activation(out=zs, in_=y, func=AF.Square, accum_out=ss[:, 0:1]), r=["y4", "zs"], w=["ss0"])
                mk.op("act", lambda e: e.activation(out=ss[:, 1:2], in_=ss[:, 0:1], func=AF.Ln, scale=1.0 / 1024, bias=cs[:, 128:129]),
                      r=["ss0", "cs"], w=["ss1"])
                mk.op("act", lambda e: e.activation(out=ss[:, 1:2], in_=ss[:, 1:2], func=AF.Exp, scale=-0.5), r=["ss1"], w=["ss2"])
                mk.op("dve", lambda e: e.scalar_tensor_tensor(out=yn, in0=y, scalar=ss[:, 1:2], in1=sr[:, 80:80 + 1024], op0=ALU.mult, op1=ALU.mult),
                      r=["y4", "ss2", "sr"], w=["yn"])
                yield
                po_ = psb[6][:].bitcast(BF16)
                for j in range(8):
                    mk.op("pe", lambda e, j=j: e.transpose(out=po_[:, j * 128:(j + 1) * 128], in_=yn[:, j * 128:(j + 1) * 128], identity=ident_bf[:]),
                          r=["yn", "ident_bf"], w=[("ps", 6)])
                mk.op("act", lambda e: e.copy(out=yT.rearrange("p c t -> p (c t)"), in_=po_), r=[("ps", 6)], w=["yT"])
                mk.op("sp", lambda e: e.dma_start(out=oSS[:, :, t0:t0 + 128].rearrange("c p t -> p c t"), in_=yT),
                      r=["yT"], w=[("oSS", t0 // TB)], dma=True)

        def drive(*gens):
            gens = [x for x in gens if x is not None]
            while gens:
                for x in list(gens):
                    try:
                        next(x)
                    except StopIteration:
                        gens.remove(x)

        stage0(0, order[0])
        if NT > 1:
            stage0(1, order[1])
        drive(stage1(0, order[0]))
        for n, t in enumerate(order):
            if n + 2 < NT:
                stage0(n + 2, order[n + 2])
            drive(stage1(n + 1, order[n + 1]) if n + 1 < NT else None, stage2(n, t))

    def merge_phase(l, branches):
        phase()
        wg = A.alloc([128, 8, 3 * D], BF16)
        wbr = {"ssd": A.alloc([128, 8, D], BF16), "gla": A.alloc([128, 4, D], BF16), "att": A.alloc([128, 4, D], BF16)}
        wo = A.alloc([128, 8, D], BF16)
        stg = [A.alloc([128, PIECE], F32) for i in range(2)]
        xb2 = [A.alloc([128, 8, TB], F32) for i in range(2)]
        hb2 = [A.alloc([128, 8, TB], BF16) for i in range(2)]
        sq = [A.alloc([128, TB], BF16) for i in range(2)]
        rstd2 = [A.alloc([128, TB], F32) for i in range(2)]
        bin_ = {"ssd": A.alloc([128, 8, TB], BF16), "gla": A.alloc([128, 4, TB], BF16), "att": A.alloc([128, 4, TB], BF16)}
        sg = [A.alloc([128, TB], F32) for i in range(2)]
        macc = [A.alloc([128, TB], F32) for i in range(2)]
        mg = A.alloc([128, 8, TB], BF16)
        xo = [A.alloc([128, TB], F32) for i in range(2)]
        bidx = {"ssd": 0, "gla": 1, "att": 2}
        bsrc = {"ssd": w_br_ssd, "gla": w_br_gla, "att": w_br_attn}
        bkc = {"ssd": 8, "gla": 4, "att": 4}
        for br in branches:
            gi = bidx[br]
            load_weight(wg[:, :, gi * D:(gi + 1) * D], w_in[l][:, C_G + gi * D:C_G + (gi + 1) * D], 8, D, stg,
                        scale_col0=PV_NORM1 + 8 * l, tag="wg" + br)
            load_weight(wbr[br], bsrc[br][l], bkc[br], D, stg, tag="wbr" + br)
        load_weight(wo, w_out[l], 8, D, stg, tag="wo")
        itc = [0]
        load_xblock(xb2[0], 0, 0)
        rms_block(xb2[0], hb2[0], sq, rstd2[0], 0, 0)

        def block(b, par, xb, hb):
            if b + 1 < NB:
                load_xblock(xb2[1 - par], b + 1, 1 - par)
            for br in branches:
                if br == "ssd":
                    mk.op("sp", lambda e, b=b: e.dma_start(
                        out=bin_["ssd"], in_=oSS[:, :, b * TB:(b + 1) * TB].rearrange("c p t -> p c t")),
                        r=[("oSS", b)], w=[("bin", "ssd")], dma=True)
                if br == "gla":
                    mk.op("sp", lambda e, b=b: e.dma_start(
                        out=bin_["gla"], in_=oG[:, :, b * TB:(b + 1) * TB].rearrange("c p t -> p c t")),
                        r=[("oG", b)], w=[("bin", "gla")], dma=True)
                if br == "att":
                    mk.op("sp", lambda e, b=b: e.dma_start(
                        out=bin_["att"], in_=oS[:, :, b * TB:(b + 1) * TB].rearrange("(c two) d t -> (two d) c t", two=2)),
                        r=[("oS", b, h) for h in range(8)], w=[("bin", "att")], dma=True)
            for m in range(8):
                mi = m % 2
                for bi, br in enumerate(branches):
                    gi = bidx[br]
                    i = itc[0] % 2
                    itc[0] += 1
                    pgt, pbr = 1 + i, 3 + i
                    for c in range(8):
                        mk.op("pe", lambda e, c=c, m=m, gi=gi, pgt=pgt: e.matmul(
                            psb[pgt][:], lhsT=wg[:, c, gi * D + m * 128:gi * D + (m + 1) * 128], rhs=hb[:, c, :],
                            start=(c == 0), stop=(c == 7)),
                            r=[("hb", par, c), ("W", "wg" + br, c, 0)], w=[("ps", pgt)])
                    mk.op("act", lambda e, i=i, pgt=pgt: e.activation(out=sg[i], in_=psb[pgt][:], func=AF.Sigmoid),
                          r=[("ps", pgt)], w=[("sg", i)])
                    kc = bkc[br]
                    for c in range(kc):
                        mk.op("pe", lambda e, c=c, m=m, br=br, pbr=pbr, kc=kc: e.matmul(
                            psb[pbr][:], lhsT=wbr[br][:, c, m * 128:(m + 1) * 128], rhs=bin_[br][:, c, :],
                            start=(c == 0), stop=(c == kc - 1)),
                            r=[("bin", br), ("W", "wbr" + br, c, 0)], w=[("ps", pbr)])
                    last = (bi == len(branches) - 1)
                    if bi == 0:
                        dst = mg[:, m, :] if last else macc[mi]
                        mk.op("dve", lambda e, i=i, pbr=pbr, dst=dst: e.tensor_tensor(out=dst, in0=psb[pbr][:], in1=sg[i], op=ALU.mult),
                              r=[("ps", pbr), ("sg", i)], w=[("mg", m) if last else ("macc", mi)])
                    else:
                        mk.op("dve", lambda e, i=i, pbr=pbr: e.tensor_tensor(out=sg[i], in0=psb[pbr][:], in1=sg[i], op=ALU.mult),
                              r=[("ps", pbr), ("sg", i)], w=[("sg", i)])
                        dst = mg[:, m, :] if last else macc[mi]
                        mk.op("dve", lambda e, i=i, mi=mi, dst=dst: e.tensor_tensor(out=dst, in0=macc[mi], in1=sg[i], op=ALU.add),
                              r=[("macc", mi), ("sg", i)], w=[("mg", m) if last else ("macc", mi)])
            if b + 1 < NB:
                rms_block(xb2[1 - par], hb2[1 - par], sq, rstd2[1 - par], 0, 1 - par)
            for m in range(8):
                po = 5 + m % 3
                for c in range(8):
                    mk.op("pe", lambda e, c=c, m=m, po=po: e.matmul(psb[po][:], lhsT=wo[:, c, m * 128:(m + 1) * 128], rhs=mg[:, c, :],
                                                                 start=(c == 0), stop=(c == 7)),
                          r=[("mg", c), ("W", "wo", c, 0)], w=[("ps", po)])
                xoi = xo[m % 2]
                mk.op("dve", lambda e, m=m, po=po, xoi=xoi: e.tensor_tensor(out=xoi, in0=psb[po][:], in1=xb[:, m, :], op=ALU.add),
                      r=[("ps", po), ("xb", par, m)], w=[("xo", m % 2)])
                mk.op("sp", lambda e, m=m, b=b, xoi=xoi: e.dma_start(out=xT[m, :, b * TB:(b + 1) * TB], in_=xoi),
                      r=[("xo", m % 2)], w=[("xT", b)], dma=True)

        for b in range(NB):
            block(b, b % 2, xb2[b % 2], hb2[b % 2])

    def ffn_phase(l):
        phase()
        wfi = A.alloc([128, 8, 2 * DFF], BF16)
        wfo = A.alloc([128, 22, D], BF16)
        stg = [A.alloc([128, PIECE], F32) for i in range(1)]
        xb = A.alloc([128, 8, TB], F32)
        hb2 = [A.alloc([128, 8, TB], BF16) for i in range(2)]
        hid = A.alloc([128, 22, TB], BF16)
        sq = [A.alloc([128, TB], BF16) for i in range(2)]
        rstd = A.alloc([128, TB], F32)
        tmpf = [A.alloc([128, TB], F32) for i in range(2)]
        xo = [A.alloc([128, TB], F32) for i in range(2)]
        load_weight(wfi, w_ffn_in[l], 8, 2 * DFF, stg, scale_col0=PV_NORM2 + 8 * l, tag="ffn_in")
        load_weight(wfo, w_ffn_out[l], 22, D, stg, tag="ffn_out")
        load_xblock(xb, 0, 0)
        rms_block(xb, hb2[0], sq, rstd, 0, 0, xpar=0)

        def block(b, par, hb):
            if b + 1 < NB:
                load_xblock(xb, b + 1, 0)
            for j in range(22):
                pg = 1 + (2 * j) % 6
                pu = pg + 1
                for c in range(8):
                    mk.op("pe", lambda e, c=c, j=j, pg=pg: e.matmul(psb[pg][:], lhsT=wfi[:, c, j * 128:(j + 1) * 128],
                                                                 rhs=hb[:, c, :], start=(c == 0), stop=(c == 7)),
                          r=[("hb", par, c), ("W", "ffn_in", c, (j * 128) // PIECE)], w=[("ps", pg)])
                for c in range(8):
                    mk.op("pe", lambda e, c=c, j=j, pu=pu: e.matmul(psb[pu][:], lhsT=wfi[:, c, DFF + j * 128:DFF + (j + 1) * 128],
                                                                 rhs=hb[:, c, :], start=(c == 0), stop=(c == 7)),
                          r=[("hb", par, c), ("W", "ffn_in", c, (DFF + j * 128) // PIECE)], w=[("ps", pu)])
                tf = tmpf[j % 2]
                mk.op("act", lambda e, pg=pg, tf=tf: e.activation(out=tf, in_=psb[pg][:], func=AF.Silu),
                      r=[("ps", pg)], w=[("tmpf", j % 2)])
                mk.op("dve", lambda e, pu=pu, tf=tf, j=j: e.tensor_tensor(out=hid[:, j, :], in0=psb[pu][:], in1=tf, op=ALU.mult),
                      r=[("ps", pu), ("tmpf", j % 2)], w=[("hid", j)])
                if j == 12 and b + 1 < NB:
                    rms_block(xb, hb2[1 - par], sq, rstd, 0, 1 - par, xpar=0)

            def load_res(m):
                mk.op("sp", lambda e: e.dma_start(out=tmpf[m % 2], in_=xT[m, :, b * TB:(b + 1) * TB]),
                      r=[("xT", b)], w=[("tmpf", m % 2)], dma=True)
            load_res(0)
            load_res(1)
            for m in range(8):
                po = 1 + m % 7
                for j in range(22):
                    mk.op("pe", lambda e, m=m, j=j, po=po: e.matmul(psb[po][:], lhsT=wfo[:, j, m * 128:(m + 1) * 128],
                                                                 rhs=hid[:, j, :], start=(j == 0), stop=(j == 21)),
                          r=[("hid", j), ("W", "ffn_out", j, 0)], w=[("ps", po)])
                xoi = xo[m % 2]
                mk.op("dve", lambda e, m=m, po=po, xoi=xoi: e.tensor_tensor(out=xoi, in0=psb[po][:], in1=tmpf[m % 2], op=ALU.add),
                      r=[("ps", po), ("tmpf", m % 2)], w=[("xo", m % 2)])
                if m + 2 < 8:
                    load_res(m + 2)
                mk.op("sp", lambda e, m=m, xoi=xoi: e.dma_start(out=xT[m, :, b * TB:(b + 1) * TB], in_=xoi),
                      r=[("xo", m % 2)], w=[("xT", b)], dma=True)

        for b in range(NB):
            block(b, b % 2, hb2[b % 2])

    def final_phase():
        phase()
        xb2 = [A.alloc([128, 8, TB], F32) for i in range(2)]
        sq = [A.alloc([128, TB], BF16) for i in range(2)]
        rstd2 = [A.alloc([128, TB], F32) for i in range(2)]
        yt = [A.alloc([128, D], F32) for i in range(4)]
        outs = [dbg_o]
        load_xblock(xb2[0], 0, 0)

        def block(b, par, xb, rstd):
            if b + 1 < NB:
                load_xblock(xb2[1 - par], b + 1, 1 - par)
            rms_block(xb, None, sq, rstd, 0, par)
            for c in range(8):
                mk.op("dve", lambda e, c=c: e.scalar_tensor_tensor(out=xb[:, c, :], in0=xb[:, c, :], scalar=pv[:, PV_FINAL + c:PV_FINAL + c + 1],
                                                               in1=rstd, op0=ALU.mult, op1=ALU.mult),
                      r=[("xb", par, c), ("rstd", par), "pv"], w=[("xb", par, c)])
            for tt in range(4):
                i = (b * 4 + tt) % 4
                for half in range(2):
                    pb = 1 + ((b * 4 + tt) * 2 + half) % 7
                    for c4 in range(4):
                        c = half * 4 + c4
                        mk.op("pe", lambda e, c=c, c4=c4, tt=tt, pb=pb: e.transpose(
                            out=psb[pb][:, c4 * 128:(c4 + 1) * 128], in_=xb[:, c, tt * 128:(tt + 1) * 128], identity=ident),
                            r=[("xb", par, c), "cs"], w=[("ps", pb)])
                    evac(("act", "dve")[half], yt[i][:, half * 512:(half + 1) * 512], psb[pb][:], r=[("ps", pb)], w=[("yt", i, half)])
                t = b * 4 + tt
                o = mk.op("sp", lambda e, t=t, i=i: e.dma_start(out=y_out[t * 128:(t + 1) * 128, :], in_=yt[i]),
                          r=[("yt", i, 0), ("yt", i, 1)], w=[("yout", t)], dma=True)
                outs.append(o)

        for b in range(NB):
            block(b, b % 2, xb2[b % 2], rstd2[b % 2])
        return outs

    for l in range(DEPTH):
        if STAGE >= 1:
            inproj_phase(l)
            attention_phase(l)
            brs = ["att"]
            if STAGE >= 2:
                gla_pass(l, 0)
                gla_pass(l, 1)
                brs = ["gla", "att"]
            if STAGE >= 3:
                ssd_pass(l, 0)
                ssd_pass(l, 1)
                brs = ["ssd", "gla", "att"]
            merge_phase(l, brs)
        if STAGE >= 0:
            ffn_phase(l)
    outs = final_phase()
    mk.emit(final_waits=outs)
    return nc


PV_NORM1 = 0
PV_NORM2 = 16
PV_FINAL = 32
PV_QW = 40
PV_KW = 42
PV_LINK = 44
PV_GLAW = 46
NGC = 1792
PV_CONV = 64
NSC = 1536
NSR = 2128
NPV = 256
NCST = 392


def _pack_pvec(inp):
    pv = np.zeros((128, NPV), np.float32)
    for l in range(DEPTH):
        pv[:, PV_NORM1 + 8 * l:PV_NORM1 + 8 * l + 8] = inp["norm1_w"][l].reshape(8, 128).T
        pv[:, PV_NORM2 + 8 * l:PV_NORM2 + 8 * l + 8] = inp["norm2_w"][l].reshape(8, 128).T
    pv[:, PV_FINAL:PV_FINAL + 8] = inp["final_norm_w"].reshape(8, 128).T
    for l in range(DEPTH):
        for j in range(16):
            cb = PV_CONV + (l * 16 + j) * 6
            pv[:, cb:cb + 5] = inp["conv_w"][l][:, j * 128:(j + 1) * 128].T
            pv[:, cb + 5] = inp["conv_b"][l][j * 128:(j + 1) * 128]
        pv[:, PV_GLAW + l] = inp["gla_norm_w"][l]
        pv[:, PV_QW + l] = np.tile(inp["att_q_norm_w"][l], 2)
        pv[:, PV_KW + l] = np.tile(inp["att_k_norm_w"][l], 2)
    return pv


def _consts():
    c = np.zeros((128, NCST), np.float32)
    c[:, 0:128] = np.eye(128, dtype=np.float32)
    c[:, 128] = EPS
    c[:, 129] = 64 * EPS
    c[:, 130] = 1.0
    for m in range(128):
        if (m % 32) < 16:
            c[m + 16, 136 + m] = -1.0
        else:
            c[m - 16, 136 + m] = 1.0
    for m in range(128):
        c[(m // 64) * 64:(m // 64 + 1) * 64, 264 + m] = 1.0
    return c


def _rope_tables(core):
    t = np.arange(T)
    seg = t // SEGL
    pos = t % SEGL + np.where((seg == 1) & (core < 4), SEGL, 0)
    row = (pos // 64).astype(np.float32)
    col = (pos % 64).astype(np.float32)
    inv = (np.float32(10000.0) ** (-np.arange(16, dtype=np.float32) / np.float32(16))).astype(np.float32)
    p = np.arange(128)
    d = p % 64
    sec = d // 32
    f = d % 16
    axis = np.where(sec[:, None] == 0, row[None, :], col[None, :]).astype(np.float32)
    ang = (axis * inv[f][:, None]).astype(np.float32)
    return np.cos(ang).astype(np.float32), np.sin(ang).astype(np.float32)


def _ssd_consts():
    g = np.zeros((128, NSC), np.float32)
    t = np.arange(128)[:, None]
    x = np.arange(128)[None, :]
    g[:, 0:128] = (t <= x)
    g[:, 128:256] = (t > x)
    g[:, 256:384] = (t >= x)
    g[:, 384:512] = (t < x)
    g[:, 512:1024] = np.tile((x >= t).astype(np.float32), (1, 4))
    g[:, 1024:1536] = np.tile((x <= t).astype(np.float32), (1, 4))
    return g


def _ssd_rows(inp):
    r = np.zeros((DEPTH, 128, NSR), np.float32)
    for l in range(DEPTH):
        row = np.concatenate([inp["ssd_dt_bias_f"][l], inp["ssd_dt_bias_b"][l], inp["ssd_a_log_f"][l], inp["ssd_a_log_b"][l],
                              inp["ssd_d"][l], inp["ssd_norm_w"][l], np.repeat(inp["ssd_d"][l], 64)]).astype(np.float32)
        r[l, :, :] = row[None, :]
    return r


def _gla_consts():
    g = np.zeros((128, NGC), np.float32)
    t = np.arange(128)[:, None]
    l_ = np.arange(128)[None, :]
    same = (t // 64) == (l_ // 64)
    for dr in range(2):
        if dr == 0:
            U = same & (t <= l_)
            R = same & ((t % 64) <= 32)
            mask = same & (l_ >= t)
        else:
            U = same & (t >= l_)
            R = same & ((t % 64) >= 31)
            mask = same & (l_ <= t)
        J = same
        o0 = dr * 896
        g[:, o0:o0 + 128] = -(U.astype(np.float32) - R.astype(np.float32)) / 16.0
        g[:, o0 + 128:o0 + 256] = -U.astype(np.float32) / 16.0
        g[:, o0 + 256:o0 + 384] = -(J.astype(np.float32) - U.astype(np.float32)) / 16.0
        g[:, o0 + 384:o0 + 896] = np.tile(mask.astype(np.float32), (1, 4))
    return g


_NC_CACHE = {}


def kernel(**inputs):
    inp = {k: np.asarray(v) for k, v in inputs.items()}
    xp = inp["x_prompt"]
    xs = inp["x_sample"]
    if "nc" not in _NC_CACHE:
        _NC_CACHE["nc"] = build_program()
    nc = _NC_CACHE["nc"]
    pv = _pack_pvec(inp)
    cst = _consts()
    gcs = _gla_consts()
    scs = _ssd_consts()
    srw = _ssd_rows(inp)
    in_maps = []
    for c in range(NCORES):
        if c < 4:
            xc = np.concatenate([xs[c], xp[c]], axis=0)
        else:
            j = 4 + 3 * (c - 4)
            xc = np.concatenate([xp[j], xp[j + 1], xp[j + 2]], axis=0)
        pvc = pv.copy()
        pvc[:, PV_LINK] = 1.0 if c < 4 else 0.0
        rc, rs_ = _rope_tables(c)
        in_maps.append({
            "x": np.ascontiguousarray(xc, dtype=np.float32),
            "w_ffn_in": inp["w_ffn_in"], "w_ffn_out": inp["w_ffn_out"],
            "w_in": inp["w_in"], "w_br_attn": inp["w_br_attn"], "w_br_gla": inp["w_br_gla"],
            "w_br_ssd": inp["w_br_ssd"], "w_out": inp["w_out"],
            "ropec": rc, "ropes": rs_, "gcst": gcs, "scst": scs, "srow": srw,
            "gla_w2_f": inp["gla_w2_f"], "gla_w2_b": inp["gla_w2_b"], "gla_b_f": inp["gla_b_f"], "gla_b_b": inp["gla_b_b"],
            "pvec": pvc, "cst": cst,
        })
    res = run_bass_kernel_spmd(nc, in_maps, core_ids=list(range(NCORES)))
    yp = np.zeros_like(xp)
    ys = np.zeros_like(xs)
    for c in range(NCORES):
        y = np.asarray(res.results[c]["y"]).reshape(T, D)
        if c < 4:
            ys[c] = y[:2 * SEGL]
            yp[c] = y[2 * SEGL:]
        else:
            j = 4 + 3 * (c - 4)
            for s in range(3):
                yp[j + s] = y[s * SEGL:(s + 1) * SEGL]
    return (yp, ys)
```

```python
import os
import numpy as np
import ml_dtypes
import concourse.bass as bass
import concourse.mybir as mybir
from concourse.bass_utils import run_bass_kernel_spmd

F32 = mybir.dt.float32
BF16 = mybir.dt.bfloat16
AF = mybir.ActivationFunctionType
ALU = mybir.AluOpType
AX = mybir.AxisListType

NCORES = 8
D = 1024
NSEG = 3
SEGL = 2048
T = NSEG * SEGL
TB = 512
NB = T // TB
DFF = 2816
DEPTH = 2
EPS = 1e-6
IN_W = 8512

STAGE = int(os.environ.get("MK_STAGE", "9"))


class _Op:
    __slots__ = ("eng", "fn", "deps", "inc", "idx", "dma", "sem", "val", "cnt")


class MK:
    ENGS = ("pe", "act", "dve", "pool", "sp")
    EPOCH = 20000
    KDMA = 12

    def __init__(self, nc):
        self.nc = nc
        self.ops = {e: [] for e in self.ENGS}
        self.last_w = {}
        self.rd = {}
        self.ndma = {e: 0 for e in self.ENGS}
        self.dma_ops = {e: [] for e in self.ENGS}
        self.pending = {}

    def barrier(self, fn):
        deps = []
        for e in self.ENGS:
            cl = [o for o in self.ops[e] if not o.dma]
            if cl and e != "pool":
                deps.append(cl[-1])
            deps.extend(self.dma_ops[e][-self.KDMA:])
        o = self.op("pool", fn, _extra=deps)
        for e in self.ENGS:
            if e != "pool":
                self.pending[e] = o
        return o

    def op(self, eng, fn, r=(), w=(), dma=False, _extra=()):
        lst = self.ops[eng]
        o = _Op()
        o.eng = eng; o.fn = fn; o.idx = len(lst); o.dma = dma; o.inc = False
        o.sem = None; o.val = 0; o.cnt = 0
        deps = {}
        for k in r:
            lw = self.last_w.get(k)
            if lw is not None:
                deps[id(lw)] = lw
        for k in w:
            lw = self.last_w.get(k)
            if lw is not None:
                deps[id(lw)] = lw
            rdk = self.rd.get(k)
            if rdk:
                for d in rdk.values():
                    deps[id(d)] = d
        for d in _extra:
            deps[id(d)] = d
        pb = self.pending.pop(eng, None)
        if pb is not None:
            deps[id(pb)] = pb
        out = []
        for d in deps.values():
            if (not d.dma) and (not dma) and d.eng == eng:
                if eng == "pe" or d.idx < o.idx - 1:
                    continue
            out.append(d)
            d.inc = True
        if dma:
            j = self.ndma[eng]
            self.ndma[eng] = j + 1
            o.sem = j % self.KDMA
            o.val = 16 * (j // self.KDMA + 1)
            if j >= self.KDMA:
                out.append(self.dma_ops[eng][j - self.KDMA])
            self.dma_ops[eng].append(o)
        o.deps = out
        for k in r:
            self.rd.setdefault(k, {})[("d", eng, o.idx) if dma else eng] = o
        for k in w:
            self.last_w[k] = o
            self.rd[k] = {}
        lst.append(o)
        return o

    def emit(self, final_waits=()):
        nc = self.nc
        nep = {}
        for e in self.ENGS:
            c = 0
            for o in self.ops[e]:
                if o.dma:
                    continue
                if o.inc:
                    c += 1
                    o.cnt = c
            nep[e] = (c + self.EPOCH - 1) // self.EPOCH + 1
        esems = {e: [nc.alloc_semaphore(name=f"s_{e}_{i}") for i in range(nep[e])] for e in self.ENGS}
        dsems = {e: [nc.alloc_semaphore(name=f"d_{e}_{i}") for i in range(self.KDMA)] if self.ndma[e] else []
                 for e in self.ENGS}
        engobj = {"pe": "tensor", "act": "scalar", "dve": "vector", "pool": "gpsimd", "sp": "sync"}

        def tok(d):
            if d.dma:
                return dsems[d.eng][d.sem], d.val
            ep = (d.cnt - 1) // self.EPOCH
            return esems[d.eng][ep], (d.cnt - 1) % self.EPOCH + 1

        def body(e, eo):
            waited = {}
            for o in self.ops[e]:
                for d in o.deps:
                    s, v = tok(d)
                    key = id(s)
                    if waited.get(key, 0) >= v:
                        continue
                    waited[key] = v
                    eo.wait_ge(s, v)
                ins = o.fn(eo)
                if o.dma:
                    ins.then_inc(dsems[e][o.sem], 16)
                elif o.inc:
                    s, v = tok(o)
                    ins.then_inc(s, 1)
            if e == "sp":
                for d in final_waits:
                    s, v = tok(d)
                    eo.wait_ge(s, v)

        with nc.Block() as block:
            for e in self.ENGS:
                dec = getattr(block, engobj[e])
                dec(lambda eo, e=e: body(e, eo))


class Arena:
    def __init__(self, nc, nbytes):
        self.n = nbytes // 2
        self.t = nc.alloc_sbuf_tensor("arena", [128, self.n], BF16)
        self.off = 0

    def reset(self):
        self.off = 0

    def alloc(self, shape, dt):
        n = int(np.prod(shape[1:]))
        if dt == F32:
            if self.off % 2:
                self.off += 1
            nb = 2 * n
        else:
            nb = n
        assert self.off + nb <= self.n, ("arena overflow", self.off, nb, self.n)
        ap = self.t[0:shape[0], self.off:self.off + nb]
        self.off += nb
        if dt == F32:
            ap = ap.bitcast(F32)
        if len(shape) == 3:
            ap = ap.rearrange("p (a b) -> p a b", a=shape[1])
        elif len(shape) == 4:
            ap = ap.rearrange("p (a b c) -> p a b c", a=shape[1], b=shape[2])
        return ap


def build_program():
    nc = bass.Bass("TRN2", target_bir_lowering=False)
    mk = MK(nc)

    def din(name, shape, dt=F32):
        return nc.dram_tensor(name, list(shape), dt, kind="ExternalInput").ap()

    def dscr(name, shape, dt):
        return nc.dram_tensor(name, list(shape), dt, kind="Internal").ap()

    x_in = din("x", [T, D])
    w_ffn_in = din("w_ffn_in", [DEPTH, D, 2 * DFF])
    w_ffn_out = din("w_ffn_out", [DEPTH, DFF, D])
    w_in = din("w_in", [DEPTH, D, IN_W])
    w_br_attn = din("w_br_attn", [DEPTH, 512, D])
    w_br_gla = din("w_br_gla", [DEPTH, 512, D])
    w_br_ssd = din("w_br_ssd", [DEPTH, 1024, D])
    w_out = din("w_out", [DEPTH, D, D])
    gla_w2 = [din("gla_w2_f", [DEPTH, 16, 256]), din("gla_w2_b", [DEPTH, 16, 256])]
    gla_b = [din("gla_b_f", [DEPTH, 256]), din("gla_b_b", [DEPTH, 256])]
    gcst = din("gcst", [128, NGC])
    scst = din("scst", [128, NSC])
    srow = din("srow", [DEPTH, 128, NSR])
    ropec = din("ropec", [128, T])
    ropes = din("ropes", [128, T])
    pvec = din("pvec", [128, NPV])
    cst = din("cst", [128, NCST])
    y_out = nc.dram_tensor("y", [T, D], F32, kind="ExternalOutput").ap()
    dbg_out = nc.dram_tensor("dbg", [128, 512], F32, kind="ExternalOutput").ap()

    xT = dscr("xT_scr", [8, 128, T], F32)
    qS = dscr("qS_scr", [8, 64, T], BF16)
    kS = dscr("kS_scr", [2, 64, T], BF16)
    vS = dscr("vS_scr", [T, 128], BF16)
    oS = dscr("oS_scr", [8, 64, T], BF16)
    xbcS = dscr("xbcS_scr", [16, 128, T], BF16)
    zS = dscr("zS_scr", [T, 1024], BF16)
    dtS = dscr("dtS_scr", [T, 32], F32)
    xcS = dscr("xcS_scr", [8, 128, T], BF16)
    xtokS = dscr("xtokS_scr", [T, 1024], BF16)
    btokS = dscr("btokS_scr", [T, 512], BF16)
    yaccS = dscr("yaccS_scr", [T, 1024], F32)
    oSS = dscr("oSS_scr", [8, 128, T], BF16)
    gqS = dscr("gqS_scr", [4, 64, T], BF16)
    gkS = dscr("gkS_scr", [4, 64, T], BF16)
    gktS = dscr("gktS_scr", [T, 256], BF16)
    gvtS = dscr("gvtS_scr", [T, 512], BF16)
    gogS = dscr("gogS_scr", [4, 128, T], BF16)
    lrS = dscr("lrS_scr", [32, T], F32)
    goacc = dscr("goacc_scr", [4, 128, T], F32)
    oG = dscr("oG_scr", [4, 128, T], BF16)

    def sb(name, shape, dt):
        return nc.alloc_sbuf_tensor(name, list(shape), dt)

    pv = sb("pv", [128, NPV], F32)
    cs = sb("cs", [128, NCST], F32)
    ident = cs[:, 0:128]
    ones_bf = sb("ones_bf", [128, 128], BF16)
    perm_bf = sb("perm_bf", [128, 128], BF16)
    b64_bf = sb("b64_bf", [128, 128], BF16)
    ident_bf = sb("ident_bf", [128, 128], BF16)
    bscr = sb("bscr", [128, 4], F32)
    mk.op("sp", lambda e: e.dma_start(out=pv[:], in_=pvec), w=["pv"], dma=True)
    mk.op("sp", lambda e: e.dma_start(out=cs[:], in_=cst), w=["cs"], dma=True)
    mk.op("dve", lambda e: e.memset(ones_bf[:], 1.0), w=["ones_bf"])
    mk.op("dve", lambda e: e.tensor_copy(out=perm_bf[:], in_=cs[:, 136:264]), r=["cs"], w=["perm_bf"])
    mk.op("dve", lambda e: e.tensor_copy(out=b64_bf[:], in_=cs[:, 264:392]), r=["cs"], w=["b64_bf"])
    mk.op("dve", lambda e: e.tensor_copy(out=ident_bf[:], in_=cs[:, 0:128]), r=["cs"], w=["ident_bf"])
    psw = [nc.alloc_psum_tensor(f"psw{i}", [128, 1024], F32) for i in range(4)]
    psb = [psw[i // 2][:, (i % 2) * 512:(i % 2 + 1) * 512] for i in range(8)]
    A = Arena(nc, nc.sbuf_bytes_remaining - 256)

    def phase():
        mk.barrier(lambda e: e.memset(bscr[:, 0:1], 0.0))
        A.reset()

    PIECE = 1024
    cast_rr = [0]

    def load_weight(dst, src, kc, ncols, stg, scale_col0=None, tag=""):
        srcv = src.rearrange("(c p) n -> p c n", p=128)
        for c in range(kc):
            for n0 in range(0, ncols, PIECE):
                n1 = min(ncols, n0 + PIECE)
                i = cast_rr[0] % len(stg)
                cast_rr[0] += 1
                st = stg[i]
                mk.op("sp", lambda e, st=st, c=c, n0=n0, n1=n1: e.dma_start(out=st[:, 0:n1 - n0], in_=srcv[:, c, n0:n1]),
                      w=[("stg", i)], dma=True)
                eng = ("act", "dve")[cast_rr[0] % 2]
                if scale_col0 is None:
                    if eng == "act":
                        fn = lambda e, st=st, c=c, n0=n0, n1=n1: e.copy(out=dst[:, c, n0:n1], in_=st[:, 0:n1 - n0])
                    else:
                        fn = lambda e, st=st, c=c, n0=n0, n1=n1: e.tensor_copy(out=dst[:, c, n0:n1], in_=st[:, 0:n1 - n0])
                    rk = [("stg", i)]
                else:
                    sc = scale_col0 + c
                    if eng == "act":
                        fn = lambda e, st=st, c=c, n0=n0, n1=n1, sc=sc: e.activation(
                            out=dst[:, c, n0:n1], in_=st[:, 0:n1 - n0], func=AF.Copy, scale=pv[:, sc:sc + 1])
                    else:
                        fn = lambda e, st=st, c=c, n0=n0, n1=n1, sc=sc: e.tensor_scalar(
                            out=dst[:, c, n0:n1], in0=st[:, 0:n1 - n0], scalar1=pv[:, sc:sc + 1], scalar2=None, op0=ALU.mult)
                    rk = [("stg", i), "pv"]
                mk.op(eng, fn, r=rk, w=[("W", tag, c, n0 // PIECE)])

    def evac(eng, out, in_, r, w):
        if eng == "act":
            mk.op("act", lambda e: e.copy(out=out, in_=in_), r=r, w=w)
        else:
            mk.op(eng, lambda e: e.tensor_copy(out=out, in_=in_), r=r, w=w)

    xin_t = [A.alloc([128, D], F32) for i in range(4)]
    xtt = [A.alloc([128, 8, 128], F32) for i in range(4)]
    dbgt = A.alloc([128, 512], F32)
    mk.op("dve", lambda e: e.memset(dbgt, 0.0), w=["dbgt"])
    mk.op("pool", lambda e: e.tensor_copy(out=dbgt[0:64, 0:128], in_=cs[64:128, 0:128]), r=["cs", "dbgt"], w=["dbgt1"])
    mk.op("dve", lambda e: e.tensor_copy(out=dbgt[0:64, 128:256], in_=cs[64:128, 0:128]), r=["cs", "dbgt"], w=["dbgt2"])
    mk.op("act", lambda e: e.copy(out=dbgt[64:128, 256:384], in_=cs[0:64, 0:128]), r=["cs", "dbgt"], w=["dbgt3"])
    dbg_o = mk.op("sp", lambda e: e.dma_start(out=dbg_out, in_=dbgt), r=["dbgt1", "dbgt2", "dbgt3"], w=["dbgo"], dma=True)

    NT0 = T // 128

    def p0_load(t):
        i = t % 4
        mk.op("sp", lambda e: e.dma_start(out=xin_t[i], in_=x_in[t * 128:(t + 1) * 128, :]),
              w=[("xin", i)], dma=True)

    def p0_tile(t):
        i = t % 4
        for half in range(2):
            pb = (2 * t + half) % 8
            for c4 in range(4):
                c = half * 4 + c4
                mk.op("pe", lambda e, c=c, c4=c4, pb=pb: e.transpose(
                    out=psb[pb][:, c4 * 128:(c4 + 1) * 128], in_=xin_t[i][:, c * 128:(c + 1) * 128], identity=ident),
                    r=[("xin", i), "cs"], w=[("ps", pb)])
            evac(("act", "dve")[half], xtt[i][:, half * 4:(half + 1) * 4, :], psb[pb][:].rearrange("p (c t) -> p c t", c=4),
                 r=[("ps", pb)], w=[("xtt", i, half)])
        mk.op("sp", lambda e: e.dma_start(
            out=xT[:, :, t * 128:(t + 1) * 128].rearrange("c p t -> p c t"), in_=xtt[i]),
            r=[("xtt", i, 0), ("xtt", i, 1)], w=[("xT", t // 4)], dma=True)

    for t in range(-3, NT0):
        if t + 3 < NT0:
            p0_load(t + 3)
        if t >= 0:
            p0_tile(t)

    def load_xblock(xb, b, par=0):
        mk.op("sp", lambda e, b=b: e.dma_start(out=xb, in_=xT[:, :, b * TB:(b + 1) * TB].rearrange("c p t -> p c t")),
              r=[("xT", b)], w=[("xb", par, c) for c in range(8)], dma=True)

    def rms_block(xb, hb, sq, rstd, pbank, par=0):
        for c in range(8):
            mk.op("act", lambda e, c=c, s=sq[c % 2]: e.activation(out=s, in_=xb[:, c, :], func=AF.Square),
                  r=[("xb", par, c)], w=[("sq", c % 2)])
            mk.op("pe", lambda e, c=c, s=sq[c % 2]: e.matmul(psb[pbank][:], lhsT=ones_bf[:], rhs=s, start=(c == 0), stop=(c == 7)),
                  r=[("sq", c % 2), "ones_bf"], w=[("ps", pbank)])
        mk.op("act", lambda e: e.activation(out=rstd, in_=psb[pbank][:], func=AF.Sqrt, scale=1.0 / D, bias=cs[:, 128:129]),
              r=[("ps", pbank), "cs"], w=[("rstd0", par)])
        mk.op("dve", lambda e: e.reciprocal(out=rstd, in_=rstd), r=[("rstd0", par)], w=[("rstd", par)])
        if hb is not None:
            for c in range(8):
                mk.op("dve", lambda e, c=c: e.tensor_tensor(out=hb[:, c, :], in0=xb[:, c, :], in1=rstd, op=ALU.mult),
                      r=[("xb", par, c), ("rstd", par)], w=[("hb", par, c)])

    C_AQ, C_AK, C_AV, C_G = 4672, 5184, 5312, 5440

    def attn_inproj_setup(l, stg):
        watt = A.alloc([128, 8, 768], BF16)
        qraw = [A.alloc([128, TB], F32) for i in range(2)]
        sqq = [A.alloc([128, TB], BF16) for i in range(2)]
        rq = [A.alloc([128, TB], F32) for i in range(2)]
        qn = [A.alloc([128, TB], BF16) for i in range(2)]
        cosb = A.alloc([128, TB], F32)
        sinb = A.alloc([128, TB], F32)
        t1 = [A.alloc([128, TB], F32) for i in range(2)]
        t2 = [A.alloc([128, TB], F32) for i in range(2)]
        qo = [A.alloc([128, TB], BF16) for i in range(2)]
        vt = [A.alloc([128, 4, 128], BF16) for i in range(2)]
        load_weight(watt, w_in[l][:, C_AQ:C_AQ + 768], 8, 768, stg, scale_col0=PV_NORM1 + 8 * l, tag="watt")
        itc = [0]

        def body(b, hb, par):
            mk.op("sp", lambda e, b=b: e.dma_start(out=cosb, in_=ropec[:, b * TB:(b + 1) * TB]), w=["cosb"], dma=True)
            mk.op("sp", lambda e, b=b: e.dma_start(out=sinb, in_=ropes[:, b * TB:(b + 1) * TB]), w=["sinb"], dma=True)
            for ch in range(5):
                i = itc[0] % 2
                itc[0] += 1
                pa, pS, pR = 1 + i, 3 + i, 5 + i
                for c in range(8):
                    mk.op("pe", lambda e, c=c, ch=ch, pa=pa: e.matmul(psb[pa][:], lhsT=watt[:, c, ch * 128:(ch + 1) * 128],
                                                                  rhs=hb[:, c, :], start=(c == 0), stop=(c == 7)),
                          r=[("hb", par, c), ("W", "watt", c, 0)], w=[("ps", pa)])
                mk.op("act", lambda e, i=i, pa=pa: e.copy(out=qraw[i], in_=psb[pa][:]), r=[("ps", pa)], w=[("qraw", i)])
                mk.op("act", lambda e, i=i: e.activation(out=sqq[i], in_=qraw[i], func=AF.Square),
                      r=[("qraw", i)], w=[("sqq", i)])
                mk.op("pe", lambda e, i=i, pS=pS: e.matmul(psb[pS][:], lhsT=b64_bf[:], rhs=sqq[i], start=True, stop=True),
                      r=[("sqq", i), "b64_bf"], w=[("ps", pS)])
                if ch < 4:
                    mk.op("act", lambda e, i=i, pS=pS: e.activation(out=rq[i], in_=psb[pS][:], func=AF.Sqrt, scale=1.0, bias=cs[:, 129:130]),
                          r=[("ps", pS), "cs"], w=[("rq0", i)])
                    wc = PV_QW + l
                else:
                    mk.op("act", lambda e, i=i, pS=pS: e.activation(out=rq[i], in_=psb[pS][:], func=AF.Sqrt, scale=1.0 / 64, bias=cs[:, 128:129]),
                          r=[("ps", pS), "cs"], w=[("rq0", i)])
                    wc = PV_KW + l
                mk.op("dve", lambda e, i=i: e.reciprocal(out=rq[i], in_=rq[i]), r=[("rq0", i)], w=[("rq", i)])
                mk.op("dve", lambda e, i=i, wc=wc: e.scalar_tensor_tensor(out=qn[i], in0=qraw[i], scalar=pv[:, wc:wc + 1], in1=rq[i],
                                                                     op0=ALU.mult, op1=ALU.mult),
                      r=[("qraw", i), ("rq", i), "pv"], w=[("qn", i)])
                mk.op("pe", lambda e, i=i, pR=pR: e.matmul(psb[pR][:], lhsT=perm_bf[:], rhs=qn[i], start=True, stop=True),
                      r=[("qn", i), "perm_bf"], w=[("ps", pR)])
                mk.op("dve", lambda e, i=i: e.tensor_tensor(out=t1[i], in0=qn[i], in1=cosb, op=ALU.mult),
                      r=[("qn", i), "cosb"], w=[("t1", i)])
                mk.op("dve", lambda e, i=i, pR=pR: e.tensor_tensor(out=t2[i], in0=psb[pR][:], in1=sinb, op=ALU.mult),
                      r=[("ps", pR), "sinb"], w=[("t2", i)])
                mk.op("dve", lambda e, i=i: e.tensor_tensor(out=qo[i], in0=t1[i], in1=t2[i], op=ALU.add),
                      r=[("t1", i), ("t2", i)], w=[("qo", i)])
                if ch < 4:
                    dst = qS[2 * ch:2 * ch + 2, :, b * TB:(b + 1) * TB].rearrange("h d t -> (h d) t")
                else:
                    dst = kS[:, :, b * TB:(b + 1) * TB].rearrange("h d t -> (h d) t")
                mk.op("sp", lambda e, i=i, dst=dst: e.dma_start(out=dst, in_=qo[i]), r=[("qo", i)], w=[("qkS", b, ch)], dma=True)
            vi = b % 2
            for tt in range(4):
                for c in range(8):
                    mk.op("pe", lambda e, c=c, tt=tt: e.matmul(psb[7][:, tt * 128:(tt + 1) * 128], lhsT=hb[:, c, tt * 128:(tt + 1) * 128],
                                                             rhs=watt[:, c, 640:768], start=(c == 0), stop=(c == 7)),
                          r=[("hb", par, c), ("W", "watt", c, 0)], w=[("ps", 7)])
            mk.op("act", lambda e, vi=vi: e.copy(out=vt[vi], in_=psb[7][:].rearrange("p (a b) -> p a b", a=4)),
                  r=[("ps", 7)], w=[("vt", vi)])
            mk.op("sp", lambda e, vi=vi, b=b: e.dma_start(out=vS[b * TB:(b + 1) * TB, :].rearrange("(a p) n -> p a n", p=128), in_=vt[vi]),
                  r=[("vt", vi)], w=[("vS", b)], dma=True)
        return body

    def inproj_phase(l):
        phase()
        stg = [A.alloc([128, PIECE], F32) for i in range(2)]
        xb2 = [A.alloc([128, 8, TB], F32) for i in range(2)]
        hb2 = [A.alloc([128, 8, TB], BF16) for i in range(2)]
        sq = [A.alloc([128, TB], BF16) for i in range(2)]
        rstd2 = [A.alloc([128, TB], F32) for i in range(2)]
        bodies = [attn_inproj_setup(l, stg)]
        if STAGE >= 2:
            bodies.append(gla_inproj_setup(l, stg))
        if STAGE >= 3:
            bodies.append(ssd_inproj_setup(l, stg))
        load_xblock(xb2[0], 0, 0)
        rms_block(xb2[0], hb2[0], sq, rstd2[0], 0, 0)
        for b in range(NB):
            par = b % 2
            if b + 1 < NB:
                load_xblock(xb2[1 - par], b + 1, 1 - par)
            for k, body in enumerate(bodies):
                if k == len(bodies) - 1 and b + 1 < NB:
                    rms_block(xb2[1 - par], hb2[1 - par], sq, rstd2[1 - par], 0, 1 - par)
                body(b, hb2[par], par)

    def attention_phase(l):
        phase()
        NKT = 32
        kt_sb = A.alloc([128, NKT * 128], BF16)
        v1 = A.alloc([128, NKT, 2, 128], BF16)
        v2 = A.alloc([128, NKT, 2, 128], BF16)
        qa = [A.alloc([128, 4, TB], BF16) for i in range(2)]
        pt = [A.alloc([128, 2 * TB], BF16) for i in range(3)]
        rs = [A.alloc([128, TB], F32) for i in range(2)]
        r0 = [A.alloc([128, TB], F32) for i in range(2)]
        ob = [A.alloc([128, TB], BF16) for i in range(2)]
        SB = [1, 2, 3, 4]
        OB = [5, 6]
        cnt = {"s": 0, "o": 0, "p": 0, "q": 0, "r": 0}
        for grp, (tok0, ntok) in enumerate([(0, 2 * SEGL), (2 * SEGL, SEGL)]):
            nkt = ntok // 128
            gk = ("grp", l, grp)
            mk.op("sp", lambda e, tok0=tok0, ntok=ntok: e.dma_start(
                out=kt_sb[:, 0:ntok], in_=kS[:, :, tok0:tok0 + ntok].rearrange("h d t -> (h d) t")),
                r=[("qkS", b, 4) for b in range(tok0 // TB, (tok0 + ntok) // TB)], w=["kt_sb"], dma=True)
            mk.op("pool", lambda e, nkt=nkt: e.memset(v1[:, 0:nkt, :, 64:128], 1.0), w=["v1ones"])
            for kv in range(2):
                mk.op("sp", lambda e, kv=kv, tok0=tok0, ntok=ntok, nkt=nkt: e.dma_start(
                    out=v1[:, 0:nkt, kv, 0:64],
                    in_=vS[tok0:tok0 + ntok, kv * 64:(kv + 1) * 64].rearrange("(a p) n -> p a n", p=128)),
                    r=[("vS", b) for b in range(tok0 // TB, (tok0 + ntok) // TB)], w=[("v1", kv)], dma=True)
            if grp == 0:
                mk.op("dve", lambda e, nkt=nkt: e.tensor_scalar(out=v2[:, 0:nkt].rearrange("p a b c -> p (a b c)"),
                                                               in0=v1[:, 0:nkt].rearrange("p a b c -> p (a b c)"),
                                                               scalar1=pv[:, PV_LINK:PV_LINK + 1], scalar2=None, op0=ALU.mult),
                      r=[("v1", 0), ("v1", 1), "v1ones", "pv"], w=["v2"])
            for qb in range(ntok // TB):
                b = tok0 // TB + qb
                qi = cnt["q"] % 2
                cnt["q"] += 1
                for half in range(2):
                    mk.op("sp", lambda e, half=half, b=b, qi=qi: e.dma_start(
                        out=qa[qi][half * 64:(half + 1) * 64, :, :],
                        in_=qS[half * 4:half * 4 + 4, :, b * TB:(b + 1) * TB].rearrange("h d t -> d h t")),
                        r=[("qkS", b, ch) for ch in range(4)], w=[("qa", qi, half)], dma=True)
                qseg = (tok0 + qb * TB) // SEGL
                for j in range(4):
                    heads = (j, 4 + j)
                    oi = cnt["o"] % 2
                    cnt["o"] += 1
                    pos = (4 + oi, 6 + oi)
                    pend = []

                    def issue_s(kt, j=j):
                        si = cnt["s"] % 2
                        cnt["s"] += 1
                        for kv in range(2):
                            mk.op("pe", lambda e, kt=kt, si=si, kv=kv, j=j, qi=qi: e.matmul(
                                psb[2 * si + kv][:], lhsT=kt_sb[kv * 64:(kv + 1) * 64, kt * 128:(kt + 1) * 128],
                                rhs=qa[qi][kv * 64:(kv + 1) * 64, j, :], start=True, stop=True),
                                r=["kt_sb", ("qa", qi, kv)], w=[("ps", 2 * si + kv)])
                        pi = cnt["p"] % 3
                        cnt["p"] += 1
                        mk.op("act", lambda e, si=si, pi=pi: e.activation(out=pt[pi], in_=psw[si][:], func=AF.Exp),
                              r=[("ps", 2 * si), ("ps", 2 * si + 1)], w=[("pt", pi)])
                        return pi

                    def issue_o(kt, pi, pos=pos):
                        kseg = (tok0 + kt * 128) // SEGL
                        vv = v1 if kseg == qseg else v2
                        for kv in range(2):
                            vk = [("v1", kv), "v1ones"] if kseg == qseg else ["v2"]
                            mk.op("pe", lambda e, kt=kt, pi=pi, vv=vv, kv=kv, po=pos[kv]: e.matmul(
                                psb[po][:], lhsT=vv[:, kt, kv, :], rhs=pt[pi][:, kv * 512:(kv + 1) * 512], start=(kt == 0), stop=(kt == nkt - 1)),
                                r=[("pt", pi)] + vk, w=[("ps", pos[kv])])

                    LA = 1
                    for kt in range(nkt):
                        pend.append((kt, issue_s(kt)))
                        if len(pend) > LA:
                            issue_o(*pend.pop(0))
                    while pend:
                        issue_o(*pend.pop(0))
                    for kv in range(2):
                        h = heads[kv]
                        po = pos[kv]
                        ri = cnt["r"] % 2
                        cnt["r"] += 1
                        mk.op("dve", lambda e, ri=ri, po=po: e.reciprocal(out=rs[ri][64:128, :], in_=psb[po][64:128, :]),
                              r=[("ps", po)], w=[("rs", ri)])
                        mk.op("pool", lambda e, ri=ri: e.tensor_copy(out=r0[ri][0:64, :], in_=rs[ri][64:128, :]),
                              r=[("rs", ri)], w=[("r0", ri)])
                        mk.op("dve", lambda e, ri=ri, po=po: e.tensor_tensor(out=ob[ri][0:64, :], in0=psb[po][0:64, :], in1=r0[ri][0:64, :], op=ALU.mult),
                              r=[("ps", po), ("r0", ri)], w=[("ob", ri)])
                        mk.op("sp", lambda e, ri=ri, h=h, b=b: e.dma_start(out=oS[h, :, b * TB:(b + 1) * TB], in_=ob[ri][0:64, :]),
                              r=[("ob", ri)], w=[("oS", b, h)], dma=True)

    C_GQ, C_GK, C_GV, C_GOG, C_GLR = 3104, 3360, 3616, 4128, 4640

    def gla_inproj_setup(l, stg):
        NW = 1568
        wg_ = A.alloc([128, 8, NW], BF16)
        ob_ = [A.alloc([128, TB], BF16) for i in range(2)]
        lrt = A.alloc([32, TB], F32)
        kt_ = [A.alloc([128, 256], BF16) for i in range(2)]
        vt_ = [A.alloc([128, 512], BF16) for i in range(2)]
        for c in range(8):
            for (n0, n1) in ((0, 784), (784, NW)):
                pass
        load_weight(wg_[:, :, 0:1024], w_in[l][:, C_GQ:C_GQ + 1024], 8, 1024, stg, scale_col0=PV_NORM1 + 8 * l, tag="wgl0")
        load_weight(wg_[:, :, 1024:NW], w_in[l][:, C_GQ + 1024:C_GQ + NW], 8, NW - 1024, stg, scale_col0=PV_NORM1 + 8 * l, tag="wgl1")

        def wkey(c, col):
            return ("W", "wgl0" if col < 1024 else "wgl1", c, 0)
        itc = [0]

        def body(b, hb, par):
            specs = [(0, gqS, 0), (128, gqS, 1), (256, gkS, 0), (384, gkS, 1)] + [(1024 + 128 * j, gogS, j) for j in range(4)]
            for (col, dstS, j) in specs:
                i = itc[0] % 2
                itc[0] += 1
                pa = 1 + i
                for c in range(8):
                    mk.op("pe", lambda e, c=c, col=col, pa=pa: e.matmul(psb[pa][:], lhsT=wg_[:, c, col:col + 128], rhs=hb[:, c, :],
                                                                    start=(c == 0), stop=(c == 7)),
                          r=[("hb", par, c), wkey(c, col)], w=[("ps", pa)])
                evac(("act", "dve")[i], ob_[i], psb[pa][:], r=[("ps", pa)], w=[("ob_", i)])
                if dstS is gogS:
                    dst = gogS[j, :, b * TB:(b + 1) * TB]
                else:
                    dst = dstS[2 * j:2 * j + 2, :, b * TB:(b + 1) * TB].rearrange("h d t -> (h d) t")
                mk.op("sp", lambda e, i=i, dst=dst: e.dma_start(out=dst, in_=ob_[i]), r=[("ob_", i)], w=[("gfm", b, col)], dma=True)
            for c in range(8):
                mk.op("pe", lambda e, c=c: e.matmul(psb[3][0:32, :], lhsT=wg_[:, c, 1536:1568], rhs=hb[:, c, :], start=(c == 0), stop=(c == 7)),
                      r=[("hb", par, c), wkey(c, 1536)], w=[("ps", 3)])
            evac("act", lrt, psb[3][0:32, :], r=[("ps", 3)], w=["lrt"])
            mk.op("sp", lambda e, b=b: e.dma_start(out=lrS[:, b * TB:(b + 1) * TB], in_=lrt), r=["lrt"], w=[("lrS", b)], dma=True)
            for tt in range(4):
                i = tt % 2
                for c in range(8):
                    mk.op("pe", lambda e, c=c, tt=tt, i=i: e.matmul(psb[4 + i][:, 0:256], lhsT=hb[:, c, tt * 128:(tt + 1) * 128],
                                                                 rhs=wg_[:, c, 256:512], start=(c == 0), stop=(c == 7)),
                          r=[("hb", par, c), wkey(c, 256)], w=[("ps", 4 + i)])
                for c in range(8):
                    mk.op("pe", lambda e, c=c, tt=tt, i=i: e.matmul(psb[6 + i][:], lhsT=hb[:, c, tt * 128:(tt + 1) * 128],
                                                                 rhs=wg_[:, c, 512:1024], start=(c == 0), stop=(c == 7)),
                          r=[("hb", par, c), wkey(c, 512)], w=[("ps", 6 + i)])
                evac("act", kt_[i], psb[4 + i][:, 0:256], r=[("ps", 4 + i)], w=[("kt_", i)])
                evac("dve", vt_[i], psb[6 + i][:], r=[("ps", 6 + i)], w=[("vt_", i)])
                t0 = b * TB + tt * 128
                mk.op("sp", lambda e, i=i, t0=t0: e.dma_start(out=gktS[t0:t0 + 128, :], in_=kt_[i]), r=[("kt_", i)], w=[("gtm", b, tt, 0)], dma=True)
                mk.op("sp", lambda e, i=i, t0=t0: e.dma_start(out=gvtS[t0:t0 + 128, :], in_=vt_[i]), r=[("vt_", i)], w=[("gtm", b, tt, 1)], dma=True)
        return body

    def gla_pass(l, dr):
        phase()
        gc = A.alloc([128, NGC], F32)
        mk.op("sp", lambda e: e.dma_start(out=gc, in_=gcst), w=["gc"], dma=True)
        o0 = dr * 896
        M12 = gc[:, o0:o0 + 256]
        M3 = gc[:, o0 + 256:o0 + 384]
        mask4f = gc[:, o0 + 384:o0 + 896]
        w2a = A.alloc([17, 256], F32)
        mk.op("sp", lambda e: e.dma_start(out=w2a[0:16, :], in_=gla_w2[dr][l]), w=["w2a0"], dma=True)
        mk.op("sp", lambda e: e.dma_start(out=w2a[16:17, :], in_=gla_b[dr][l:l + 1, :]), w=["w2a1"], dma=True)
        ones_f = A.alloc([128, 128], F32)
        mk.op("dve", lambda e: e.memset(ones_f, 1.0), w=["ones_f"])
        lrt = [A.alloc([17, 128], F32) for i in range(2)]
        for i in range(2):
            mk.op("dve", lambda e, i=i: e.memset(lrt[i], 1.0), w=[("lrt", i)])
        q_sb = [A.alloc([64, 4, 128], BF16) for i in range(2)]
        k_sb = [A.alloc([64, 4, 128], BF16) for i in range(2)]
        kt_sb = [A.alloc([128, 256], BF16) for i in range(2)]
        vt_sb = [A.alloc([128, 512], BF16) for i in range(2)]
        e_ = A.alloc([128, 256], F32)
        sp_ = A.alloc([128, 256], F32)
        E1 = A.alloc([64, 4, 128], F32)
        E1i = A.alloc([64, 4, 128], F32)
        EB2 = [A.alloc([64, 4, 128], F32) for _ in range(2)]
        Ed = A.alloc([128, 256], F32)
        qe = A.alloc([64, 4, 128], BF16)
        ke = A.alloc([64, 4, 128], BF16)
        qb2 = [A.alloc([64, 4, 128], BF16) for _ in range(2)]
        kd2 = [A.alloc([128, 256], BF16) for _ in range(2)]
        attm2 = [A.alloc([128, 4, 128], BF16) for _ in range(2)]
        S = A.alloc([64, 4, 128], F32)
        Sb = [A.alloc([64, 4, 128], BF16) for i in range(2)]
        osb = A.alloc([128, 512], F32)
        if dr == 1:
            oprev2 = [A.alloc([128, 4, 128], F32) for _ in range(2)]
            gog2 = [A.alloc([128, 4, 128], BF16) for _ in range(2)]
            sqo = A.alloc([128, 512], BF16)
            rr = A.alloc([128, 512], F32)
            sg_ = A.alloc([128, 512], F32)
            oo = A.alloc([128, 512], BF16)
        NT = T // 128
        order = list(range(NT)) if dr == 0 else list(range(NT - 1, -1, -1))
        tiles_per_seg = SEGL // 128
        def tileA(n, t):
            i = n % 2
            t0 = t * 128
            seg = t // tiles_per_seg
            mk.op("sp", lambda e, i=i, t0=t0: e.dma_start(out=lrt[i][0:16, :], in_=lrS[dr * 16:(dr + 1) * 16, t0:t0 + 128]),
                  r=[("lrS", t0 // TB)], w=[("lrt", i)], dma=True)
            mk.op("sp", lambda e, i=i, t0=t0: e.dma_start(out=q_sb[i], in_=gqS[:, :, t0:t0 + 128].rearrange("h d t -> d h t")),
                  r=[("gfm", t0 // TB, 0), ("gfm", t0 // TB, 128)], w=[("q_sb", i)], dma=True)
            mk.op("sp", lambda e, i=i, t0=t0: e.dma_start(out=k_sb[i], in_=gkS[:, :, t0:t0 + 128].rearrange("h d t -> d h t")),
                  r=[("gfm", t0 // TB, 256), ("gfm", t0 // TB, 384)], w=[("k_sb", i)], dma=True)
            mk.op("sp", lambda e, i=i, t0=t0: e.dma_start(out=kt_sb[i], in_=gktS[t0:t0 + 128, :]),
                  r=[("gtm", t0 // TB, (t0 % TB) // 128, 0)], w=[("kt_sb", i)], dma=True)
            mk.op("sp", lambda e, i=i, t0=t0: e.dma_start(out=vt_sb[i], in_=gvtS[t0:t0 + 128, :]),
                  r=[("gtm", t0 // TB, (t0 % TB) // 128, 1)], w=[("vt_sb", i)], dma=True)
            if dr == 1:
                mk.op("sp", lambda e, t0=t0: e.dma_start(out=oprev2[i], in_=goacc[:, :, t0:t0 + 128].rearrange("h v t -> v h t")),
                      r=[("goacc", t0 // 128)], w=[("oprev", i)], dma=True)
                mk.op("sp", lambda e, t0=t0: e.dma_start(out=gog2[i], in_=gogS[:, :, t0:t0 + 128].rearrange("h v t -> v h t")),
                      r=[("gfm", t0 // TB, 1024 + 128 * j) for j in range(4)], w=[("gog", i)], dma=True)
            mk.op("pe", lambda e, i=i: e.matmul(psb[0][:, 0:256], lhsT=lrt[i], rhs=w2a, start=True, stop=True),
                  r=[("lrt", i), "w2a0", "w2a1"], w=[("ps", 0)])
            mk.op("act", lambda e: e.activation(out=e_, in_=psb[0][:, 0:256], func=AF.Exp, scale=-1.0), r=[("ps", 0)], w=["e_"])
            mk.op("act", lambda e: e.activation(out=sp_, in_=e_, func=AF.Ln, bias=cs[:, 130:131], scale=1.0), r=["e_", "cs"], w=["sp_"])
            yield
            for h in range(4):
                pbk = 1 + h // 2
                mk.op("pe", lambda e, h=h, pbk=pbk: e.matmul(psb[pbk][0:64, (h % 2) * 256:(h % 2 + 1) * 256], lhsT=sp_[:, h * 64:(h + 1) * 64],
                                                           rhs=M12, start=True, stop=True),
                      r=["sp_", "gc"], w=[("ps", pbk)])
            mk.op("pe", lambda e: e.matmul(psb[3][:, 0:256], lhsT=M3, rhs=sp_, start=True, stop=True), r=["sp_", "gc"], w=[("ps", 3)])
            yield
            for hp in range(2):
                v = psb[1 + hp][0:64, :].rearrange("p (h a l) -> p h a l", h=2, a=2)
                mk.op("act", lambda e, hp=hp, v=v: e.activation(out=E1[:, 2 * hp:2 * hp + 2, :], in_=v[:, :, 0, :], func=AF.Exp),
                      r=[("ps", 1 + hp)], w=[("E1", hp)])
                mk.op("act", lambda e, hp=hp, v=v: e.activation(out=E1i[:, 2 * hp:2 * hp + 2, :], in_=v[:, :, 0, :], func=AF.Exp, scale=-1.0),
                      r=[("ps", 1 + hp)], w=[("E1i", hp)])
                mk.op("act", lambda e, hp=hp, v=v: e.activation(out=EB2[i][:, 2 * hp:2 * hp + 2, :], in_=v[:, :, 1, :], func=AF.Exp),
                      r=[("ps", 1 + hp)], w=[("EB", i, hp)])
            mk.op("act", lambda e: e.activation(out=Ed, in_=psb[3][:, 0:256], func=AF.Exp), r=[("ps", 3)], w=["Ed"])
            yield
            ek = [("E1", 0), ("E1", 1)]
            mk.op("dve", lambda e, i=i: e.scalar_tensor_tensor(out=qe, in0=q_sb[i], scalar=0.125, in1=E1, op0=ALU.mult, op1=ALU.mult),
                  r=[("q_sb", i)] + ek, w=["qe"])
            mk.op("dve", lambda e, i=i: e.tensor_tensor(out=ke, in0=k_sb[i], in1=E1i, op=ALU.mult),
                  r=[("k_sb", i), ("E1i", 0), ("E1i", 1)], w=["ke"])
            mk.op("dve", lambda e, i=i: e.scalar_tensor_tensor(out=qb2[i], in0=q_sb[i], scalar=0.125, in1=EB2[i], op0=ALU.mult, op1=ALU.mult),
                  r=[("q_sb", i), ("EB", i, 0), ("EB", i, 1)], w=[("qb", i)])
            mk.op("dve", lambda e, i=i: e.tensor_tensor(out=kd2[i], in0=kt_sb[i], in1=Ed, op=ALU.mult), r=[("kt_sb", i), "Ed"], w=[("kd", i)])
            yield
            for h in range(4):
                mk.op("pe", lambda e, h=h: e.matmul(psb[4][:, h * 128:(h + 1) * 128], lhsT=ke[:, h, :], rhs=qe[:, h, :], start=True, stop=True),
                      r=["ke", "qe"], w=[("ps", 4)])
            mk.op("dve", lambda e: e.tensor_tensor(out=attm2[i].rearrange("p h l -> p (h l)"), in0=psb[4][:], in1=mask4f, op=ALU.mult),
                  r=[("ps", 4), "gc"], w=[("attm", i)])
            for ci, cc in enumerate((0, 1) if dr == 0 else (1, 0)):
                sbk = 5 if ci == 0 else 7
                for h in range(4):
                    mk.op("pe", lambda e, h=h, cc=cc, sbk=sbk: e.matmul(psb[sbk][0:64, h * 128:(h + 1) * 128],
                                                                    lhsT=kd2[i][cc * 64:(cc + 1) * 64, h * 64:(h + 1) * 64],
                                                                    rhs=vt_sb[i][cc * 64:(cc + 1) * 64, h * 128:(h + 1) * 128], start=True, stop=True),
                          r=[("kd", i), ("vt_sb", i)], w=[("ps", sbk)])
            yield

        def tileB(n, t):
            i = n % 2
            t0 = t * 128
            seg = t // tiles_per_seg
            first_in_seg = (t % tiles_per_seg == 0) if dr == 0 else (t % tiles_per_seg == tiles_per_seg - 1)
            if first_in_seg:
                linked = (seg == 1) if dr == 0 else (seg == 0)
                if linked:
                    mk.op("dve", lambda e: e.tensor_scalar(out=S, in0=S, scalar1=pv[0:64, PV_LINK:PV_LINK + 1], scalar2=None, op0=ALU.mult),
                          r=["S", "pv"], w=["S"])
                else:
                    mk.op("dve", lambda e: e.memset(S, 0.0), r=["S"], w=["S"])
            corder = (0, 1) if dr == 0 else (1, 0)
            mk.op("act", lambda e: e.copy(out=Sb[0], in_=S), r=["S"], w=[("Sb", 0)])
            yield
            for ci, cc in enumerate(corder):
                sbk = 5 if ci == 0 else 7
                col = cc * 64 + (63 if dr == 0 else 0)
                mk.op("dve", lambda e, col=col: e.tensor_tensor(out=S, in0=S, in1=EB2[i][:, :, col:col + 1].broadcast_to([64, 4, 128]), op=ALU.mult),
                      r=[("EB", i, 0), ("EB", i, 1), "S"] + ([("Sb", 0)] if ci == 0 else []), w=["S"])
                mk.op("dve", lambda e, sbk=sbk: e.tensor_tensor(out=S.rearrange("p h v -> p (h v)"), in0=psb[sbk][0:64, :], in1=S.rearrange("p h v -> p (h v)"), op=ALU.add),
                      r=[("ps", sbk), "S"], w=["S"])
                if ci == 0:
                    mk.op("act", lambda e: e.copy(out=Sb[1], in_=S), r=["S"], w=[("Sb", 1)])
                yield
            for h in range(4):
                mk.op("pe", lambda e, h=h, i=i: e.matmul(psb[6][:, h * 128:(h + 1) * 128], lhsT=vt_sb[i][:, h * 128:(h + 1) * 128], rhs=attm2[i][:, h, :],
                                                      start=True, stop=False),
                      r=[("vt_sb", i), ("attm", i)], w=[("ps", 6)])
                for ci, cc in enumerate(corder):
                    mk.op("pe", lambda e, h=h, ci=ci, cc=cc: e.matmul(psb[6][:, h * 128 + cc * 64:h * 128 + (cc + 1) * 64], lhsT=Sb[ci][:, h, :],
                                                                   rhs=qb2[i][:, h, cc * 64:(cc + 1) * 64], start=False, stop=(ci == 1),
                                                                   skip_group_check=True),
                          r=[("Sb", ci), ("qb", i)], w=[("ps", 6)])
            if dr == 0:
                mk.op("act", lambda e: e.copy(out=osb, in_=psb[6][:]), r=[("ps", 6)], w=["osb"])
                mk.op("sp", lambda e, t0=t0: e.dma_start(out=goacc[:, :, t0:t0 + 128].rearrange("h v t -> v h t"),
                                                       in_=osb.rearrange("p (h l) -> p h l", h=4)),
                      r=["osb"], w=[("goacc", t0 // 128)], dma=True)
            else:
                mk.op("dve", lambda e: e.tensor_tensor(out=osb, in0=psb[6][:], in1=oprev2[i].rearrange("p h l -> p (h l)"), op=ALU.add),
                      r=[("ps", 6), ("oprev", i)], w=["osb"])
                mk.op("act", lambda e: e.activation(out=sqo, in_=osb, func=AF.Square), r=["osb"], w=["sqo"])
                mk.op("pe", lambda e: e.matmul(psb[7][:], lhsT=ones_bf[:], rhs=sqo, start=True, stop=True), r=["sqo", "ones_bf"], w=[("ps", 7)])
                mk.op("act", lambda e: e.activation(out=rr, in_=psb[7][:], func=AF.Ln, scale=1.0 / 128, bias=cs[:, 128:129]),
                      r=[("ps", 7), "cs"], w=["rr0"])
                mk.op("act", lambda e: e.activation(out=rr, in_=rr, func=AF.Exp, scale=-0.5), r=["rr0"], w=["rr"])
                mk.op("act", lambda e: e.activation(out=sg_, in_=gog2[i].rearrange("p h l -> p (h l)"), func=AF.Exp, scale=-1.0), r=[("gog", i)], w=["sg0"])
                mk.op("dve", lambda e: e.tensor_scalar(out=sg_, in0=sg_, scalar1=1.0, scalar2=None, op0=ALU.add), r=["sg0"], w=["sg1"])
                mk.op("dve", lambda e: e.reciprocal(out=sg_, in_=sg_), r=["sg1"], w=["sg2"])
                mk.op("dve", lambda e: e.tensor_tensor(out=sg_, in0=sg_, in1=gog2[i].rearrange("p h l -> p (h l)"), op=ALU.mult), r=["sg2", ("gog", i)], w=["sg3"])
                mk.op("dve", lambda e: e.scalar_tensor_tensor(out=rr, in0=rr, scalar=pv[:, PV_GLAW + l:PV_GLAW + l + 1], in1=sg_,
                                                            op0=ALU.mult, op1=ALU.mult), r=["rr", "sg3", "pv"], w=["rr2"])
                mk.op("dve", lambda e: e.tensor_tensor(out=oo, in0=osb, in1=rr, op=ALU.mult), r=["osb", "rr2"], w=["oo"])
                mk.op("sp", lambda e, t0=t0: e.dma_start(out=oG[:, :, t0:t0 + 128].rearrange("h v t -> v h t"),
                                                       in_=oo.rearrange("p (h l) -> p h l", h=4)),
                      r=["oo"], w=[("oG", t0 // TB)], dma=True)
            yield

        def drive(*gens):
            gens = [x for x in gens if x is not None]
            while gens:
                for x in list(gens):
                    try:
                        next(x)
                    except StopIteration:
                        gens.remove(x)

        drive(tileA(0, order[0]))
        for n, t in enumerate(order):
            drive(tileA(n + 1, order[n + 1]) if n + 1 < NT else None, tileB(n, t))

    def ssd_inproj_setup(l, stg):
        NW = 3104
        wz = A.alloc([128, 8, NW], BF16)
        ob_ = [A.alloc([128, TB], BF16) for i in range(2)]
        zt = [A.alloc([128, 1024], BF16) for i in range(2)]
        dtt = [A.alloc([128, 32], F32) for i in range(2)]
        for k, n0 in enumerate(range(0, NW, 1024)):
            n1 = min(NW, n0 + 1024)
            load_weight(wz[:, :, n0:n1], w_in[l][:, n0:n1], 8, n1 - n0, stg, scale_col0=PV_NORM1 + 8 * l, tag="wz%d" % k)

        def wkey(c, col):
            return ("W", "wz%d" % (col // 1024), c, 0)
        itc = [0]

        def body(b, hb, par):
            for j in range(16):
                i = itc[0] % 2
                itc[0] += 1
                pa = 1 + i
                col = 1024 + 128 * j
                for c in range(8):
                    mk.op("pe", lambda e, c=c, col=col, pa=pa: e.matmul(psb[pa][:], lhsT=wz[:, c, col:col + 128], rhs=hb[:, c, :],
                                                                    start=(c == 0), stop=(c == 7)),
                          r=[("hb", par, c), wkey(c, col)], w=[("ps", pa)])
                evac(("act", "dve")[i], ob_[i], psb[pa][:], r=[("ps", pa)], w=[("ob_", i)])
                mk.op("sp", lambda e, i=i, j=j, b=b: e.dma_start(out=xbcS[j, :, b * TB:(b + 1) * TB], in_=ob_[i]),
                      r=[("ob_", i)], w=[("xbcS", b)], dma=True)
            for tt in range(4):
                i = tt % 2
                for hf in range(2):
                    for c in range(8):
                        mk.op("pe", lambda e, c=c, tt=tt, hf=hf, i=i: e.matmul(psb[3 + 2 * i + hf][:], lhsT=hb[:, c, tt * 128:(tt + 1) * 128],
                                                                            rhs=wz[:, c, hf * 512:(hf + 1) * 512], start=(c == 0), stop=(c == 7)),
                              r=[("hb", par, c), wkey(c, 0)], w=[("ps", 3 + 2 * i + hf)])
                    evac(("act", "dve")[hf], zt[i][:, hf * 512:(hf + 1) * 512], psb[3 + 2 * i + hf][:], r=[("ps", 3 + 2 * i + hf)], w=[("zt", i, hf)])
                for c in range(8):
                    mk.op("pe", lambda e, c=c, tt=tt: e.matmul(psb[7][:, 0:32], lhsT=hb[:, c, tt * 128:(tt + 1) * 128],
                                                             rhs=wz[:, c, 3072:3104], start=(c == 0), stop=(c == 7)),
                          r=[("hb", par, c), wkey(c, 3072)], w=[("ps", 7)])
                evac("act", dtt[i], psb[7][:, 0:32], r=[("ps", 7)], w=[("dtt", i)])
                t0 = b * TB + tt * 128
                mk.op("sp", lambda e, i=i, t0=t0: e.dma_start(out=zS[t0:t0 + 128, :], in_=zt[i]), r=[("zt", i, 0), ("zt", i, 1)], w=[("zS", t0 // 128)], dma=True)
                mk.op("sp", lambda e, i=i, t0=t0: e.dma_start(out=dtS[t0:t0 + 128, :], in_=dtt[i]), r=[("dtt", i)], w=[("dtS", t0 // 128)], dma=True)
        return body

    def ssd_pass(l, dr):
        phase()
        sc = A.alloc([128, NSC], F32)
        mk.op("sp", lambda e: e.dma_start(out=sc, in_=scst), w=["sc"], dma=True)
        sr = A.alloc([128, NSR], F32)
        mk.op("sp", lambda e: e.dma_start(out=sr, in_=srow[l]), w=["sr"], dma=True)
        U = sc[:, dr * 256:dr * 256 + 128]
        SL = sc[:, dr * 256 + 128:dr * 256 + 256]
        mask4 = sc[:, 512 + dr * 512:512 + (dr + 1) * 512]
        ones_f = A.alloc([128, 128], F32)
        mk.op("dve", lambda e: e.memset(ones_f, 1.0), w=["ones_f"])
        arow = A.alloc([128, 16], F32)
        mk.op("act", lambda e: e.activation(out=arow, in_=sr[:, 32 + dr * 16:48 + dr * 16], func=AF.Exp), r=["sr"], w=["arow0"])
        mk.op("dve", lambda e: e.tensor_scalar(out=arow, in0=arow, scalar1=-1.0, scalar2=None, op0=ALU.mult), r=["arow0"], w=["arow"])
        S = A.alloc([128, 1024], F32)
        Sbf = A.alloc([128, 1024], BF16)
        Wt = A.alloc([128, 16, 128], BF16)
        xw = A.alloc([128, 1024], BF16)
        xdt = A.alloc([128, 1024], BF16)
        yd = A.alloc([128, 1024], F32)
        y = A.alloc([128, 1024], F32)
        adtU = A.alloc([128, 16, 128], F32)
        D2 = lambda shape, dt_: [A.alloc(shape, dt_) for _ in range(2)]
        D3 = lambda shape, dt_: [A.alloc(shape, dt_) for _ in range(3)]
        dtr = D3([128, 32], F32)
        dt = D2([128, 16], F32)
        ldt = D2([128, 16], F32)
        adt = D2([128, 16], F32)
        ex = D2([128, 48], F32)
        wdec = D2([128, 16], F32)
        Lm = D2([128, 16, 128], BF16)
        SM = D2([128, 4, 128], BF16)
        xtok = D2([128, 1024], BF16)
        btok = D2([128, 512], BF16)
        if dr == 0:
            dg = A.alloc([128, 80, 128], BF16)
            for j in range(16):
                for k in range(5):
                    cb = PV_CONV + (l * 16 + j) * 6 + k
                    eng = ("pool", "dve", "act")[(j * 5 + k) % 3]
                    if eng == "act":
                        mk.op("act", lambda e, j=j, k=k, cb=cb: e.activation(out=dg[:, j * 5 + k, :], in_=ident, func=AF.Copy, scale=pv[:, cb:cb + 1]),
                              r=["cs", "pv"], w=[("dg", j)])
                    else:
                        mk.op(eng, lambda e, j=j, k=k, cb=cb: e.tensor_scalar(out=dg[:, j * 5 + k, :], in0=ident, scalar1=pv[:, cb:cb + 1],
                                                                          scalar2=None, op0=ALU.mult), r=["cs", "pv"], w=[("dg", j)])
            xh = D3([128, 16, 132], BF16)
            xc = D2([128, 16, 128], BF16)
        else:
            bc = D3([128, 8, 128], BF16)
            yprev = D3([128, 1024], F32)
            zt = D3([128, 1024], BF16)
            xtok3 = D3([128, 1024], BF16)
            btok3 = D3([128, 512], BF16)
            zs = A.alloc([128, 1024], F32)
            tmpd = A.alloc([128, 1024], F32)
            ss = A.alloc([128, 2], F32)
            yn = A.alloc([128, 1024], BF16)
            yT = A.alloc([128, 8, 128], BF16)
        NT = T // 128
        tps = SEGL // 128
        order = list(range(NT)) if dr == 0 else list(range(NT - 1, -1, -1))

        def stage0(n, t):
            q3 = n % 3
            t0 = t * 128
            seg = t // tps
            K3 = lambda nm: (nm, "q", q3)
            mk.op("sp", lambda e: e.dma_start(out=dtr[q3], in_=dtS[t0:t0 + 128, :]), r=[("dtS", t)], w=[K3("dtr")], dma=True)
            if dr == 0:
                xh_ = xh[q3]
                mk.op("sp", lambda e: e.dma_start(out=xh_[:, :, 2:130], in_=xbcS[:, :, t0:t0 + 128].rearrange("c p t -> p c t")),
                      r=[("xbcS", t0 // TB)], w=[K3("xh")], dma=True)
                for side in range(2):
                    at_edge = (t % tps == 0) if side == 0 else (t % tps == tps - 1)
                    lk = at_edge and ((seg == 1 and side == 0) or (seg == 0 and side == 1))
                    dcols = slice(0, 2) if side == 0 else slice(130, 132)
                    src0 = t0 - 2 if side == 0 else t0 + 128
                    if at_edge and not lk:
                        mk.op("pool", lambda e, dcols=dcols: e.memset(xh_[:, :, dcols], 0.0), w=[("xhh", q3, side)])
                    else:
                        nb_ = (src0 // TB)
                        mk.op("sp", lambda e, dcols=dcols, src0=src0: e.dma_start(
                            out=xh_[:, :, dcols], in_=xbcS[:, :, src0:src0 + 2].rearrange("c p t -> p c t")),
                            r=[("xbcS", nb_)], w=[("xhh", q3, side)], dma=True)
                        if lk:
                            mk.op("pool", lambda e, dcols=dcols: e.tensor_scalar(out=xh_[:, :, dcols], in0=xh_[:, :, dcols],
                                                                              scalar1=pv[:, PV_LINK:PV_LINK + 1], scalar2=None, op0=ALU.mult),
                                  r=[("xhh", q3, side), "pv"], w=[("xhh", q3, side)])
            else:
                mk.op("sp", lambda e: e.dma_start(out=bc[q3], in_=xcS[:, :, t0:t0 + 128].rearrange("c p t -> p c t")),
                      r=[("xcS", t)], w=[K3("bc")], dma=True)
                mk.op("sp", lambda e: e.dma_start(out=xtok3[q3], in_=xtokS[t0:t0 + 128, :]), r=[("xtokS", t)], w=[K3("xtok")], dma=True)
                mk.op("sp", lambda e: e.dma_start(out=btok3[q3], in_=btokS[t0:t0 + 128, :]), r=[("btokS", t)], w=[K3("btok")], dma=True)
                mk.op("sp", lambda e: e.dma_start(out=yprev[q3], in_=yaccS[t0:t0 + 128, :]), r=[("yaccS", t)], w=[K3("yprev")], dma=True)
                mk.op("sp", lambda e: e.dma_start(out=zt[q3], in_=zS[t0:t0 + 128, :]), r=[("zS", t)], w=[K3("zt")], dma=True)

        def stage1(n, t):
            p = n % 2
            q3 = n % 3
            t0 = t * 128
            seg = t // tps
            K = lambda nm: (nm, p)
            K3 = lambda nm: (nm, "q", q3)
            if dr == 0:
                xh_, xc_ = xh[q3], xc[p]
                xk = [K3("xh"), ("xhh", q3, 0), ("xhh", q3, 1)]
                for j in range(16):
                    bk = j % 4
                    for k in range(5):
                        mk.op("pe", lambda e, j=j, k=k, bk=bk: e.matmul(psb[bk][:, (j // 4) * 128:(j // 4 + 1) * 128], lhsT=dg[:, j * 5 + k, :],
                                                                    rhs=xh_[:, j, k:k + 128], start=(k == 0), stop=(k == 4)),
                              r=xk + [("dg", j)], w=[("ps", bk)])
                    cb = PV_CONV + (l * 16 + j) * 6 + 5
                    mk.op("act", lambda e, j=j, bk=bk, cb=cb: e.activation(out=xc_[:, j, :], in_=psb[bk][:, (j // 4) * 128:(j // 4 + 1) * 128],
                                                                       func=AF.Silu, bias=pv[:, cb:cb + 1]),
                          r=[("ps", bk), "pv"], w=[("xc", p, j // 8)])
                    yield
                mk.op("sp", lambda e: e.dma_start(out=xcS[:, :, t0:t0 + 128].rearrange("c p t -> p c t"), in_=xc_[:, 8:16, :]),
                      r=[("xc", p, 1)], w=[("xcS", t)], dma=True)
                pv_ = psb[0][:].bitcast(BF16)
                for j in range(8):
                    mk.op("pe", lambda e, j=j: e.transpose(out=pv_[:, j * 128:(j + 1) * 128], in_=xc_[:, j, :], identity=ident_bf[:]),
                          r=[("xc", p, 0), "ident_bf"], w=[("ps", 0)])
                mk.op("act", lambda e: e.copy(out=xtok[p], in_=pv_), r=[("ps", 0)], w=[K("xtok")])
                yield
                pb_ = psb[1][:].bitcast(BF16)
                for j in range(4):
                    mk.op("pe", lambda e, j=j: e.transpose(out=pb_[:, j * 128:(j + 1) * 128], in_=xc_[:, 8 + j, :], identity=ident_bf[:]),
                          r=[("xc", p, 1), "ident_bf"], w=[("ps", 1)])
                mk.op("dve", lambda e: e.tensor_copy(out=btok[p], in_=pb_[:, 0:512]), r=[("ps", 1)], w=[K("btok")])
                yield
                mk.op("sp", lambda e: e.dma_start(out=xtokS[t0:t0 + 128, :], in_=xtok[p]), r=[K("xtok")], w=[("xtokS", t)], dma=True)
                mk.op("sp", lambda e: e.dma_start(out=btokS[t0:t0 + 128, :], in_=btok[p]), r=[K("btok")], w=[("btokS", t)], dma=True)
                BC = xc_[:, 8:16, :]
                bck = [("xc", p, 1)]
            else:
                BC = bc[q3]
                bck = [K3("bc")]
            dt_, adt_, ex_, ldt_ = dt[p], adt[p], ex[p], ldt[p]
            mk.op("dve", lambda e: e.tensor_tensor(out=dt_, in0=dtr[q3][:, dr * 16:(dr + 1) * 16], in1=sr[:, dr * 16:(dr + 1) * 16], op=ALU.add),
                  r=[K3("dtr"), "sr"], w=[K("dt0")])
            mk.op("act", lambda e: e.activation(out=dt_, in_=dt_, func=AF.Exp), r=[K("dt0")], w=[K("dt1")])
            mk.op("act", lambda e: e.activation(out=dt_, in_=dt_, func=AF.Ln, bias=cs[:, 130:131], scale=1.0), r=[K("dt1"), "cs"], w=[K("dt")])
            mk.op("dve", lambda e: e.tensor_tensor(out=adt_, in0=dt_, in1=arow, op=ALU.mult), r=[K("dt"), "arow"], w=[K("adt")])
            yield
            mk.op("pe", lambda e: e.matmul(psb[2][:, 0:16], lhsT=U, rhs=adt_, start=True, stop=True), r=[K("adt"), "sc"], w=[("ps", 2)])
            mk.op("pe", lambda e: e.matmul(psb[2][:, 16:32], lhsT=SL, rhs=adt_, start=True, stop=True), r=[K("adt"), "sc"], w=[("ps", 2)])
            mk.op("pe", lambda e: e.matmul(psb[2][:, 32:48], lhsT=ones_f, rhs=adt_, start=True, stop=True), r=[K("adt"), "ones_f"], w=[("ps", 2)])
            mk.op("act", lambda e: e.activation(out=ex_, in_=psb[2][:, 0:48], func=AF.Exp), r=[("ps", 2)], w=[K("ex")])
            mk.op("dve", lambda e: e.tensor_tensor(out=wdec[p], in0=dt_, in1=ex_[:, 16:32], op=ALU.mult), r=[K("dt"), K("ex")], w=[K("wdec")])
            yield
            mk.op("dve", lambda e: e.tensor_tensor(out=adtU, in0=U.unsqueeze(1).broadcast_to([128, 16, 128]),
                                                   in1=adt_.unsqueeze(2).broadcast_to([128, 16, 128]), op=ALU.mult),
                  r=[K("adt"), "sc"], w=[("adtU", q) for q in range(4)])
            yield
            for q in range(4):
                mk.op("pe", lambda e, q=q: e.matmul(psb[q][:], lhsT=SL, rhs=adtU[:, 4 * q:4 * q + 4, :].rearrange("p h l -> p (h l)"),
                                                  start=True, stop=True), r=[("adtU", q), "sc"], w=[("ps", q)])
                mk.op("act", lambda e, q=q: e.activation(out=Lm[p][:, 4 * q:4 * q + 4, :].rearrange("p h l -> p (h l)"), in_=psb[q][:], func=AF.Exp),
                      r=[("ps", q)], w=[("Lm", p, q)])
                yield
            for g in range(4):
                mk.op("pe", lambda e, g=g: e.matmul(psb[3][:, g * 128:(g + 1) * 128], lhsT=BC[:, g, :], rhs=BC[:, 4 + g, :], start=True, stop=True),
                      r=bck, w=[("ps", 3)])
            mk.op("dve", lambda e: e.tensor_tensor(out=SM[p].rearrange("p g l -> p (g l)"), in0=psb[3][:], in1=mask4, op=ALU.mult),
                  r=[("ps", 3), "sc"], w=[K("SM")])
            yield
            return

        def stage2(n, t):
            p = n % 2
            t0 = t * 128
            seg = t // tps
            K = lambda nm: (nm, p)
            if dr == 0:
                BC = xc[p][:, 8:16, :]
                bck = [("xc", p, 1)]
            else:
                BC = bc[n % 3]
                bck = [("bc", "q", n % 3)]
            first_in_seg = (t % tps == 0) if dr == 0 else (t % tps == tps - 1)
            if first_in_seg:
                linked = (seg == 1) if dr == 0 else (seg == 0)
                if linked:
                    mk.op("dve", lambda e: e.tensor_scalar(out=S, in0=S, scalar1=pv[:, PV_LINK:PV_LINK + 1], scalar2=None, op0=ALU.mult),
                          r=["S", "pv"], w=["S"])
                else:
                    mk.op("dve", lambda e: e.memset(S, 0.0), r=["S"], w=["S"])
            q3 = n % 3
            K3 = lambda nm: (nm, "q", q3)
            if dr == 0:
                ex_, xtok_, btok_ = ex[p], xtok[p], btok[p]
                kx, kb = ("xtok", p), ("btok", p)
            else:
                ex_, xtok_, btok_ = ex[p], xtok3[q3], btok3[q3]
                kx, kb = K3("xtok"), K3("btok")
            mk.op("dve", lambda e: e.tensor_tensor(out=Wt.rearrange("p (g a) l -> p g a l", g=4), in0=Lm[p].rearrange("p (g a) l -> p g a l", g=4),
                                                   in1=SM[p].unsqueeze(2).broadcast_to([128, 4, 4, 128]), op=ALU.mult),
                  r=[("Lm", p, q) for q in range(4)] + [K("SM")], w=["Wt"])
            mk.op("dve", lambda e: e.tensor_tensor(out=xdt.rearrange("p (h d) -> p h d", h=16), in0=xtok_.rearrange("p (h d) -> p h d", h=16),
                                                   in1=dt[p].unsqueeze(2).broadcast_to([128, 16, 64]), op=ALU.mult),
                  r=[kx, K("dt")], w=["xdt"])
            yield
            for h in range(16):
                mk.op("pe", lambda e, h=h: e.matmul(psb[4 + h // 8][:, (h % 8) * 64:(h % 8 + 1) * 64], lhsT=Wt[:, h, :], rhs=xdt[:, h * 64:(h + 1) * 64],
                                                  start=True, stop=True), r=["Wt", "xdt"], w=[("ps", 4 + h // 8)])
                if h % 4 == 3:
                    yield
            mk.op("act", lambda e: e.copy(out=Sbf, in_=S), r=["S"], w=["Sbf"])
            for g in range(4):
                mk.op("pe", lambda e, g=g: e.matmul(psb[6 + g // 2][:, (g % 2) * 256:(g % 2 + 1) * 256], lhsT=BC[:, 4 + g, :],
                                                  rhs=Sbf[:, g * 256:(g + 1) * 256], start=True, stop=True),
                      r=bck + ["Sbf"], w=[("ps", 6 + g // 2)])
            yield
            mk.op("dve", lambda e: e.tensor_tensor(out=y.rearrange("p (h d) -> p h d", h=16), in0=psw[3][:].rearrange("p (h d) -> p h d", h=16),
                                                   in1=ex_[:, 0:16].unsqueeze(2).broadcast_to([128, 16, 64]), op=ALU.mult),
                  r=[("ps", 6), ("ps", 7), K("ex")], w=["y0"])
            mk.op("dve", lambda e: e.tensor_tensor(out=y, in0=psw[2][:], in1=y, op=ALU.add), r=[("ps", 4), ("ps", 5), "y0"], w=["yfull"])
            mk.op("dve", lambda e: e.tensor_tensor(out=xw.rearrange("p (h d) -> p h d", h=16), in0=xtok_.rearrange("p (h d) -> p h d", h=16),
                                                   in1=wdec[p].unsqueeze(2).broadcast_to([128, 16, 64]), op=ALU.mult),
                  r=[kx, K("wdec")], w=["xw"])
            yield
            for g in range(4):
                mk.op("pe", lambda e, g=g: e.matmul(psb[4 + g // 2][:, (g % 2) * 256:(g % 2 + 1) * 256], lhsT=btok_[:, g * 128:(g + 1) * 128],
                                                  rhs=xw[:, g * 256:(g + 1) * 256], start=True, stop=True),
                      r=[kb, "xw"], w=[("ps", 4 + g // 2)])
            mk.op("dve", lambda e: e.tensor_tensor(out=S.rearrange("p (h d) -> p h d", h=16), in0=S.rearrange("p (h d) -> p h d", h=16),
                                                   in1=ex_[:, 32:48].unsqueeze(2).broadcast_to([128, 16, 64]), op=ALU.mult),
                  r=[K("ex"), "S", "Sbf"], w=["S"])
            mk.op("dve", lambda e: e.tensor_tensor(out=S, in0=psw[2][:], in1=S, op=ALU.add), r=[("ps", 4), ("ps", 5), "S"], w=["S"])
            yield
            yk = ["yfull"]
            if dr == 0:
                mk.op("sp", lambda e: e.dma_start(out=yaccS[t0:t0 + 128, :], in_=y), r=yk, w=[("yaccS", t)], dma=True)
            else:
                mk.op("dve", lambda e: e.tensor_tensor(out=tmpd, in0=xtok_, in1=sr[:, 1104:2128], op=ALU.mult), r=[kx, "sr"], w=["tmpd"])
                mk.op("dve", lambda e: e.tensor_tensor(out=tmpd, in0=tmpd, in1=yprev[q3], op=ALU.add), r=["tmpd", K3("yprev")], w=["tmpd2"])
                mk.op("dve", lambda e: e.tensor_tensor(out=y, in0=y, in1=tmpd, op=ALU.add), r=yk + ["tmpd2"], w=["y3"])
                mk.op("act", lambda e: e.activation(out=zs, in_=zt[q3], func=AF.Silu), r=[K3("zt")], w=["zs"])
                mk.op("dve", lambda e: e.tensor_tensor(out=y, in0=y, in1=zs, op=ALU.mult), r=["y3", "zs"], w=["y4"])
                yield
                mk.op("act", lambda e: e.activation(out=zs, in_=y, func=AF.Square, accum_out=ss[:, 0:1]), r=["y4", "zs"], w=["ss0"])
                mk.op("act", lambda e: e.activation(out=ss[:, 1:2], in_=ss[:, 0:1], func=AF.Ln, scale=1.0 / 1024, bias=cs[:, 128:129]),
                      r=["ss0", "cs"], w=["ss1"])
                mk.op("act", lambda e: e.activation(out=ss[:, 1:2], in_=ss[:, 1:2], func=AF.Exp, scale=-0.5), r=["ss1"], w=["ss2"])
                mk.op("dve", lambda e: e.scalar_tensor_tensor(out=yn, in0=y, scalar=ss[:, 1:2], in1=sr[:, 80:80 + 1024], op0=ALU.mult, op1=ALU.mult),
                      r=["y4", "ss2", "sr"], w=["yn"])
                yield
                po_ = psb[6][:].bitcast(BF16)
                for j in range(8):
                    mk.op("pe", lambda e, j=j: e.transpose(out=po_[:, j * 128:(j + 1) * 128], in_=yn[:, j * 128:(j + 1) * 128], identity=ident_bf[:]),
                          r=["yn", "ident_bf"], w=[("ps", 6)])
                mk.op("act", lambda e: e.copy(out=yT.rearrange("p c t -> p (c t)"), in_=po_), r=[("ps", 6)], w=["yT"])
                mk.op("sp", lambda e: e.dma_start(out=oSS[:, :, t0:t0 + 128].rearrange("c p t -> p c t"), in_=yT),
                      r=["yT"], w=[("oSS", t0 // TB)], dma=True)

        def drive(*gens):
            gens = [x for x in gens if x is not None]
            while gens:
                for x in list(gens):
                    try:
                        next(x)
                    except StopIteration:
                        gens.remove(x)

        stage0(0, order[0])
        if NT > 1:
            stage0(1, order[1])
        drive(stage1(0, order[0]))
        for n, t in enumerate(order):
            if n + 2 < NT:
                stage0(n + 2, order[n + 2])
            drive(stage1(n + 1, order[n + 1]) if n + 1 < NT else None, stage2(n, t))

    def merge_phase(l, branches):
        phase()
        wg = A.alloc([128, 8, 3 * D], BF16)
        wbr = {"ssd": A.alloc([128, 8, D], BF16), "gla": A.alloc([128, 4, D], BF16), "att": A.alloc([128, 4, D], BF16)}
        wo = A.alloc([128, 8, D], BF16)
        stg = [A.alloc([128, PIECE], F32) for i in range(2)]
        xb2 = [A.alloc([128, 8, TB], F32) for i in range(2)]
        hb2 = [A.alloc([128, 8, TB], BF16) for i in range(2)]
        sq = [A.alloc([128, TB], BF16) for i in range(2)]
        rstd2 = [A.alloc([128, TB], F32) for i in range(2)]
        bin_ = {"ssd": A.alloc([128, 8, TB], BF16), "gla": A.alloc([128, 4, TB], BF16), "att": A.alloc([128, 4, TB], BF16)}
        sg = [A.alloc([128, TB], F32) for i in range(2)]
        macc = [A.alloc([128, TB], F32) for i in range(2)]
        mg = A.alloc([128, 8, TB], BF16)
        xo = [A.alloc([128, TB], F32) for i in range(2)]
        bidx = {"ssd": 0, "gla": 1, "att": 2}
        bsrc = {"ssd": w_br_ssd, "gla": w_br_gla, "att": w_br_attn}
        bkc = {"ssd": 8, "gla": 4, "att": 4}
        for br in branches:
            gi = bidx[br]
            load_weight(wg[:, :, gi * D:(gi + 1) * D], w_in[l][:, C_G + gi * D:C_G + (gi + 1) * D], 8, D, stg,
                        scale_col0=PV_NORM1 + 8 * l, tag="wg" + br)
            load_weight(wbr[br], bsrc[br][l], bkc[br], D, stg, tag="wbr" + br)
        load_weight(wo, w_out[l], 8, D, stg, tag="wo")
        itc = [0]
        load_xblock(xb2[0], 0, 0)
        rms_block(xb2[0], hb2[0], sq, rstd2[0], 0, 0)

        def block(b, par, xb, hb):
            if b + 1 < NB:
                load_xblock(xb2[1 - par], b + 1, 1 - par)
            for br in branches:
                if br == "ssd":
                    mk.op("sp", lambda e, b=b: e.dma_start(
                        out=bin_["ssd"], in_=oSS[:, :, b * TB:(b + 1) * TB].rearrange("c p t -> p c t")),
                        r=[("oSS", b)], w=[("bin", "ssd")], dma=True)
                if br == "gla":
                    mk.op("sp", lambda e, b=b: e.dma_start(
                        out=bin_["gla"], in_=oG[:, :, b * TB:(b + 1) * TB].rearrange("c p t -> p c t")),
                        r=[("oG", b)], w=[("bin", "gla")], dma=True)
                if br == "att":
                    mk.op("sp", lambda e, b=b: e.dma_start(
                        out=bin_["att"], in_=oS[:, :, b * TB:(b + 1) * TB].rearrange("(c two) d t -> (two d) c t", two=2)),
                        r=[("oS", b, h) for h in range(8)], w=[("bin", "att")], dma=True)
            for m in range(8):
                mi = m % 2
                for bi, br in enumerate(branches):
                    gi = bidx[br]
                    i = itc[0] % 2
                    itc[0] += 1
                    pgt, pbr = 1 + i, 3 + i
                    for c in range(8):
                        mk.op("pe", lambda e, c=c, m=m, gi=gi, pgt=pgt: e.matmul(
                            psb[pgt][:], lhsT=wg[:, c, gi * D + m * 128:gi * D + (m + 1) * 128], rhs=hb[:, c, :],
                            start=(c == 0), stop=(c == 7)),
                            r=[("hb", par, c), ("W", "wg" + br, c, 0)], w=[("ps", pgt)])
                    mk.op("act", lambda e, i=i, pgt=pgt: e.activation(out=sg[i], in_=psb[pgt][:], func=AF.Sigmoid),
                          r=[("ps", pgt)], w=[("sg", i)])
                    kc = bkc[br]
                    for c in range(kc):
                        mk.op("pe", lambda e, c=c, m=m, br=br, pbr=pbr, kc=kc: e.matmul(
                            psb[pbr][:], lhsT=wbr[br][:, c, m * 128:(m + 1) * 128], rhs=bin_[br][:, c, :],
                            start=(c == 0), stop=(c == kc - 1)),
                            r=[("bin", br), ("W", "wbr" + br, c, 0)], w=[("ps", pbr)])
                    last = (bi == len(branches) - 1)
                    if bi == 0:
                        dst = mg[:, m, :] if last else macc[mi]
                        mk.op("dve", lambda e, i=i, pbr=pbr, dst=dst: e.tensor_tensor(out=dst, in0=psb[pbr][:], in1=sg[i], op=ALU.mult),
                              r=[("ps", pbr), ("sg", i)], w=[("mg", m) if last else ("macc", mi)])
                    else:
                        mk.op("dve", lambda e, i=i, pbr=pbr: e.tensor_tensor(out=sg[i], in0=psb[pbr][:], in1=sg[i], op=ALU.mult),
                              r=[("ps", pbr), ("sg", i)], w=[("sg", i)])
                        dst = mg[:, m, :] if last else macc[mi]
                        mk.op("dve", lambda e, i=i, mi=mi, dst=dst: e.tensor_tensor(out=dst, in0=macc[mi], in1=sg[i], op=ALU.add),
                              r=[("macc", mi), ("sg", i)], w=[("mg", m) if last else ("macc", mi)])
            if b + 1 < NB:
                rms_block(xb2[1 - par], hb2[1 - par], sq, rstd2[1 - par], 0, 1 - par)
            for m in range(8):
                po = 5 + m % 3
                for c in range(8):
                    mk.op("pe", lambda e, c=c, m=m, po=po: e.matmul(psb[po][:], lhsT=wo[:, c, m * 128:(m + 1) * 128], rhs=mg[:, c, :],
                                                                 start=(c == 0), stop=(c == 7)),
                          r=[("mg", c), ("W", "wo", c, 0)], w=[("ps", po)])
                xoi = xo[m % 2]
                mk.op("dve", lambda e, m=m, po=po, xoi=xoi: e.tensor_tensor(out=xoi, in0=psb[po][:], in1=xb[:, m, :], op=ALU.add),
                      r=[("ps", po), ("xb", par, m)], w=[("xo", m % 2)])
                mk.op("sp", lambda e, m=m, b=b, xoi=xoi: e.dma_start(out=xT[m, :, b * TB:(b + 1) * TB], in_=xoi),
                      r=[("xo", m % 2)], w=[("xT", b)], dma=True)

        for b in range(NB):
            block(b, b % 2, xb2[b % 2], hb2[b % 2])

    def ffn_phase(l):
        phase()
        wfi = A.alloc([128, 8, 2 * DFF], BF16)
        wfo = A.alloc([128, 22, D], BF16)
        stg = [A.alloc([128, PIECE], F32) for i in range(2)]
        xb = A.alloc([128, 8, TB], F32)
        hb = A.alloc([128, 8, TB], BF16)
        hid = A.alloc([128, 22, TB], BF16)
        sq = [A.alloc([128, TB], BF16) for i in range(2)]
        rstd = A.alloc([128, TB], F32)
        tmpf = [A.alloc([128, TB], F32) for i in range(2)]
        xo = [A.alloc([128, TB], F32) for i in range(2)]
        load_weight(wfi, w_ffn_in[l], 8, 2 * DFF, stg, scale_col0=PV_NORM2 + 8 * l, tag="ffn_in")
        load_weight(wfo, w_ffn_out[l], 22, D, stg, tag="ffn_out")
        for b in range(NB):
            load_xblock(xb, b)
            rms_block(xb, hb, sq, rstd, 0)
            for j in range(22):
                pg = 1 + (2 * j) % 6
                pu = pg + 1
                for c in range(8):
                    mk.op("pe", lambda e, c=c, j=j, pg=pg: e.matmul(psb[pg][:], lhsT=wfi[:, c, j * 128:(j + 1) * 128],
                                                                 rhs=hb[:, c, :], start=(c == 0), stop=(c == 7)),
                          r=[("hb", 0, c), ("W", "ffn_in", c, (j * 128) // PIECE)], w=[("ps", pg)])
                for c in range(8):
                    mk.op("pe", lambda e, c=c, j=j, pu=pu: e.matmul(psb[pu][:], lhsT=wfi[:, c, DFF + j * 128:DFF + (j + 1) * 128],
                                                                 rhs=hb[:, c, :], start=(c == 0), stop=(c == 7)),
                          r=[("hb", 0, c), ("W", "ffn_in", c, (DFF + j * 128) // PIECE)], w=[("ps", pu)])
                tf = tmpf[j % 2]
                mk.op("act", lambda e, pg=pg, tf=tf: e.activation(out=tf, in_=psb[pg][:], func=AF.Silu),
                      r=[("ps", pg)], w=[("tmpf", j % 2)])
                mk.op("dve", lambda e, pu=pu, tf=tf, j=j: e.tensor_tensor(out=hid[:, j, :], in0=psb[pu][:], in1=tf, op=ALU.mult),
                      r=[("ps", pu), ("tmpf", j % 2)], w=[("hid", j)])
            for m in range(8):
                po = 1 + m % 7
                for j in range(22):
                    mk.op("pe", lambda e, m=m, j=j, po=po: e.matmul(psb[po][:], lhsT=wfo[:, j, m * 128:(m + 1) * 128],
                                                                 rhs=hid[:, j, :], start=(j == 0), stop=(j == 21)),
                          r=[("hid", j), ("W", "ffn_out", j, 0)], w=[("ps", po)])
                xoi = xo[m % 2]
                mk.op("dve", lambda e, m=m, po=po, xoi=xoi: e.tensor_tensor(out=xoi, in0=psb[po][:], in1=xb[:, m, :], op=ALU.add),
                      r=[("ps", po), ("xb", 0, m)], w=[("xo", m % 2)])
                mk.op("sp", lambda e, m=m, b=b, xoi=xoi: e.dma_start(out=xT[m, :, b * TB:(b + 1) * TB], in_=xoi),
                      r=[("xo", m % 2)], w=[("xT", b)], dma=True)


    def final_phase():
        phase()
        xb2 = [A.alloc([128, 8, TB], F32) for i in range(2)]
        sq = [A.alloc([128, TB], BF16) for i in range(2)]
        rstd2 = [A.alloc([128, TB], F32) for i in range(2)]
        yt = [A.alloc([128, D], F32) for i in range(4)]
        outs = [dbg_o]
        load_xblock(xb2[0], 0, 0)

        def block(b, par, xb, rstd):
            if b + 1 < NB:
                load_xblock(xb2[1 - par], b + 1, 1 - par)
            rms_block(xb, None, sq, rstd, 0, par)
            for c in range(8):
                mk.op("dve", lambda e, c=c: e.scalar_tensor_tensor(out=xb[:, c, :], in0=xb[:, c, :], scalar=pv[:, PV_FINAL + c:PV_FINAL + c + 1],
                                                               in1=rstd, op0=ALU.mult, op1=ALU.mult),
                      r=[("xb", par, c), ("rstd", par), "pv"], w=[("xb", par, c)])
            for tt in range(4):
                i = (b * 4 + tt) % 4
                for half in range(2):
                    pb = 1 + ((b * 4 + tt) * 2 + half) % 7
                    for c4 in range(4):
                        c = half * 4 + c4
                        mk.op("pe", lambda e, c=c, c4=c4, tt=tt, pb=pb: e.transpose(
                            out=psb[pb][:, c4 * 128:(c4 + 1) * 128], in_=xb[:, c, tt * 128:(tt + 1) * 128], identity=ident),
                            r=[("xb", par, c), "cs"], w=[("ps", pb)])
                    evac(("act", "dve")[half], yt[i][:, half * 512:(half + 1) * 512], psb[pb][:], r=[("ps", pb)], w=[("yt", i, half)])
                t = b * 4 + tt
                o = mk.op("sp", lambda e, t=t, i=i: e.dma_start(out=y_out[t * 128:(t + 1) * 128, :], in_=yt[i]),
                          r=[("yt", i, 0), ("yt", i, 1)], w=[("yout", t)], dma=True)
                outs.append(o)

        for b in range(NB):
            block(b, b % 2, xb2[b % 2], rstd2[b % 2])
        return outs

    for l in range(DEPTH):
        if STAGE >= 1:
            inproj_phase(l)
            attention_phase(l)
            brs = ["att"]
            if STAGE >= 2:
                gla_pass(l, 0)
                gla_pass(l, 1)
                brs = ["gla", "att"]
            if STAGE >= 3:
                ssd_pass(l, 0)
                ssd_pass(l, 1)
                brs = ["ssd", "gla", "att"]
            merge_phase(l, brs)
        if STAGE >= 0:
            ffn_phase(l)
    outs = final_phase()
    mk.emit(final_waits=outs)
    return nc


PV_NORM1 = 0
PV_NORM2 = 16
PV_FINAL = 32
PV_QW = 40
PV_KW = 42
PV_LINK = 44
PV_GLAW = 46
NGC = 1792
PV_CONV = 64
NSC = 1536
NSR = 2128
NPV = 256
NCST = 392


def _pack_pvec(inp):
    pv = np.zeros((128, NPV), np.float32)
    for l in range(DEPTH):
        pv[:, PV_NORM1 + 8 * l:PV_NORM1 + 8 * l + 8] = inp["norm1_w"][l].reshape(8, 128).T
        pv[:, PV_NORM2 + 8 * l:PV_NORM2 + 8 * l + 8] = inp["norm2_w"][l].reshape(8, 128).T
    pv[:, PV_FINAL:PV_FINAL + 8] = inp["final_norm_w"].reshape(8, 128).T
    for l in range(DEPTH):
        for j in range(16):
            cb = PV_CONV + (l * 16 + j) * 6
            pv[:, cb:cb + 5] = inp["conv_w"][l][:, j * 128:(j + 1) * 128].T
            pv[:, cb + 5] = inp["conv_b"][l][j * 128:(j + 1) * 128]
        pv[:, PV_GLAW + l] = inp["gla_norm_w"][l]
        pv[:, PV_QW + l] = np.tile(inp["att_q_norm_w"][l], 2)
        pv[:, PV_KW + l] = np.tile(inp["att_k_norm_w"][l], 2)
    return pv


def _consts():
    c = np.zeros((128, NCST), np.float32)
    c[:, 0:128] = np.eye(128, dtype=np.float32)
    c[:, 128] = EPS
    c[:, 129] = 64 * EPS
    c[:, 130] = 1.0
    for m in range(128):
        if (m % 32) < 16:
            c[m + 16, 136 + m] = -1.0
        else:
            c[m - 16, 136 + m] = 1.0
    for m in range(128):
        c[(m // 64) * 64:(m // 64 + 1) * 64, 264 + m] = 1.0
    return c


def _rope_tables(core):
    t = np.arange(T)
    seg = t // SEGL
    pos = t % SEGL + np.where((seg == 1) & (core < 4), SEGL, 0)
    row = (pos // 64).astype(np.float32)
    col = (pos % 64).astype(np.float32)
    inv = (np.float32(10000.0) ** (-np.arange(16, dtype=np.float32) / np.float32(16))).astype(np.float32)
    p = np.arange(128)
    d = p % 64
    sec = d // 32
    f = d % 16
    axis = np.where(sec[:, None] == 0, row[None, :], col[None, :]).astype(np.float32)
    ang = (axis * inv[f][:, None]).astype(np.float32)
    return np.cos(ang).astype(np.float32), np.sin(ang).astype(np.float32)


def _ssd_consts():
    g = np.zeros((128, NSC), np.float32)
    t = np.arange(128)[:, None]
    x = np.arange(128)[None, :]
    g[:, 0:128] = (t <= x)
    g[:, 128:256] = (t > x)
    g[:, 256:384] = (t >= x)
    g[:, 384:512] = (t < x)
    g[:, 512:1024] = np.tile((x >= t).astype(np.float32), (1, 4))
    g[:, 1024:1536] = np.tile((x <= t).astype(np.float32), (1, 4))
    return g


def _ssd_rows(inp):
    r = np.zeros((DEPTH, 128, NSR), np.float32)
    for l in range(DEPTH):
        row = np.concatenate([inp["ssd_dt_bias_f"][l], inp["ssd_dt_bias_b"][l], inp["ssd_a_log_f"][l], inp["ssd_a_log_b"][l],
                              inp["ssd_d"][l], inp["ssd_norm_w"][l], np.repeat(inp["ssd_d"][l], 64)]).astype(np.float32)
        r[l, :, :] = row[None, :]
    return r


def _gla_consts():
    g = np.zeros((128, NGC), np.float32)
    t = np.arange(128)[:, None]
    l_ = np.arange(128)[None, :]
    same = (t // 64) == (l_ // 64)
    for dr in range(2):
        if dr == 0:
            U = same & (t <= l_)
            R = same & ((t % 64) <= 32)
            mask = same & (l_ >= t)
        else:
            U = same & (t >= l_)
            R = same & ((t % 64) >= 31)
            mask = same & (l_ <= t)
        J = same
        o0 = dr * 896
        g[:, o0:o0 + 128] = -(U.astype(np.float32) - R.astype(np.float32)) / 16.0
        g[:, o0 + 128:o0 + 256] = -U.astype(np.float32) / 16.0
        g[:, o0 + 256:o0 + 384] = -(J.astype(np.float32) - U.astype(np.float32)) / 16.0
        g[:, o0 + 384:o0 + 896] = np.tile(mask.astype(np.float32), (1, 4))
    return g


_NC_CACHE = {}


def kernel(**inputs):
    inp = {k: np.asarray(v) for k, v in inputs.items()}
    xp = inp["x_prompt"]
    xs = inp["x_sample"]
    if "nc" not in _NC_CACHE:
        _NC_CACHE["nc"] = build_program()
    nc = _NC_CACHE["nc"]
    pv = _pack_pvec(inp)
    cst = _consts()
    gcs = _gla_consts()
    scs = _ssd_consts()
    srw = _ssd_rows(inp)
    in_maps = []
    for c in range(NCORES):
        if c < 4:
            xc = np.concatenate([xs[c], xp[c]], axis=0)
        else:
            j = 4 + 3 * (c - 4)
            xc = np.concatenate([xp[j], xp[j + 1], xp[j + 2]], axis=0)
        pvc = pv.copy()
        pvc[:, PV_LINK] = 1.0 if c < 4 else 0.0
        rc, rs_ = _rope_tables(c)
        in_maps.append({
            "x": np.ascontiguousarray(xc, dtype=np.float32),
            "w_ffn_in": inp["w_ffn_in"], "w_ffn_out": inp["w_ffn_out"],
            "w_in": inp["w_in"], "w_br_attn": inp["w_br_attn"], "w_br_gla": inp["w_br_gla"],
            "w_br_ssd": inp["w_br_ssd"], "w_out": inp["w_out"],
            "ropec": rc, "ropes": rs_, "gcst": gcs, "scst": scs, "srow": srw,
            "gla_w2_f": inp["gla_w2_f"], "gla_w2_b": inp["gla_w2_b"], "gla_b_f": inp["gla_b_f"], "gla_b_b": inp["gla_b_b"],
            "pvec": pvc, "cst": cst,
        })
    res = run_bass_kernel_spmd(nc, in_maps, core_ids=list(range(NCORES)))
    yp = np.zeros_like(xp)
    ys = np.zeros_like(xs)
    for c in range(NCORES):
        y = np.asarray(res.results[c]["y"]).reshape(T, D)
        if c < 4:
            ys[c] = y[:2 * SEGL]
            yp[c] = y[2 * SEGL:]
        else:
            j = 4 + 3 * (c - 4)
            for s in range(3):
                yp[j + s] = y[s * SEGL:(s + 1) * SEGL]
    return (yp, ys)
```

```python
import os
import numpy as np
import ml_dtypes
import concourse.bass as bass
import concourse.mybir as mybir
from concourse.bass_utils import run_bass_kernel_spmd

F32 = mybir.dt.float32
BF16 = mybir.dt.bfloat16
AF = mybir.ActivationFunctionType
ALU = mybir.AluOpType
AX = mybir.AxisListType

NCORES = 8
D = 1024
NSEG = 3
SEGL = 2048
T = NSEG * SEGL
TB = 512
NB = T // TB
DFF = 2816
DEPTH = 2
EPS = 1e-6
IN_W = 8512

STAGE = int(os.environ.get("MK_STAGE", "9"))


class _Op:
    __slots__ = ("eng", "fn", "deps", "inc", "idx", "dma", "sem", "val", "cnt")


class MK:
    ENGS = ("pe", "act", "dve", "pool", "sp")
    EPOCH = 20000
    KDMA = 12

    def __init__(self, nc):
        self.nc = nc
        self.ops = {e: [] for e in self.ENGS}
        self.last_w = {}
        self.rd = {}
        self.ndma = {e: 0 for e in self.ENGS}
        self.dma_ops = {e: [] for e in self.ENGS}
        self.pending = {}

    def barrier(self, fn):
        deps = []
        for e in self.ENGS:
            cl = [o for o in self.ops[e] if not o.dma]
            if cl and e != "pool":
                deps.append(cl[-1])
            deps.extend(self.dma_ops[e][-self.KDMA:])
        o = self.op("pool", fn, _extra=deps)
        for e in self.ENGS:
            if e != "pool":
                self.pending[e] = o
        return o

    STORE_KEYS = {"xT", "qkS", "vS", "oS", "gfm", "lrS", "gtm", "goacc", "oG", "xbcS", "zS", "dtS", "xcS", "xtokS",
                  "btokS", "yaccS", "oSS", "yout", "dbgo"}

    def op(self, eng, fn, r=(), w=(), dma=False, _extra=()):
        if dma and eng == "sp" and w:
            k0 = w[0]
            nm = k0 if isinstance(k0, str) else k0[0]
            if nm in self.STORE_KEYS:
                eng = "pool"
        lst = self.ops[eng]
        o = _Op()
        o.eng = eng; o.fn = fn; o.idx = len(lst); o.dma = dma; o.inc = False
        o.sem = None; o.val = 0; o.cnt = 0
        deps = {}
        for k in r:
            lw = self.last_w.get(k)
            if lw is not None:
                deps[id(lw)] = lw
        for k in w:
            lw = self.last_w.get(k)
            if lw is not None:
                deps[id(lw)] = lw
            rdk = self.rd.get(k)
            if rdk:
                for d in rdk.values():
                    deps[id(d)] = d
        for d in _extra:
            deps[id(d)] = d
        pb = self.pending.pop(eng, None)
        if pb is not None:
            deps[id(pb)] = pb
        out = []
        for d in deps.values():
            if (not d.dma) and (not dma) and d.eng == eng:
                if eng == "pe" or d.idx < o.idx - 1:
                    continue
            out.append(d)
            d.inc = True
        if dma:
            j = self.ndma[eng]
            self.ndma[eng] = j + 1
            o.sem = j % self.KDMA
            o.val = 16 * (j // self.KDMA + 1)
            if j >= self.KDMA:
                out.append(self.dma_ops[eng][j - self.KDMA])
            self.dma_ops[eng].append(o)
        o.deps = out
        for k in r:
            self.rd.setdefault(k, {})[("d", eng, o.idx) if dma else eng] = o
        for k in w:
            self.last_w[k] = o
            self.rd[k] = {}
        lst.append(o)
        return o

    def emit(self, final_waits=()):
        nc = self.nc
        nep = {}
        for e in self.ENGS:
            c = 0
            for o in self.ops[e]:
                if o.dma:
                    continue
                if o.inc:
                    c += 1
                    o.cnt = c
            nep[e] = (c + self.EPOCH - 1) // self.EPOCH + 1
        esems = {e: [nc.alloc_semaphore(name=f"s_{e}_{i}") for i in range(nep[e])] for e in self.ENGS}
        dsems = {e: [nc.alloc_semaphore(name=f"d_{e}_{i}") for i in range(self.KDMA)] if self.ndma[e] else []
                 for e in self.ENGS}
        engobj = {"pe": "tensor", "act": "scalar", "dve": "vector", "pool": "gpsimd", "sp": "sync"}

        def tok(d):
            if d.dma:
                return dsems[d.eng][d.sem], d.val
            ep = (d.cnt - 1) // self.EPOCH
            return esems[d.eng][ep], (d.cnt - 1) % self.EPOCH + 1

        def body(e, eo):
            waited = {}
            for o in self.ops[e]:
                for d in o.deps:
                    s, v = tok(d)
                    key = id(s)
                    if waited.get(key, 0) >= v:
                        continue
                    waited[key] = v
                    eo.wait_ge(s, v)
                ins = o.fn(eo)
                if o.dma:
                    ins.then_inc(dsems[e][o.sem], 16)
                elif o.inc:
                    s, v = tok(o)
                    ins.then_inc(s, 1)
            if e == "sp":
                for d in final_waits:
                    s, v = tok(d)
                    eo.wait_ge(s, v)

        with nc.Block() as block:
            for e in self.ENGS:
                dec = getattr(block, engobj[e])
                dec(lambda eo, e=e: body(e, eo))


class Arena:
    def __init__(self, nc, nbytes):
        self.n = nbytes // 2
        self.t = nc.alloc_sbuf_tensor("arena", [128, self.n], BF16)
        self.off = 0

    def reset(self):
        self.off = 0

    def alloc(self, shape, dt):
        n = int(np.prod(shape[1:]))
        if dt == F32:
            if self.off % 2:
                self.off += 1
            nb = 2 * n
        else:
            nb = n
        assert self.off + nb <= self.n, ("arena overflow", self.off, nb, self.n)
        ap = self.t[0:shape[0], self.off:self.off + nb]
        self.off += nb
        if dt == F32:
            ap = ap.bitcast(F32)
        if len(shape) == 3:
            ap = ap.rearrange("p (a b) -> p a b", a=shape[1])
        elif len(shape) == 4:
            ap = ap.rearrange("p (a b c) -> p a b c", a=shape[1], b=shape[2])
        return ap


def build_program():
    nc = bass.Bass("TRN2", target_bir_lowering=False)
    mk = MK(nc)

    def din(name, shape, dt=F32):
        return nc.dram_tensor(name, list(shape), dt, kind="ExternalInput").ap()

    def dscr(name, shape, dt):
        return nc.dram_tensor(name, list(shape), dt, kind="Internal").ap()

    x_in = din("x", [T, D])
    w_ffn_in = din("w_ffn_in", [DEPTH, D, 2 * DFF])
    w_ffn_out = din("w_ffn_out", [DEPTH, DFF, D])
    w_in = din("w_in", [DEPTH, D, IN_W])
    w_br_attn = din("w_br_attn", [DEPTH, 512, D])
    w_br_gla = din("w_br_gla", [DEPTH, 512, D])
    w_br_ssd = din("w_br_ssd", [DEPTH, 1024, D])
    w_out = din("w_out", [DEPTH, D, D])
    gla_w2 = [din("gla_w2_f", [DEPTH, 16, 256]), din("gla_w2_b", [DEPTH, 16, 256])]
    gla_b = [din("gla_b_f", [DEPTH, 256]), din("gla_b_b", [DEPTH, 256])]
    gcst = din("gcst", [128, NGC])
    scst = din("scst", [128, NSC])
    srow = din("srow", [DEPTH, 128, NSR])
    ropec = din("ropec", [128, T])
    ropes = din("ropes", [128, T])
    pvec = din("pvec", [128, NPV])
    cst = din("cst", [128, NCST])
    y_out = nc.dram_tensor("y", [T, D], F32, kind="ExternalOutput").ap()
    dbg_out = nc.dram_tensor("dbg", [128, 512], F32, kind="ExternalOutput").ap()

    xT = dscr("xT_scr", [8, 128, T], F32)
    qS = dscr("qS_scr", [8, 64, T], BF16)
    kS = dscr("kS_scr", [2, 64, T], BF16)
    vS = dscr("vS_scr", [T, 128], BF16)
    oS = dscr("oS_scr", [8, 64, T], BF16)
    xbcS = dscr("xbcS_scr", [16, 128, T], BF16)
    zS = dscr("zS_scr", [T, 1024], BF16)
    dtS = dscr("dtS_scr", [T, 32], F32)
    xcS = dscr("xcS_scr", [8, 128, T], BF16)
    xtokS = dscr("xtokS_scr", [T, 1024], BF16)
    btokS = dscr("btokS_scr", [T, 512], BF16)
    yaccS = dscr("yaccS_scr", [T, 1024], F32)
    oSS = dscr("oSS_scr", [8, 128, T], BF16)
    gqS = dscr("gqS_scr", [4, 64, T], BF16)
    gkS = dscr("gkS_scr", [4, 64, T], BF16)
    gktS = dscr("gktS_scr", [T, 256], BF16)
    gvtS = dscr("gvtS_scr", [T, 512], BF16)
    gogS = dscr("gogS_scr", [4, 128, T], BF16)
    lrS = dscr("lrS_scr", [32, T], F32)
    goacc = dscr("goacc_scr", [4, 128, T], F32)
    oG = dscr("oG_scr", [4, 128, T], BF16)

    def sb(name, shape, dt):
        return nc.alloc_sbuf_tensor(name, list(shape), dt)

    pv = sb("pv", [128, NPV], F32)
    cs = sb("cs", [128, NCST], F32)
    ident = cs[:, 0:128]
    ones_bf = sb("ones_bf", [128, 128], BF16)
    perm_bf = sb("perm_bf", [128, 128], BF16)
    b64_bf = sb("b64_bf", [128, 128], BF16)
    ident_bf = sb("ident_bf", [128, 128], BF16)
    bscr = sb("bscr", [128, 4], F32)
    mk.op("sp", lambda e: e.dma_start(out=pv[:], in_=pvec), w=["pv"], dma=True)
    mk.op("sp", lambda e: e.dma_start(out=cs[:], in_=cst), w=["cs"], dma=True)
    mk.op("dve", lambda e: e.memset(ones_bf[:], 1.0), w=["ones_bf"])
    mk.op("dve", lambda e: e.tensor_copy(out=perm_bf[:], in_=cs[:, 136:264]), r=["cs"], w=["perm_bf"])
    mk.op("dve", lambda e: e.tensor_copy(out=b64_bf[:], in_=cs[:, 264:392]), r=["cs"], w=["b64_bf"])
    mk.op("dve", lambda e: e.tensor_copy(out=ident_bf[:], in_=cs[:, 0:128]), r=["cs"], w=["ident_bf"])
    psw = [nc.alloc_psum_tensor(f"psw{i}", [128, 1024], F32) for i in range(4)]
    psb = [psw[i // 2][:, (i % 2) * 512:(i % 2 + 1) * 512] for i in range(8)]
    A = Arena(nc, nc.sbuf_bytes_remaining - 256)

    def phase():
        mk.barrier(lambda e: e.memset(bscr[:, 0:1], 0.0))
        A.reset()

    PIECE = 1024
    cast_rr = [0]

    def load_weight(dst, src, kc, ncols, stg, scale_col0=None, tag=""):
        srcv = src.rearrange("(c p) n -> p c n", p=128)
        for c in range(kc):
            for n0 in range(0, ncols, PIECE):
                n1 = min(ncols, n0 + PIECE)
                i = cast_rr[0] % len(stg)
                cast_rr[0] += 1
                st = stg[i]
                mk.op("sp", lambda e, st=st, c=c, n0=n0, n1=n1: e.dma_start(out=st[:, 0:n1 - n0], in_=srcv[:, c, n0:n1]),
                      w=[("stg", i)], dma=True)
                eng = ("act", "dve")[cast_rr[0] % 2]
                if scale_col0 is None:
                    if eng == "act":
                        fn = lambda e, st=st, c=c, n0=n0, n1=n1: e.copy(out=dst[:, c, n0:n1], in_=st[:, 0:n1 - n0])
                    else:
                        fn = lambda e, st=st, c=c, n0=n0, n1=n1: e.tensor_copy(out=dst[:, c, n0:n1], in_=st[:, 0:n1 - n0])
                    rk = [("stg", i)]
                else:
                    sc = scale_col0 + c
                    if eng == "act":
                        fn = lambda e, st=st, c=c, n0=n0, n1=n1, sc=sc: e.activation(
                            out=dst[:, c, n0:n1], in_=st[:, 0:n1 - n0], func=AF.Copy, scale=pv[:, sc:sc + 1])
                    else:
                        fn = lambda e, st=st, c=c, n0=n0, n1=n1, sc=sc: e.tensor_scalar(
                            out=dst[:, c, n0:n1], in0=st[:, 0:n1 - n0], scalar1=pv[:, sc:sc + 1], scalar2=None, op0=ALU.mult)
                    rk = [("stg", i), "pv"]
                mk.op(eng, fn, r=rk, w=[("W", tag, c, n0 // PIECE)])

    def evac(eng, out, in_, r, w):
        if eng == "act":
            mk.op("act", lambda e: e.copy(out=out, in_=in_), r=r, w=w)
        else:
            mk.op(eng, lambda e: e.tensor_copy(out=out, in_=in_), r=r, w=w)

    xin_t = [A.alloc([128, D], F32) for i in range(4)]
    xtt = [A.alloc([128, 8, 128], F32) for i in range(4)]
    dbgt = A.alloc([128, 512], F32)
    mk.op("dve", lambda e: e.memset(dbgt, 0.0), w=["dbgt"])
    mk.op("pool", lambda e: e.tensor_copy(out=dbgt[0:64, 0:128], in_=cs[64:128, 0:128]), r=["cs", "dbgt"], w=["dbgt1"])
    mk.op("dve", lambda e: e.tensor_copy(out=dbgt[0:64, 128:256], in_=cs[64:128, 0:128]), r=["cs", "dbgt"], w=["dbgt2"])
    mk.op("act", lambda e: e.copy(out=dbgt[64:128, 256:384], in_=cs[0:64, 0:128]), r=["cs", "dbgt"], w=["dbgt3"])
    dbg_o = mk.op("sp", lambda e: e.dma_start(out=dbg_out, in_=dbgt), r=["dbgt1", "dbgt2", "dbgt3"], w=["dbgo"], dma=True)

    NT0 = T // 128

    def p0_load(t):
        i = t % 4
        mk.op("sp", lambda e: e.dma_start(out=xin_t[i], in_=x_in[t * 128:(t + 1) * 128, :]),
              w=[("xin", i)], dma=True)

    def p0_tile(t):
        i = t % 4
        for half in range(2):
            pb = (2 * t + half) % 8
            for c4 in range(4):
                c = half * 4 + c4
                mk.op("pe", lambda e, c=c, c4=c4, pb=pb: e.transpose(
                    out=psb[pb][:, c4 * 128:(c4 + 1) * 128], in_=xin_t[i][:, c * 128:(c + 1) * 128], identity=ident),
                    r=[("xin", i), "cs"], w=[("ps", pb)])
            evac(("act", "dve")[half], xtt[i][:, half * 4:(half + 1) * 4, :], psb[pb][:].rearrange("p (c t) -> p c t", c=4),
                 r=[("ps", pb)], w=[("xtt", i, half)])
        mk.op("sp", lambda e: e.dma_start(
            out=xT[:, :, t * 128:(t + 1) * 128].rearrange("c p t -> p c t"), in_=xtt[i]),
            r=[("xtt", i, 0), ("xtt", i, 1)], w=[("xT", t // 4)], dma=True)

    for t in range(-3, NT0):
        if t + 3 < NT0:
            p0_load(t + 3)
        if t >= 0:
            p0_tile(t)

    def load_xblock(xb, b, par=0):
        mk.op("sp", lambda e, b=b: e.dma_start(out=xb, in_=xT[:, :, b * TB:(b + 1) * TB].rearrange("c p t -> p c t")),
              r=[("xT", b)], w=[("xb", par, c) for c in range(8)], dma=True)

    def rms_block(xb, hb, sq, rstd, pbank, par=0):
        for c in range(8):
            mk.op("act", lambda e, c=c, s=sq[c % 2]: e.activation(out=s, in_=xb[:, c, :], func=AF.Square),
                  r=[("xb", par, c)], w=[("sq", c % 2)])
            mk.op("pe", lambda e, c=c, s=sq[c % 2]: e.matmul(psb[pbank][:], lhsT=ones_bf[:], rhs=s, start=(c == 0), stop=(c == 7)),
                  r=[("sq", c % 2), "ones_bf"], w=[("ps", pbank)])
        mk.op("act", lambda e: e.activation(out=rstd, in_=psb[pbank][:], func=AF.Sqrt, scale=1.0 / D, bias=cs[:, 128:129]),
              r=[("ps", pbank), "cs"], w=[("rstd0", par)])
        mk.op("dve", lambda e: e.reciprocal(out=rstd, in_=rstd), r=[("rstd0", par)], w=[("rstd", par)])
        if hb is not None:
            for c in range(8):
                mk.op("dve", lambda e, c=c: e.tensor_tensor(out=hb[:, c, :], in0=xb[:, c, :], in1=rstd, op=ALU.mult),
                      r=[("xb", par, c), ("rstd", par)], w=[("hb", par, c)])

    C_AQ, C_AK, C_AV, C_G = 4672, 5184, 5312, 5440

    def attn_inproj_setup(l, stg):
        watt = A.alloc([128, 8, 768], BF16)
        qraw = [A.alloc([128, TB], F32) for i in range(2)]
        sqq = [A.alloc([128, TB], BF16) for i in range(2)]
        rq = [A.alloc([128, TB], F32) for i in range(2)]
        qn = [A.alloc([128, TB], BF16) for i in range(2)]
        cosb = A.alloc([128, TB], F32)
        sinb = A.alloc([128, TB], F32)
        t1 = [A.alloc([128, TB], F32) for i in range(2)]
        t2 = [A.alloc([128, TB], F32) for i in range(2)]
        qo = [A.alloc([128, TB], BF16) for i in range(2)]
        vt = [A.alloc([128, 4, 128], BF16) for i in range(2)]
        load_weight(watt, w_in[l][:, C_AQ:C_AQ + 768], 8, 768, stg, scale_col0=PV_NORM1 + 8 * l, tag="watt")
        itc = [0]

        def body(b, hb, par):
            mk.op("sp", lambda e, b=b: e.dma_start(out=cosb, in_=ropec[:, b * TB:(b + 1) * TB]), w=["cosb"], dma=True)
            mk.op("sp", lambda e, b=b: e.dma_start(out=sinb, in_=ropes[:, b * TB:(b + 1) * TB]), w=["sinb"], dma=True)
            for ch in range(5):
                i = itc[0] % 2
                itc[0] += 1
                pa, pS, pR = 1 + i, 3 + i, 5 + i
                for c in range(8):
                    mk.op("pe", lambda e, c=c, ch=ch, pa=pa: e.matmul(psb[pa][:], lhsT=watt[:, c, ch * 128:(ch + 1) * 128],
                                                                  rhs=hb[:, c, :], start=(c == 0), stop=(c == 7)),
                          r=[("hb", par, c), ("W", "watt", c, 0)], w=[("ps", pa)])
                mk.op("act", lambda e, i=i, pa=pa: e.copy(out=qraw[i], in_=psb[pa][:]), r=[("ps", pa)], w=[("qraw", i)])
                mk.op("act", lambda e, i=i: e.activation(out=sqq[i], in_=qraw[i], func=AF.Square),
                      r=[("qraw", i)], w=[("sqq", i)])
                mk.op("pe", lambda e, i=i, pS=pS: e.matmul(psb[pS][:], lhsT=b64_bf[:], rhs=sqq[i], start=True, stop=True),
                      r=[("sqq", i), "b64_bf"], w=[("ps", pS)])
                if ch < 4:
                    mk.op("act", lambda e, i=i, pS=pS: e.activation(out=rq[i], in_=psb[pS][:], func=AF.Sqrt, scale=1.0, bias=cs[:, 129:130]),
                          r=[("ps", pS), "cs"], w=[("rq0", i)])
                    wc = PV_QW + l
                else:
                    mk.op("act", lambda e, i=i, pS=pS: e.activation(out=rq[i], in_=psb[pS][:], func=AF.Sqrt, scale=1.0 / 64, bias=cs[:, 128:129]),
                          r=[("ps", pS), "cs"], w=[("rq0", i)])
                    wc = PV_KW + l
                mk.op("dve", lambda e, i=i: e.reciprocal(out=rq[i], in_=rq[i]), r=[("rq0", i)], w=[("rq", i)])
                mk.op("dve", lambda e, i=i, wc=wc: e.scalar_tensor_tensor(out=qn[i], in0=qraw[i], scalar=pv[:, wc:wc + 1], in1=rq[i],
                                                                     op0=ALU.mult, op1=ALU.mult),
                      r=[("qraw", i), ("rq", i), "pv"], w=[("qn", i)])
                mk.op("pe", lambda e, i=i, pR=pR: e.matmul(psb[pR][:], lhsT=perm_bf[:], rhs=qn[i], start=True, stop=True),
                      r=[("qn", i), "perm_bf"], w=[("ps", pR)])
                mk.op("dve", lambda e, i=i: e.tensor_tensor(out=t1[i], in0=qn[i], in1=cosb, op=ALU.mult),
                      r=[("qn", i), "cosb"], w=[("t1", i)])
                mk.op("dve", lambda e, i=i, pR=pR: e.tensor_tensor(out=t2[i], in0=psb[pR][:], in1=sinb, op=ALU.mult),
                      r=[("ps", pR), "sinb"], w=[("t2", i)])
                mk.op("dve", lambda e, i=i: e.tensor_tensor(out=qo[i], in0=t1[i], in1=t2[i], op=ALU.add),
                      r=[("t1", i), ("t2", i)], w=[("qo", i)])
                if ch < 4:
                    dst = qS[2 * ch:2 * ch + 2, :, b * TB:(b + 1) * TB].rearrange("h d t -> (h d) t")
                else:
                    dst = kS[:, :, b * TB:(b + 1) * TB].rearrange("h d t -> (h d) t")
                mk.op("sp", lambda e, i=i, dst=dst: e.dma_start(out=dst, in_=qo[i]), r=[("qo", i)], w=[("qkS", b, ch)], dma=True)
            vi = b % 2
            for tt in range(4):
                for c in range(8):
                    mk.op("pe", lambda e, c=c, tt=tt: e.matmul(psb[7][:, tt * 128:(tt + 1) * 128], lhsT=hb[:, c, tt * 128:(tt + 1) * 128],
                                                             rhs=watt[:, c, 640:768], start=(c == 0), stop=(c == 7)),
                          r=[("hb", par, c), ("W", "watt", c, 0)], w=[("ps", 7)])
            mk.op("act", lambda e, vi=vi: e.copy(out=vt[vi], in_=psb[7][:].rearrange("p (a b) -> p a b", a=4)),
                  r=[("ps", 7)], w=[("vt", vi)])
            mk.op("sp", lambda e, vi=vi, b=b: e.dma_start(out=vS[b * TB:(b + 1) * TB, :].rearrange("(a p) n -> p a n", p=128), in_=vt[vi]),
                  r=[("vt", vi)], w=[("vS", b)], dma=True)
        return body

    def inproj_phase(l):
        phase()
        stg = [A.alloc([128, PIECE], F32) for i in range(2)]
        xb2 = [A.alloc([128, 8, TB], F32) for i in range(2)]
        hb2 = [A.alloc([128, 8, TB], BF16) for i in range(2)]
        sq = [A.alloc([128, TB], BF16) for i in range(2)]
        rstd2 = [A.alloc([128, TB], F32) for i in range(2)]
        bodies = [attn_inproj_setup(l, stg)]
        if STAGE >= 2:
            bodies.append(gla_inproj_setup(l, stg))
        if STAGE >= 3:
            bodies.append(ssd_inproj_setup(l, stg))
        load_xblock(xb2[0], 0, 0)
        rms_block(xb2[0], hb2[0], sq, rstd2[0], 0, 0)
        for b in range(NB):
            par = b % 2
            if b + 1 < NB:
                load_xblock(xb2[1 - par], b + 1, 1 - par)
            for k, body in enumerate(bodies):
                if k == len(bodies) - 1 and b + 1 < NB:
                    rms_block(xb2[1 - par], hb2[1 - par], sq, rstd2[1 - par], 0, 1 - par)
                body(b, hb2[par], par)

    def attention_phase(l):
        phase()
        NKT = 32
        kt_sb = A.alloc([128, NKT * 128], BF16)
        v1 = A.alloc([128, NKT, 2, 128], BF16)
        v2 = A.alloc([128, NKT, 2, 128], BF16)
        qa = [A.alloc([128, 4, TB], BF16) for i in range(2)]
        pt = [A.alloc([128, 2 * TB], BF16) for i in range(3)]
        rs = [A.alloc([128, TB], F32) for i in range(2)]
        r0 = [A.alloc([128, TB], F32) for i in range(2)]
        ob = [A.alloc([128, TB], BF16) for i in range(2)]
        SB = [1, 2, 3, 4]
        OB = [5, 6]
        cnt = {"s": 0, "o": 0, "p": 0, "q": 0, "r": 0}
        for grp, (tok0, ntok) in enumerate([(0, 2 * SEGL), (2 * SEGL, SEGL)]):
            nkt = ntok // 128
            gk = ("grp", l, grp)
            mk.op("sp", lambda e, tok0=tok0, ntok=ntok: e.dma_start(
                out=kt_sb[:, 0:ntok], in_=kS[:, :, tok0:tok0 + ntok].rearrange("h d t -> (h d) t")),
                r=[("qkS", b, 4) for b in range(tok0 // TB, (tok0 + ntok) // TB)], w=["kt_sb"], dma=True)
            mk.op("pool", lambda e, nkt=nkt: e.memset(v1[:, 0:nkt, :, 64:128], 1.0), w=["v1ones"])
            for kv in range(2):
                mk.op("sp", lambda e, kv=kv, tok0=tok0, ntok=ntok, nkt=nkt: e.dma_start(
                    out=v1[:, 0:nkt, kv, 0:64],
                    in_=vS[tok0:tok0 + ntok, kv * 64:(kv + 1) * 64].rearrange("(a p) n -> p a n", p=128)),
                    r=[("vS", b) for b in range(tok0 // TB, (tok0 + ntok) // TB)], w=[("v1", kv)], dma=True)
            if grp == 0:
                mk.op("dve", lambda e, nkt=nkt: e.tensor_scalar(out=v2[:, 0:nkt].rearrange("p a b c -> p (a b c)"),
                                                               in0=v1[:, 0:nkt].rearrange("p a b c -> p (a b c)"),
                                                               scalar1=pv[:, PV_LINK:PV_LINK + 1], scalar2=None, op0=ALU.mult),
                      r=[("v1", 0), ("v1", 1), "v1ones", "pv"], w=["v2"])
            for qb in range(ntok // TB):
                b = tok0 // TB + qb
                qi = cnt["q"] % 2
                cnt["q"] += 1
                for half in range(2):
                    mk.op("sp", lambda e, half=half, b=b, qi=qi: e.dma_start(
                        out=qa[qi][half * 64:(half + 1) * 64, :, :],
                        in_=qS[half * 4:half * 4 + 4, :, b * TB:(b + 1) * TB].rearrange("h d t -> d h t")),
                        r=[("qkS", b, ch) for ch in range(4)], w=[("qa", qi, half)], dma=True)
                qseg = (tok0 + qb * TB) // SEGL
                for j in range(4):
                    heads = (j, 4 + j)
                    oi = cnt["o"] % 2
                    cnt["o"] += 1
                    pos = (4 + 2 * oi, 5 + 2 * oi)
                    pend = []

                    def issue_s(kt, j=j):
                        si = cnt["s"] % 2
                        cnt["s"] += 1
                        for kv in range(2):
                            mk.op("pe", lambda e, kt=kt, si=si, kv=kv, j=j, qi=qi: e.matmul(
                                psb[2 * si + kv][:], lhsT=kt_sb[kv * 64:(kv + 1) * 64, kt * 128:(kt + 1) * 128],
                                rhs=qa[qi][kv * 64:(kv + 1) * 64, j, :], start=True, stop=True),
                                r=["kt_sb", ("qa", qi, kv)], w=[("ps", 2 * si + kv)])
                        pi = cnt["p"] % 3
                        cnt["p"] += 1
                        mk.op("act", lambda e, si=si, pi=pi: e.activation(out=pt[pi], in_=psw[si][:], func=AF.Exp),
                              r=[("ps", 2 * si), ("ps", 2 * si + 1)], w=[("pt", pi)])
                        return pi

                    def issue_o(kt, pi, pos=pos):
                        kseg = (tok0 + kt * 128) // SEGL
                        vv = v1 if kseg == qseg else v2
                        for kv in range(2):
                            vk = [("v1", kv), "v1ones"] if kseg == qseg else ["v2"]
                            mk.op("pe", lambda e, kt=kt, pi=pi, vv=vv, kv=kv, po=pos[kv]: e.matmul(
                                psb[po][:], lhsT=vv[:, kt, kv, :], rhs=pt[pi][:, kv * 512:(kv + 1) * 512], start=(kt == 0), stop=(kt == nkt - 1)),
                                r=[("pt", pi)] + vk, w=[("ps", pos[kv])])

                    LA = 1
                    for kt in range(nkt):
                        pend.append((kt, issue_s(kt)))
                        if len(pend) > LA:
                            issue_o(*pend.pop(0))
                    while pend:
                        issue_o(*pend.pop(0))
                    for kv in range(2):
                        h = heads[kv]
                        po = pos[kv]
                        ri = cnt["r"] % 2
                        cnt["r"] += 1
                        mk.op("dve", lambda e, ri=ri, po=po: e.reciprocal(out=rs[ri][64:128, :], in_=psb[po][64:128, :]),
                              r=[("ps", po)], w=[("rs", ri)])
                        mk.op("pool", lambda e, ri=ri: e.tensor_copy(out=r0[ri][0:64, :], in_=rs[ri][64:128, :]),
                              r=[("rs", ri)], w=[("r0", ri)])
                        mk.op("dve", lambda e, ri=ri, po=po: e.tensor_tensor(out=ob[ri][0:64, :], in0=psb[po][0:64, :], in1=r0[ri][0:64, :], op=ALU.mult),
                              r=[("ps", po), ("r0", ri)], w=[("ob", ri)])
                        mk.op("sp", lambda e, ri=ri, h=h, b=b: e.dma_start(out=oS[h, :, b * TB:(b + 1) * TB], in_=ob[ri][0:64, :]),
                              r=[("ob", ri)], w=[("oS", b, h)], dma=True)

    C_GQ, C_GK, C_GV, C_GOG, C_GLR = 3104, 3360, 3616, 4128, 4640

    def gla_inproj_setup(l, stg):
        NW = 1568
        wg_ = A.alloc([128, 8, NW], BF16)
        ob_ = [A.alloc([128, TB], BF16) for i in range(2)]
        lrt = A.alloc([32, TB], F32)
        kt_ = [A.alloc([128, 256], BF16) for i in range(2)]
        vt_ = [A.alloc([128, 512], BF16) for i in range(2)]
        for c in range(8):
            for (n0, n1) in ((0, 784), (784, NW)):
                pass
        load_weight(wg_[:, :, 0:1024], w_in[l][:, C_GQ:C_GQ + 1024], 8, 1024, stg, scale_col0=PV_NORM1 + 8 * l, tag="wgl0")
        load_weight(wg_[:, :, 1024:NW], w_in[l][:, C_GQ + 1024:C_GQ + NW], 8, NW - 1024, stg, scale_col0=PV_NORM1 + 8 * l, tag="wgl1")

        def wkey(c, col):
            return ("W", "wgl0" if col < 1024 else "wgl1", c, 0)
        itc = [0]

        def body(b, hb, par):
            specs = [(0, gqS, 0), (128, gqS, 1), (256, gkS, 0), (384, gkS, 1)] + [(1024 + 128 * j, gogS, j) for j in range(4)]
            for (col, dstS, j) in specs:
                i = itc[0] % 2
                itc[0] += 1
                pa = 1 + i
                for c in range(8):
                    mk.op("pe", lambda e, c=c, col=col, pa=pa: e.matmul(psb[pa][:], lhsT=wg_[:, c, col:col + 128], rhs=hb[:, c, :],
                                                                    start=(c == 0), stop=(c == 7)),
                          r=[("hb", par, c), wkey(c, col)], w=[("ps", pa)])
                evac(("act", "dve")[i], ob_[i], psb[pa][:], r=[("ps", pa)], w=[("ob_", i)])
                if dstS is gogS:
                    dst = gogS[j, :, b * TB:(b + 1) * TB]
                else:
                    dst = dstS[2 * j:2 * j + 2, :, b * TB:(b + 1) * TB].rearrange("h d t -> (h d) t")
                mk.op("sp", lambda e, i=i, dst=dst: e.dma_start(out=dst, in_=ob_[i]), r=[("ob_", i)], w=[("gfm", b, col)], dma=True)
            for c in range(8):
                mk.op("pe", lambda e, c=c: e.matmul(psb[3][0:32, :], lhsT=wg_[:, c, 1536:1568], rhs=hb[:, c, :], start=(c == 0), stop=(c == 7)),
                      r=[("hb", par, c), wkey(c, 1536)], w=[("ps", 3)])
            evac("act", lrt, psb[3][0:32, :], r=[("ps", 3)], w=["lrt"])
            mk.op("sp", lambda e, b=b: e.dma_start(out=lrS[:, b * TB:(b + 1) * TB], in_=lrt), r=["lrt"], w=[("lrS", b)], dma=True)
            for tt in range(4):
                i = tt % 2
                for c in range(8):
                    mk.op("pe", lambda e, c=c, tt=tt, i=i: e.matmul(psb[4 + i][:, 0:256], lhsT=hb[:, c, tt * 128:(tt + 1) * 128],
                                                                 rhs=wg_[:, c, 256:512], start=(c == 0), stop=(c == 7)),
                          r=[("hb", par, c), wkey(c, 256)], w=[("ps", 4 + i)])
                for c in range(8):
                    mk.op("pe", lambda e, c=c, tt=tt, i=i: e.matmul(psb[6 + i][:], lhsT=hb[:, c, tt * 128:(tt + 1) * 128],
                                                                 rhs=wg_[:, c, 512:1024], start=(c == 0), stop=(c == 7)),
                          r=[("hb", par, c), wkey(c, 512)], w=[("ps", 6 + i)])
                evac("act", kt_[i], psb[4 + i][:, 0:256], r=[("ps", 4 + i)], w=[("kt_", i)])
                evac("dve", vt_[i], psb[6 + i][:], r=[("ps", 6 + i)], w=[("vt_", i)])
                t0 = b * TB + tt * 128
                mk.op("sp", lambda e, i=i, t0=t0: e.dma_start(out=gktS[t0:t0 + 128, :], in_=kt_[i]), r=[("kt_", i)], w=[("gtm", b, tt, 0)], dma=True)
                mk.op("sp", lambda e, i=i, t0=t0: e.dma_start(out=gvtS[t0:t0 + 128, :], in_=vt_[i]), r=[("vt_", i)], w=[("gtm", b, tt, 1)], dma=True)
        return body

    def gla_pass(l, dr):
        phase()
        gc = A.alloc([128, NGC], F32)
        mk.op("sp", lambda e: e.dma_start(out=gc, in_=gcst), w=["gc"], dma=True)
        o0 = dr * 896
        M12 = gc[:, o0:o0 + 256]
        M3 = gc[:, o0 + 256:o0 + 384]
        mask4f = gc[:, o0 + 384:o0 + 896]
        w2a = A.alloc([17, 256], F32)
        mk.op("sp", lambda e: e.dma_start(out=w2a[0:16, :], in_=gla_w2[dr][l]), w=["w2a0"], dma=True)
        mk.op("sp", lambda e: e.dma_start(out=w2a[16:17, :], in_=gla_b[dr][l:l + 1, :]), w=["w2a1"], dma=True)
        ones_f = A.alloc([128, 128], F32)
        mk.op("dve", lambda e: e.memset(ones_f, 1.0), w=["ones_f"])
        lrt = [A.alloc([17, 128], F32) for i in range(2)]
        for i in range(2):
            mk.op("dve", lambda e, i=i: e.memset(lrt[i], 1.0), w=[("lrt", i)])
        q_sb = [A.alloc([64, 4, 128], BF16) for i in range(2)]
        k_sb = [A.alloc([64, 4, 128], BF16) for i in range(2)]
        kt_sb = [A.alloc([128, 256], BF16) for i in range(2)]
        vt_sb = [A.alloc([128, 512], BF16) for i in range(2)]
        e_ = A.alloc([128, 256], F32)
        sp_ = A.alloc([128, 256], F32)
        E1 = A.alloc([64, 4, 128], F32)
        E1i = A.alloc([64, 4, 128], F32)
        EB2 = [A.alloc([64, 4, 128], F32) for _ in range(2)]
        Ed = A.alloc([128, 256], F32)
        qe = A.alloc([64, 4, 128], BF16)
        ke = A.alloc([64, 4, 128], BF16)
        qb2 = [A.alloc([64, 4, 128], BF16) for _ in range(2)]
        kd2 = [A.alloc([128, 256], BF16) for _ in range(2)]
        attm2 = [A.alloc([128, 4, 128], BF16) for _ in range(2)]
        S = A.alloc([64, 4, 128], F32)
        Sb = [A.alloc([64, 4, 128], BF16) for i in range(2)]
        osb = A.alloc([128, 512], F32)
        if dr == 1:
            oprev2 = [A.alloc([128, 4, 128], F32) for _ in range(2)]
            gog2 = [A.alloc([128, 4, 128], BF16) for _ in range(2)]
            sqo = A.alloc([128, 512], BF16)
            rr = A.alloc([128, 512], F32)
            sg_ = A.alloc([128, 512], F32)
            oo = A.alloc([128, 512], BF16)
        NT = T // 128
        order = list(range(NT)) if dr == 0 else list(range(NT - 1, -1, -1))
        tiles_per_seg = SEGL // 128
        def tileA(n, t):
            i = n % 2
            t0 = t * 128
            seg = t // tiles_per_seg
            mk.op("sp", lambda e, i=i, t0=t0: e.dma_start(out=lrt[i][0:16, :], in_=lrS[dr * 16:(dr + 1) * 16, t0:t0 + 128]),
                  r=[("lrS", t0 // TB)], w=[("lrt", i)], dma=True)
            mk.op("sp", lambda e, i=i, t0=t0: e.dma_start(out=q_sb[i], in_=gqS[:, :, t0:t0 + 128].rearrange("h d t -> d h t")),
                  r=[("gfm", t0 // TB, 0), ("gfm", t0 // TB, 128)], w=[("q_sb", i)], dma=True)
            mk.op("sp", lambda e, i=i, t0=t0: e.dma_start(out=k_sb[i], in_=gkS[:, :, t0:t0 + 128].rearrange("h d t -> d h t")),
                  r=[("gfm", t0 // TB, 256), ("gfm", t0 // TB, 384)], w=[("k_sb", i)], dma=True)
            mk.op("sp", lambda e, i=i, t0=t0: e.dma_start(out=kt_sb[i], in_=gktS[t0:t0 + 128, :]),
                  r=[("gtm", t0 // TB, (t0 % TB) // 128, 0)], w=[("kt_sb", i)], dma=True)
            mk.op("sp", lambda e, i=i, t0=t0: e.dma_start(out=vt_sb[i], in_=gvtS[t0:t0 + 128, :]),
                  r=[("gtm", t0 // TB, (t0 % TB) // 128, 1)], w=[("vt_sb", i)], dma=True)
            if dr == 1:
                mk.op("sp", lambda e, t0=t0: e.dma_start(out=oprev2[i], in_=goacc[:, :, t0:t0 + 128].rearrange("h v t -> v h t")),
                      r=[("goacc", t0 // 128)], w=[("oprev", i)], dma=True)
                mk.op("sp", lambda e, t0=t0: e.dma_start(out=gog2[i], in_=gogS[:, :, t0:t0 + 128].rearrange("h v t -> v h t")),
                      r=[("gfm", t0 // TB, 1024 + 128 * j) for j in range(4)], w=[("gog", i)], dma=True)
            mk.op("pe", lambda e, i=i: e.matmul(psb[0][:, 0:256], lhsT=lrt[i], rhs=w2a, start=True, stop=True),
                  r=[("lrt", i), "w2a0", "w2a1"], w=[("ps", 0)])
            mk.op("act", lambda e: e.activation(out=e_, in_=psb[0][:, 0:256], func=AF.Exp, scale=-1.0), r=[("ps", 0)], w=["e_"])
            mk.op("act", lambda e: e.activation(out=sp_, in_=e_, func=AF.Ln, bias=cs[:, 130:131], scale=1.0), r=["e_", "cs"], w=["sp_"])
            yield
            for h in range(4):
                pbk = 1 + h // 2
                mk.op("pe", lambda e, h=h, pbk=pbk: e.matmul(psb[pbk][0:64, (h % 2) * 256:(h % 2 + 1) * 256], lhsT=sp_[:, h * 64:(h + 1) * 64],
                                                           rhs=M12, start=True, stop=True),
                      r=["sp_", "gc"], w=[("ps", pbk)])
            mk.op("pe", lambda e: e.matmul(psb[3][:, 0:256], lhsT=M3, rhs=sp_, start=True, stop=True), r=["sp_", "gc"], w=[("ps", 3)])
            yield
            for hp in range(2):
                v = psb[1 + hp][0:64, :].rearrange("p (h a l) -> p h a l", h=2, a=2)
                mk.op("act", lambda e, hp=hp, v=v: e.activation(out=E1[:, 2 * hp:2 * hp + 2, :], in_=v[:, :, 0, :], func=AF.Exp),
                      r=[("ps", 1 + hp)], w=[("E1", hp)])
                mk.op("act", lambda e, hp=hp, v=v: e.activation(out=E1i[:, 2 * hp:2 * hp + 2, :], in_=v[:, :, 0, :], func=AF.Exp, scale=-1.0),
                      r=[("ps", 1 + hp)], w=[("E1i", hp)])
                mk.op("act", lambda e, hp=hp, v=v: e.activation(out=EB2[i][:, 2 * hp:2 * hp + 2, :], in_=v[:, :, 1, :], func=AF.Exp),
                      r=[("ps", 1 + hp)], w=[("EB", i, hp)])
            mk.op("act", lambda e: e.activation(out=Ed, in_=psb[3][:, 0:256], func=AF.Exp), r=[("ps", 3)], w=["Ed"])
            yield
            ek = [("E1", 0), ("E1", 1)]
            mk.op("dve", lambda e, i=i: e.scalar_tensor_tensor(out=qe, in0=q_sb[i], scalar=0.125, in1=E1, op0=ALU.mult, op1=ALU.mult),
                  r=[("q_sb", i)] + ek, w=["qe"])
            mk.op("dve", lambda e, i=i: e.tensor_tensor(out=ke, in0=k_sb[i], in1=E1i, op=ALU.mult),
                  r=[("k_sb", i), ("E1i", 0), ("E1i", 1)], w=["ke"])
            mk.op("dve", lambda e, i=i: e.scalar_tensor_tensor(out=qb2[i], in0=q_sb[i], scalar=0.125, in1=EB2[i], op0=ALU.mult, op1=ALU.mult),
                  r=[("q_sb", i), ("EB", i, 0), ("EB", i, 1)], w=[("qb", i)])
            mk.op("dve", lambda e, i=i: e.tensor_tensor(out=kd2[i], in0=kt_sb[i], in1=Ed, op=ALU.mult), r=[("kt_sb", i), "Ed"], w=[("kd", i)])
            yield
            for h in range(4):
                mk.op("pe", lambda e, h=h: e.matmul(psb[4][:, h * 128:(h + 1) * 128], lhsT=ke[:, h, :], rhs=qe[:, h, :], start=True, stop=True),
                      r=["ke", "qe"], w=[("ps", 4)])
            mk.op("dve", lambda e: e.tensor_tensor(out=attm2[i].rearrange("p h l -> p (h l)"), in0=psb[4][:], in1=mask4f, op=ALU.mult),
                  r=[("ps", 4), "gc"], w=[("attm", i)])
            for ci, cc in enumerate((0, 1) if dr == 0 else (1, 0)):
                sbk = 5 if ci == 0 else 7
                for h in range(4):
                    mk.op("pe", lambda e, h=h, cc=cc, sbk=sbk: e.matmul(psb[sbk][0:64, h * 128:(h + 1) * 128],
                                                                    lhsT=kd2[i][cc * 64:(cc + 1) * 64, h * 64:(h + 1) * 64],
                                                                    rhs=vt_sb[i][cc * 64:(cc + 1) * 64, h * 128:(h + 1) * 128], start=True, stop=True),
                          r=[("kd", i), ("vt_sb", i)], w=[("ps", sbk)])
            yield

        def tileB(n, t):
            i = n % 2
            t0 = t * 128
            seg = t // tiles_per_seg
            first_in_seg = (t % tiles_per_seg == 0) if dr == 0 else (t % tiles_per_seg == tiles_per_seg - 1)
            if first_in_seg:
                linked = (seg == 1) if dr == 0 else (seg == 0)
                if linked:
                    mk.op("dve", lambda e: e.tensor_scalar(out=S, in0=S, scalar1=pv[0:64, PV_LINK:PV_LINK + 1], scalar2=None, op0=ALU.mult),
                          r=["S", "pv"], w=["S"])
                else:
                    mk.op("dve", lambda e: e.memset(S, 0.0), r=["S"], w=["S"])
            corder = (0, 1) if dr == 0 else (1, 0)
            mk.op("act", lambda e: e.copy(out=Sb[0], in_=S), r=["S"], w=[("Sb", 0)])
            yield
            for ci, cc in enumerate(corder):
                sbk = 5 if ci == 0 else 7
                col = cc * 64 + (63 if dr == 0 else 0)
                mk.op("dve", lambda e, col=col: e.tensor_tensor(out=S, in0=S, in1=EB2[i][:, :, col:col + 1].broadcast_to([64, 4, 128]), op=ALU.mult),
                      r=[("EB", i, 0), ("EB", i, 1), "S"] + ([("Sb", 0)] if ci == 0 else []), w=["S"])
                mk.op("dve", lambda e, sbk=sbk: e.tensor_tensor(out=S.rearrange("p h v -> p (h v)"), in0=psb[sbk][0:64, :], in1=S.rearrange("p h v -> p (h v)"), op=ALU.add),
                      r=[("ps", sbk), "S"], w=["S"])
                if ci == 0:
                    mk.op("act", lambda e: e.copy(out=Sb[1], in_=S), r=["S"], w=[("Sb", 1)])
                yield
            for h in range(4):
                mk.op("pe", lambda e, h=h, i=i: e.matmul(psb[6][:, h * 128:(h + 1) * 128], lhsT=vt_sb[i][:, h * 128:(h + 1) * 128], rhs=attm2[i][:, h, :],
                                                      start=True, stop=False),
                      r=[("vt_sb", i), ("attm", i)], w=[("ps", 6)])
                for ci, cc in enumerate(corder):
                    mk.op("pe", lambda e, h=h, ci=ci, cc=cc: e.matmul(psb[6][:, h * 128 + cc * 64:h * 128 + (cc + 1) * 64], lhsT=Sb[ci][:, h, :],
                                                                   rhs=qb2[i][:, h, cc * 64:(cc + 1) * 64], start=False, stop=(ci == 1),
                                                                   skip_group_check=True),
                          r=[("Sb", ci), ("qb", i)], w=[("ps", 6)])
            if dr == 0:
                mk.op("act", lambda e: e.copy(out=osb, in_=psb[6][:]), r=[("ps", 6)], w=["osb"])
                mk.op("sp", lambda e, t0=t0: e.dma_start(out=goacc[:, :, t0:t0 + 128].rearrange("h v t -> v h t"),
                                                       in_=osb.rearrange("p (h l) -> p h l", h=4)),
                      r=["osb"], w=[("goacc", t0 // 128)], dma=True)
            else:
                mk.op("dve", lambda e: e.tensor_tensor(out=osb, in0=psb[6][:], in1=oprev2[i].rearrange("p h l -> p (h l)"), op=ALU.add),
                      r=[("ps", 6), ("oprev", i)], w=["osb"])
                mk.op("act", lambda e: e.activation(out=sqo, in_=osb, func=AF.Square), r=["osb"], w=["sqo"])
                mk.op("pe", lambda e: e.matmul(psb[7][:], lhsT=ones_bf[:], rhs=sqo, start=True, stop=True), r=["sqo", "ones_bf"], w=[("ps", 7)])
                mk.op("act", lambda e: e.activation(out=rr, in_=psb[7][:], func=AF.Ln, scale=1.0 / 128, bias=cs[:, 128:129]),
                      r=[("ps", 7), "cs"], w=["rr0"])
                mk.op("act", lambda e: e.activation(out=rr, in_=rr, func=AF.Exp, scale=-0.5), r=["rr0"], w=["rr"])
                mk.op("act", lambda e: e.activation(out=sg_, in_=gog2[i].rearrange("p h l -> p (h l)"), func=AF.Exp, scale=-1.0), r=[("gog", i)], w=["sg0"])
                mk.op("dve", lambda e: e.tensor_scalar(out=sg_, in0=sg_, scalar1=1.0, scalar2=None, op0=ALU.add), r=["sg0"], w=["sg1"])
                mk.op("dve", lambda e: e.reciprocal(out=sg_, in_=sg_), r=["sg1"], w=["sg2"])
                mk.op("dve", lambda e: e.tensor_tensor(out=sg_, in0=sg_, in1=gog2[i].rearrange("p h l -> p (h l)"), op=ALU.mult), r=["sg2", ("gog", i)], w=["sg3"])
                mk.op("dve", lambda e: e.scalar_tensor_tensor(out=rr, in0=rr, scalar=pv[:, PV_GLAW + l:PV_GLAW + l + 1], in1=sg_,
                                                            op0=ALU.mult, op1=ALU.mult), r=["rr", "sg3", "pv"], w=["rr2"])
                mk.op("dve", lambda e: e.tensor_tensor(out=oo, in0=osb, in1=rr, op=ALU.mult), r=["osb", "rr2"], w=["oo"])
                mk.op("sp", lambda e, t0=t0: e.dma_start(out=oG[:, :, t0:t0 + 128].rearrange("h v t -> v h t"),
                                                       in_=oo.rearrange("p (h l) -> p h l", h=4)),
                      r=["oo"], w=[("oG", t0 // TB)], dma=True)
            yield

        def drive(*gens):
            gens = [x for x in gens if x is not None]
            while gens:
                for x in list(gens):
                    try:
                        next(x)
                    except StopIteration:
                        gens.remove(x)

        drive(tileA(0, order[0]))
        for n, t in enumerate(order):
            drive(tileA(n + 1, order[n + 1]) if n + 1 < NT else None, tileB(n, t))

    def ssd_inproj_setup(l, stg):
        NW = 3104
        wz = A.alloc([128, 8, NW], BF16)
        ob_ = [A.alloc([128, TB], BF16) for i in range(2)]
        zt = [A.alloc([128, 1024], BF16) for i in range(2)]
        dtt = [A.alloc([128, 32], F32) for i in range(2)]
        for k, n0 in enumerate(range(0, NW, 1024)):
            n1 = min(NW, n0 + 1024)
            load_weight(wz[:, :, n0:n1], w_in[l][:, n0:n1], 8, n1 - n0, stg, scale_col0=PV_NORM1 + 8 * l, tag="wz%d" % k)

        def wkey(c, col):
            return ("W", "wz%d" % (col // 1024), c, 0)
        itc = [0]

        def body(b, hb, par):
            for j in range(16):
                i = itc[0] % 2
                itc[0] += 1
                pa = 1 + i
                col = 1024 + 128 * j
                for c in range(8):
                    mk.op("pe", lambda e, c=c, col=col, pa=pa: e.matmul(psb[pa][:], lhsT=wz[:, c, col:col + 128], rhs=hb[:, c, :],
                                                                    start=(c == 0), stop=(c == 7)),
                          r=[("hb", par, c), wkey(c, col)], w=[("ps", pa)])
                evac(("act", "dve")[i], ob_[i], psb[pa][:], r=[("ps", pa)], w=[("ob_", i)])
                mk.op("sp", lambda e, i=i, j=j, b=b: e.dma_start(out=xbcS[j, :, b * TB:(b + 1) * TB], in_=ob_[i]),
                      r=[("ob_", i)], w=[("xbcS", b)], dma=True)
            for tt in range(4):
                i = tt % 2
                for hf in range(2):
                    for c in range(8):
                        mk.op("pe", lambda e, c=c, tt=tt, hf=hf, i=i: e.matmul(psb[3 + 2 * i + hf][:], lhsT=hb[:, c, tt * 128:(tt + 1) * 128],
                                                                            rhs=wz[:, c, hf * 512:(hf + 1) * 512], start=(c == 0), stop=(c == 7)),
                              r=[("hb", par, c), wkey(c, 0)], w=[("ps", 3 + 2 * i + hf)])
                    evac(("act", "dve")[hf], zt[i][:, hf * 512:(hf + 1) * 512], psb[3 + 2 * i + hf][:], r=[("ps", 3 + 2 * i + hf)], w=[("zt", i, hf)])
                for c in range(8):
                    mk.op("pe", lambda e, c=c, tt=tt: e.matmul(psb[7][:, 0:32], lhsT=hb[:, c, tt * 128:(tt + 1) * 128],
                                                             rhs=wz[:, c, 3072:3104], start=(c == 0), stop=(c == 7)),
                          r=[("hb", par, c), wkey(c, 3072)], w=[("ps", 7)])
                evac("act", dtt[i], psb[7][:, 0:32], r=[("ps", 7)], w=[("dtt", i)])
                t0 = b * TB + tt * 128
                mk.op("sp", lambda e, i=i, t0=t0: e.dma_start(out=zS[t0:t0 + 128, :], in_=zt[i]), r=[("zt", i, 0), ("zt", i, 1)], w=[("zS", t0 // 128)], dma=True)
                mk.op("sp", lambda e, i=i, t0=t0: e.dma_start(out=dtS[t0:t0 + 128, :], in_=dtt[i]), r=[("dtt", i)], w=[("dtS", t0 // 128)], dma=True)
        return body

    def ssd_pass(l, dr):
        phase()
        sc = A.alloc([128, NSC], F32)
        mk.op("sp", lambda e: e.dma_start(out=sc, in_=scst), w=["sc"], dma=True)
        sr = A.alloc([128, NSR], F32)
        mk.op("sp", lambda e: e.dma_start(out=sr, in_=srow[l]), w=["sr"], dma=True)
        U = sc[:, dr * 256:dr * 256 + 128]
        SL = sc[:, dr * 256 + 128:dr * 256 + 256]
        mask4 = sc[:, 512 + dr * 512:512 + (dr + 1) * 512]
        ones_f = A.alloc([128, 128], F32)
        mk.op("dve", lambda e: e.memset(ones_f, 1.0), w=["ones_f"])
        arow = A.alloc([128, 16], F32)
        mk.op("act", lambda e: e.activation(out=arow, in_=sr[:, 32 + dr * 16:48 + dr * 16], func=AF.Exp), r=["sr"], w=["arow0"])
        mk.op("dve", lambda e: e.tensor_scalar(out=arow, in0=arow, scalar1=-1.0, scalar2=None, op0=ALU.mult), r=["arow0"], w=["arow"])
        S = A.alloc([128, 1024], F32)
        Sbf = A.alloc([128, 1024], BF16)
        Wt = A.alloc([128, 16, 128], BF16)
        xw = A.alloc([128, 1024], BF16)
        xdt = A.alloc([128, 1024], BF16)
        yd = A.alloc([128, 1024], F32)
        y = A.alloc([128, 1024], F32)
        adtU = A.alloc([128, 16, 128], F32)
        D2 = lambda shape, dt_: [A.alloc(shape, dt_) for _ in range(2)]
        D3 = lambda shape, dt_: [A.alloc(shape, dt_) for _ in range(3)]
        dtr = D3([128, 32], F32)
        dt = D2([128, 16], F32)
        ldt = D2([128, 16], F32)
        adt = D2([128, 16], F32)
        ex = D2([128, 48], F32)
        wdec = D2([128, 16], F32)
        Lm = D2([128, 16, 128], BF16)
        SM = D2([128, 4, 128], BF16)
        xtok = D2([128, 1024], BF16)
        btok = D2([128, 512], BF16)
        if dr == 0:
            dg = A.alloc([128, 80, 128], BF16)
            for j in range(16):
                for k in range(5):
                    cb = PV_CONV + (l * 16 + j) * 6 + k
                    eng = ("pool", "dve", "act")[(j * 5 + k) % 3]
                    if eng == "act":
                        mk.op("act", lambda e, j=j, k=k, cb=cb: e.activation(out=dg[:, j * 5 + k, :], in_=ident, func=AF.Copy, scale=pv[:, cb:cb + 1]),
                              r=["cs", "pv"], w=[("dg", j)])
                    else:
                        mk.op(eng, lambda e, j=j, k=k, cb=cb: e.tensor_scalar(out=dg[:, j * 5 + k, :], in0=ident, scalar1=pv[:, cb:cb + 1],
                                                                          scalar2=None, op0=ALU.mult), r=["cs", "pv"], w=[("dg", j)])
            xh = D3([128, 16, 132], BF16)
            xc = D2([128, 16, 128], BF16)
        else:
            bc = D3([128, 8, 128], BF16)
            yprev = D3([128, 1024], F32)
            zt = D3([128, 1024], BF16)
            xtok3 = D3([128, 1024], BF16)
            btok3 = D3([128, 512], BF16)
            zs = A.alloc([128, 1024], F32)
            tmpd = A.alloc([128, 1024], F32)
            ss = A.alloc([128, 2], F32)
            yn = A.alloc([128, 1024], BF16)
            yT = A.alloc([128, 8, 128], BF16)
        NT = T // 128
        tps = SEGL // 128
        order = list(range(NT)) if dr == 0 else list(range(NT - 1, -1, -1))

        def stage0(n, t):
            q3 = n % 3
            t0 = t * 128
            seg = t // tps
            K3 = lambda nm: (nm, "q", q3)
            mk.op("sp", lambda e: e.dma_start(out=dtr[q3], in_=dtS[t0:t0 + 128, :]), r=[("dtS", t)], w=[K3("dtr")], dma=True)
            if dr == 0:
                xh_ = xh[q3]
                mk.op("sp", lambda e: e.dma_start(out=xh_[:, :, 2:130], in_=xbcS[:, :, t0:t0 + 128].rearrange("c p t -> p c t")),
                      r=[("xbcS", t0 // TB)], w=[K3("xh")], dma=True)
                for side in range(2):
                    at_edge = (t % tps == 0) if side == 0 else (t % tps == tps - 1)
                    lk = at_edge and ((seg == 1 and side == 0) or (seg == 0 and side == 1))
                    dcols = slice(0, 2) if side == 0 else slice(130, 132)
                    src0 = t0 - 2 if side == 0 else t0 + 128
                    if at_edge and not lk:
                        mk.op("pool", lambda e, dcols=dcols: e.memset(xh_[:, :, dcols], 0.0), w=[("xhh", q3, side)])
                    else:
                        nb_ = (src0 // TB)
                        mk.op("sp", lambda e, dcols=dcols, src0=src0: e.dma_start(
                            out=xh_[:, :, dcols], in_=xbcS[:, :, src0:src0 + 2].rearrange("c p t -> p c t")),
                            r=[("xbcS", nb_)], w=[("xhh", q3, side)], dma=True)
                        if lk:
                            mk.op("pool", lambda e, dcols=dcols: e.tensor_scalar(out=xh_[:, :, dcols], in0=xh_[:, :, dcols],
                                                                              scalar1=pv[:, PV_LINK:PV_LINK + 1], scalar2=None, op0=ALU.mult),
                                  r=[("xhh", q3, side), "pv"], w=[("xhh", q3, side)])
            else:
                mk.op("sp", lambda e: e.dma_start(out=bc[q3], in_=xcS[:, :, t0:t0 + 128].rearrange("c p t -> p c t")),
                      r=[("xcS", t)], w=[K3("bc")], dma=True)
                mk.op("sp", lambda e: e.dma_start(out=xtok3[q3], in_=xtokS[t0:t0 + 128, :]), r=[("xtokS", t)], w=[K3("xtok")], dma=True)
                mk.op("sp", lambda e: e.dma_start(out=btok3[q3], in_=btokS[t0:t0 + 128, :]), r=[("btokS", t)], w=[K3("btok")], dma=True)
                mk.op("sp", lambda e: e.dma_start(out=yprev[q3], in_=yaccS[t0:t0 + 128, :]), r=[("yaccS", t)], w=[K3("yprev")], dma=True)
                mk.op("sp", lambda e: e.dma_start(out=zt[q3], in_=zS[t0:t0 + 128, :]), r=[("zS", t)], w=[K3("zt")], dma=True)

        def stage1(n, t):
            p = n % 2
            q3 = n % 3
            t0 = t * 128
            seg = t // tps
            K = lambda nm: (nm, p)
            K3 = lambda nm: (nm, "q", q3)
            if dr == 0:
                xh_, xc_ = xh[q3], xc[p]
                xk = [K3("xh"), ("xhh", q3, 0), ("xhh", q3, 1)]
                for j in range(16):
                    bk = j % 4
                    for k in range(5):
                        mk.op("pe", lambda e, j=j, k=k, bk=bk: e.matmul(psb[bk][:, (j // 4) * 128:(j // 4 + 1) * 128], lhsT=dg[:, j * 5 + k, :],
                                                                    rhs=xh_[:, j, k:k + 128], start=(k == 0), stop=(k == 4)),
                              r=xk + [("dg", j)], w=[("ps", bk)])
                    cb = PV_CONV + (l * 16 + j) * 6 + 5
                    mk.op("act", lambda e, j=j, bk=bk, cb=cb: e.activation(out=xc_[:, j, :], in_=psb[bk][:, (j // 4) * 128:(j // 4 + 1) * 128],
                                                                       func=AF.Silu, bias=pv[:, cb:cb + 1]),
                          r=[("ps", bk), "pv"], w=[("xc", p, j // 8)])
                    yield
                mk.op("sp", lambda e: e.dma_start(out=xcS[:, :, t0:t0 + 128].rearrange("c p t -> p c t"), in_=xc_[:, 8:16, :]),
                      r=[("xc", p, 1)], w=[("xcS", t)], dma=True)
                pv_ = psb[0][:].bitcast(BF16)
                for j in range(8):
                    mk.op("pe", lambda e, j=j: e.transpose(out=pv_[:, j * 128:(j + 1) * 128], in_=xc_[:, j, :], identity=ident_bf[:]),
                          r=[("xc", p, 0), "ident_bf"], w=[("ps", 0)])
                mk.op("act", lambda e: e.copy(out=xtok[p], in_=pv_), r=[("ps", 0)], w=[K("xtok")])
                yield
                pb_ = psb[1][:].bitcast(BF16)
                for j in range(4):
                    mk.op("pe", lambda e, j=j: e.transpose(out=pb_[:, j * 128:(j + 1) * 128], in_=xc_[:, 8 + j, :], identity=ident_bf[:]),
                          r=[("xc", p, 1), "ident_bf"], w=[("ps", 1)])
                mk.op("dve", lambda e: e.tensor_copy(out=btok[p], in_=pb_[:, 0:512]), r=[("ps", 1)], w=[K("btok")])
                yield
                mk.op("sp", lambda e: e.dma_start(out=xtokS[t0:t0 + 128, :], in_=xtok[p]), r=[K("xtok")], w=[("xtokS", t)], dma=True)
                mk.op("sp", lambda e: e.dma_start(out=btokS[t0:t0 + 128, :], in_=btok[p]), r=[K("btok")], w=[("btokS", t)], dma=True)
                BC = xc_[:, 8:16, :]
                bck = [("xc", p, 1)]
            else:
                BC = bc[q3]
                bck = [K3("bc")]
            dt_, adt_, ex_, ldt_ = dt[p], adt[p], ex[p], ldt[p]
            mk.op("dve", lambda e: e.tensor_tensor(out=dt_, in0=dtr[q3][:, dr * 16:(dr + 1) * 16], in1=sr[:, dr * 16:(dr + 1) * 16], op=ALU.add),
                  r=[K3("dtr"), "sr"], w=[K("dt0")])
            mk.op("act", lambda e: e.activation(out=dt_, in_=dt_, func=AF.Exp), r=[K("dt0")], w=[K("dt1")])
            mk.op("act", lambda e: e.activation(out=dt_, in_=dt_, func=AF.Ln, bias=cs[:, 130:131], scale=1.0), r=[K("dt1"), "cs"], w=[K("dt")])
            mk.op("dve", lambda e: e.tensor_tensor(out=adt_, in0=dt_, in1=arow, op=ALU.mult), r=[K("dt"), "arow"], w=[K("adt")])
            yield
            mk.op("pe", lambda e: e.matmul(psb[2][:, 0:16], lhsT=U, rhs=adt_, start=True, stop=True), r=[K("adt"), "sc"], w=[("ps", 2)])
            mk.op("pe", lambda e: e.matmul(psb[2][:, 16:32], lhsT=SL, rhs=adt_, start=True, stop=True), r=[K("adt"), "sc"], w=[("ps", 2)])
            mk.op("pe", lambda e: e.matmul(psb[2][:, 32:48], lhsT=ones_f, rhs=adt_, start=True, stop=True), r=[K("adt"), "ones_f"], w=[("ps", 2)])
            mk.op("act", lambda e: e.activation(out=ex_, in_=psb[2][:, 0:48], func=AF.Exp), r=[("ps", 2)], w=[K("ex")])
            mk.op("dve", lambda e: e.tensor_tensor(out=wdec[p], in0=dt_, in1=ex_[:, 16:32], op=ALU.mult), r=[K("dt"), K("ex")], w=[K("wdec")])
            yield
            mk.op("dve", lambda e: e.tensor_tensor(out=adtU, in0=U.unsqueeze(1).broadcast_to([128, 16, 128]),
                                                   in1=adt_.unsqueeze(2).broadcast_to([128, 16, 128]), op=ALU.mult),
                  r=[K("adt"), "sc"], w=[("adtU", q) for q in range(4)])
            yield
            for q in range(4):
                mk.op("pe", lambda e, q=q: e.matmul(psb[q][:], lhsT=SL, rhs=adtU[:, 4 * q:4 * q + 4, :].rearrange("p h l -> p (h l)"),
                                                  start=True, stop=True), r=[("adtU", q), "sc"], w=[("ps", q)])
                mk.op("act", lambda e, q=q: e.activation(out=Lm[p][:, 4 * q:4 * q + 4, :].rearrange("p h l -> p (h l)"), in_=psb[q][:], func=AF.Exp),
                      r=[("ps", q)], w=[("Lm", p, q)])
                yield
            for g in range(4):
                mk.op("pe", lambda e, g=g: e.matmul(psb[3][:, g * 128:(g + 1) * 128], lhsT=BC[:, g, :], rhs=BC[:, 4 + g, :], start=True, stop=True),
                      r=bck, w=[("ps", 3)])
            mk.op("dve", lambda e: e.tensor_tensor(out=SM[p].rearrange("p g l -> p (g l)"), in0=psb[3][:], in1=mask4, op=ALU.mult),
                  r=[("ps", 3), "sc"], w=[K("SM")])
            yield
            return

        def stage2(n, t):
            p = n % 2
            t0 = t * 128
            seg = t // tps
            K = lambda nm: (nm, p)
            if dr == 0:
                BC = xc[p][:, 8:16, :]
                bck = [("xc", p, 1)]
            else:
                BC = bc[n % 3]
                bck = [("bc", "q", n % 3)]
            first_in_seg = (t % tps == 0) if dr == 0 else (t % tps == tps - 1)
            if first_in_seg:
                linked = (seg == 1) if dr == 0 else (seg == 0)
                if linked:
                    mk.op("dve", lambda e: e.tensor_scalar(out=S, in0=S, scalar1=pv[:, PV_LINK:PV_LINK + 1], scalar2=None, op0=ALU.mult),
                          r=["S", "pv"], w=["S"])
                else:
                    mk.op("dve", lambda e: e.memset(S, 0.0), r=["S"], w=["S"])
            q3 = n % 3
            K3 = lambda nm: (nm, "q", q3)
            if dr == 0:
                ex_, xtok_, btok_ = ex[p], xtok[p], btok[p]
                kx, kb = ("xtok", p), ("btok", p)
            else:
                ex_, xtok_, btok_ = ex[p], xtok3[q3], btok3[q3]
                kx, kb = K3("xtok"), K3("btok")
            mk.op("dve", lambda e: e.tensor_tensor(out=Wt.rearrange("p (g a) l -> p g a l", g=4), in0=Lm[p].rearrange("p (g a) l -> p g a l", g=4),
                                                   in1=SM[p].unsqueeze(2).broadcast_to([128, 4, 4, 128]), op=ALU.mult),
                  r=[("Lm", p, q) for q in range(4)] + [K("SM")], w=["Wt"])
            mk.op("dve", lambda e: e.tensor_tensor(out=xdt.rearrange("p (h d) -> p h d", h=16), in0=xtok_.rearrange("p (h d) -> p h d", h=16),
                                                   in1=dt[p].unsqueeze(2).broadcast_to([128, 16, 64]), op=ALU.mult),
                  r=[kx, K("dt")], w=["xdt"])
            yield
            for h in range(16):
                mk.op("pe", lambda e, h=h: e.matmul(psb[4 + h // 8][:, (h % 8) * 64:(h % 8 + 1) * 64], lhsT=Wt[:, h, :], rhs=xdt[:, h * 64:(h + 1) * 64],
                                                  start=True, stop=True), r=["Wt", "xdt"], w=[("ps", 4 + h // 8)])
                if h % 4 == 3:
                    yield
            mk.op("act", lambda e: e.copy(out=Sbf, in_=S), r=["S"], w=["Sbf"])
            for g in range(4):
                mk.op("pe", lambda e, g=g: e.matmul(psb[6 + g // 2][:, (g % 2) * 256:(g % 2 + 1) * 256], lhsT=BC[:, 4 + g, :],
                                                  rhs=Sbf[:, g * 256:(g + 1) * 256], start=True, stop=True),
                      r=bck + ["Sbf"], w=[("ps", 6 + g // 2)])
            yield
            mk.op("dve", lambda e: e.tensor_tensor(out=y.rearrange("p (h d) -> p h d", h=16), in0=psw[3][:].rearrange("p (h d) -> p h d", h=16),
                                                   in1=ex_[:, 0:16].unsqueeze(2).broadcast_to([128, 16, 64]), op=ALU.mult),
                  r=[("ps", 6), ("ps", 7), K("ex")], w=["y0"])
            mk.op("dve", lambda e: e.tensor_tensor(out=y, in0=psw[2][:], in1=y, op=ALU.add), r=[("ps", 4), ("ps", 5), "y0"], w=["yfull"])
            mk.op("dve", lambda e: e.tensor_tensor(out=xw.rearrange("p (h d) -> p h d", h=16), in0=xtok_.rearrange("p (h d) -> p h d", h=16),
                                                   in1=wdec[p].unsqueeze(2).broadcast_to([128, 16, 64]), op=ALU.mult),
                  r=[kx, K("wdec")], w=["xw"])
            yield
            for g in range(4):
                mk.op("pe", lambda e, g=g: e.matmul(psb[4 + g // 2][:, (g % 2) * 256:(g % 2 + 1) * 256], lhsT=btok_[:, g * 128:(g + 1) * 128],
                                                  rhs=xw[:, g * 256:(g + 1) * 256], start=True, stop=True),
                      r=[kb, "xw"], w=[("ps", 4 + g // 2)])
            mk.op("dve", lambda e: e.tensor_tensor(out=S.rearrange("p (h d) -> p h d", h=16), in0=S.rearrange("p (h d) -> p h d", h=16),
                                                   in1=ex_[:, 32:48].unsqueeze(2).broadcast_to([128, 16, 64]), op=ALU.mult),
                  r=[K("ex"), "S", "Sbf"], w=["S"])
            mk.op("dve", lambda e: e.tensor_tensor(out=S, in0=psw[2][:], in1=S, op=ALU.add), r=[("ps", 4), ("ps", 5), "S"], w=["S"])
            yield
            yk = ["yfull"]
            if dr == 0:
                mk.op("sp", lambda e: e.dma_start(out=yaccS[t0:t0 + 128, :], in_=y), r=yk, w=[("yaccS", t)], dma=True)
            else:
                mk.op("dve", lambda e: e.tensor_tensor(out=tmpd, in0=xtok_, in1=sr[:, 1104:2128], op=ALU.mult), r=[kx, "sr"], w=["tmpd"])
                mk.op("dve", lambda e: e.tensor_tensor(out=tmpd, in0=tmpd, in1=yprev[q3], op=ALU.add), r=["tmpd", K3("yprev")], w=["tmpd2"])
                mk.op("dve", lambda e: e.tensor_tensor(out=y, in0=y, in1=tmpd, op=ALU.add), r=yk + ["tmpd2"], w=["y3"])
                mk.op("act", lambda e: e.activation(out=zs, in_=zt[q3], func=AF.Silu), r=[K3("zt")], w=["zs"])
                mk.op("dve", lambda e: e.tensor_tensor(out=y, in0=y, in1=zs, op=ALU.mult), r=["y3", "zs"], w=["y4"])
                yield
                mk.op("act", lambda e: e.activation(out=zs, in_=y, func=AF.Square, accum_out=ss[:, 0:1]), r=["y4", "zs"], w=["ss0"])
                mk.op("act", lambda e: e.activation(out=ss[:, 1:2], in_=ss[:, 0:1], func=AF.Ln, scale=1.0 / 1024, bias=cs[:, 128:129]),
                      r=["ss0", "cs"], w=["ss1"])
                mk.op("act", lambda e: e.activation(out=ss[:, 1:2], in_=ss[:, 1:2], func=AF.Exp, scale=-0.5), r=["ss1"], w=["ss2"])
                mk.op("dve", lambda e: e.scalar_tensor_tensor(out=yn, in0=y, scalar=ss[:, 1:2], in1=sr[:, 80:80 + 1024], op0=ALU.mult, op1=ALU.mult),
                      r=["y4", "ss2", "sr"], w=["yn"])
                yield
                po_ = psb[6][:].bitcast(BF16)
                for j in range(8):
                    mk.op("pe", lambda e, j=j: e.transpose(out=po_[:, j * 128:(j + 1) * 128], in_=yn[:, j * 128:(j + 1) * 128], identity=ident_bf[:]),
                          r=["yn", "ident_bf"], w=[("ps", 6)])
                mk.op("act", lambda e: e.copy(out=yT.rearrange("p c t -> p (c t)"), in_=po_), r=[("ps", 6)], w=["yT"])
                mk.op("sp", lambda e: e.dma_start(out=oSS[:, :, t0:t0 + 128].rearrange("c p t -> p c t"), in_=yT),
                      r=["yT"], w=[("oSS", t0 // TB)], dma=True)

        def drive(*gens):
            gens = [x for x in gens if x is not None]
            while gens:
                for x in list(gens):
                    try:
                        next(x)
                    except StopIteration:
                        gens.remove(x)

        stage0(0, order[0])
        if NT > 1:
            stage0(1, order[1])
        drive(stage1(0, order[0]))
        for n, t in enumerate(order):
            if n + 2 < NT:
                stage0(n + 2, order[n + 2])
            drive(stage1(n + 1, order[n + 1]) if n + 1 < NT else None, stage2(n, t))

    def merge_phase(l, branches):
        phase()
        wg = A.alloc([128, 8, 3 * D], BF16)
        wbr = {"ssd": A.alloc([128, 8, D], BF16), "gla": A.alloc([128, 4, D], BF16), "att": A.alloc([128, 4, D], BF16)}
        wo = A.alloc([128, 8, D], BF16)
        stg = [A.alloc([128, PIECE], F32) for i in range(2)]
        xb2 = [A.alloc([128, 8, TB], F32) for i in range(2)]
        hb2 = [A.alloc([128, 8, TB], BF16) for i in range(2)]
        sq = [A.alloc([128, TB], BF16) for i in range(2)]
        rstd2 = [A.alloc([128, TB], F32) for i in range(2)]
        bin_ = {"ssd": A.alloc([128, 8, TB], BF16), "gla": A.alloc([128, 4, TB], BF16), "att": A.alloc([128, 4, TB], BF16)}
        sg = [A.alloc([128, TB], F32) for i in range(2)]
        macc = [A.alloc([128, TB], F32) for i in range(2)]
        mg = A.alloc([128, 8, TB], BF16)
        xo = [A.alloc([128, TB], F32) for i in range(2)]
        bidx = {"ssd": 0, "gla": 1, "att": 2}
        bsrc = {"ssd": w_br_ssd, "gla": w_br_gla, "att": w_br_attn}
        bkc = {"ssd": 8, "gla": 4, "att": 4}
        for br in branches:
            gi = bidx[br]
            load_weight(wg[:, :, gi * D:(gi + 1) * D], w_in[l][:, C_G + gi * D:C_G + (gi + 1) * D], 8, D, stg,
                        scale_col0=PV_NORM1 + 8 * l, tag="wg" + br)
            load_weight(wbr[br], bsrc[br][l], bkc[br], D, stg, tag="wbr" + br)
        load_weight(wo, w_out[l], 8, D, stg, tag="wo")
        itc = [0]
        load_xblock(xb2[0], 0, 0)
        rms_block(xb2[0], hb2[0], sq, rstd2[0], 0, 0)

        def block(b, par, xb, hb):
            if b + 1 < NB:
                load_xblock(xb2[1 - par], b + 1, 1 - par)
            for br in branches:
                if br == "ssd":
                    mk.op("sp", lambda e, b=b: e.dma_start(
                        out=bin_["ssd"], in_=oSS[:, :, b * TB:(b + 1) * TB].rearrange("c p t -> p c t")),
                        r=[("oSS", b)], w=[("bin", "ssd")], dma=True)
                if br == "gla":
                    mk.op("sp", lambda e, b=b: e.dma_start(
                        out=bin_["gla"], in_=oG[:, :, b * TB:(b + 1) * TB].rearrange("c p t -> p c t")),
                        r=[("oG", b)], w=[("bin", "gla")], dma=True)
                if br == "att":
                    mk.op("sp", lambda e, b=b: e.dma_start(
                        out=bin_["att"], in_=oS[:, :, b * TB:(b + 1) * TB].rearrange("(c two) d t -> (two d) c t", two=2)),
                        r=[("oS", b, h) for h in range(8)], w=[("bin", "att")], dma=True)
            for m in range(8):
                mi = m % 2
                for bi, br in enumerate(branches):
                    gi = bidx[br]
                    i = itc[0] % 2
                    itc[0] += 1
                    pgt, pbr = 1 + i, 3 + i
                    for c in range(8):
                        mk.op("pe", lambda e, c=c, m=m, gi=gi, pgt=pgt: e.matmul(
                            psb[pgt][:], lhsT=wg[:, c, gi * D + m * 128:gi * D + (m + 1) * 128], rhs=hb[:, c, :],
                            start=(c == 0), stop=(c == 7)),
                            r=[("hb", par, c), ("W", "wg" + br, c, 0)], w=[("ps", pgt)])
                    mk.op("act", lambda e, i=i, pgt=pgt: e.activation(out=sg[i], in_=psb[pgt][:], func=AF.Sigmoid),
                          r=[("ps", pgt)], w=[("sg", i)])
                    kc = bkc[br]
                    for c in range(kc):
                        mk.op("pe", lambda e, c=c, m=m, br=br, pbr=pbr, kc=kc: e.matmul(
                            psb[pbr][:], lhsT=wbr[br][:, c, m * 128:(m + 1) * 128], rhs=bin_[br][:, c, :],
                            start=(c == 0), stop=(c == kc - 1)),
                            r=[("bin", br), ("W", "wbr" + br, c, 0)], w=[("ps", pbr)])
                    last = (bi == len(branches) - 1)
                    if bi == 0:
                        dst = mg[:, m, :] if last else macc[mi]
                        mk.op("dve", lambda e, i=i, pbr=pbr, dst=dst: e.tensor_tensor(out=dst, in0=psb[pbr][:], in1=sg[i], op=ALU.mult),
                              r=[("ps", pbr), ("sg", i)], w=[("mg", m) if last else ("macc", mi)])
                    else:
                        mk.op("dve", lambda e, i=i, pbr=pbr: e.tensor_tensor(out=sg[i], in0=psb[pbr][:], in1=sg[i], op=ALU.mult),
                              r=[("ps", pbr), ("sg", i)], w=[("sg", i)])
                        dst = mg[:, m, :] if last else macc[mi]
                        mk.op("dve", lambda e, i=i, mi=mi, dst=dst: e.tensor_tensor(out=dst, in0=macc[mi], in1=sg[i], op=ALU.add),
                              r=[("macc", mi), ("sg", i)], w=[("mg", m) if last else ("macc", mi)])
            if b + 1 < NB:
                rms_block(xb2[1 - par], hb2[1 - par], sq, rstd2[1 - par], 0, 1 - par)
            for m in range(8):
                po = 5 + m % 3
                for c in range(8):
                    mk.op("pe", lambda e, c=c, m=m, po=po: e.matmul(psb[po][:], lhsT=wo[:, c, m * 128:(m + 1) * 128], rhs=mg[:, c, :],
                                                                 start=(c == 0), stop=(c == 7)),
                          r=[("mg", c), ("W", "wo", c, 0)], w=[("ps", po)])
                xoi = xo[m % 2]
                mk.op("dve", lambda e, m=m, po=po, xoi=xoi: e.tensor_tensor(out=xoi, in0=psb[po][:], in1=xb[:, m, :], op=ALU.add),
                      r=[("ps", po), ("xb", par, m)], w=[("xo", m % 2)])
                mk.op("sp", lambda e, m=m, b=b, xoi=xoi: e.dma_start(out=xT[m, :, b * TB:(b + 1) * TB], in_=xoi),
                      r=[("xo", m % 2)], w=[("xT", b)], dma=True)

        for b in range(NB):
            block(b, b % 2, xb2[b % 2], hb2[b % 2])

    def ffn_phase(l):
        phase()
        wfi = A.alloc([128, 8, 2 * DFF], BF16)
        wfo = A.alloc([128, 22, D], BF16)
        stg = [A.alloc([128, PIECE], F32) for i in range(2)]
        xb = A.alloc([128, 8, TB], F32)
        hb = A.alloc([128, 8, TB], BF16)
        hid = A.alloc([128, 22, TB], BF16)
        sq = [A.alloc([128, TB], BF16) for i in range(2)]
        rstd = A.alloc([128, TB], F32)
        tmpf = [A.alloc([128, TB], F32) for i in range(2)]
        xo = [A.alloc([128, TB], F32) for i in range(2)]
        load_weight(wfi, w_ffn_in[l], 8, 2 * DFF, stg, scale_col0=PV_NORM2 + 8 * l, tag="ffn_in")
        load_weight(wfo, w_ffn_out[l], 22, D, stg, tag="ffn_out")
        for b in range(NB):
            load_xblock(xb, b)
            rms_block(xb, hb, sq, rstd, 0)
            for j in range(22):
                pg = 1 + (2 * j) % 6
                pu = pg + 1
                for c in range(8):
                    mk.op("pe", lambda e, c=c, j=j, pg=pg: e.matmul(psb[pg][:], lhsT=wfi[:, c, j * 128:(j + 1) * 128],
                                                                 rhs=hb[:, c, :], start=(c == 0), stop=(c == 7)),
                          r=[("hb", 0, c), ("W", "ffn_in", c, (j * 128) // PIECE)], w=[("ps", pg)])
                for c in range(8):
                    mk.op("pe", lambda e, c=c, j=j, pu=pu: e.matmul(psb[pu][:], lhsT=wfi[:, c, DFF + j * 128:DFF + (j + 1) * 128],
                                                                 rhs=hb[:, c, :], start=(c == 0), stop=(c == 7)),
                          r=[("hb", 0, c), ("W", "ffn_in", c, (DFF + j * 128) // PIECE)], w=[("ps", pu)])
                tf = tmpf[j % 2]
                mk.op("act", lambda e, pg=pg, tf=tf: e.activation(out=tf, in_=psb[pg][:], func=AF.Silu),
                      r=[("ps", pg)], w=[("tmpf", j % 2)])
                mk.op("dve", lambda e, pu=pu, tf=tf, j=j: e.tensor_tensor(out=hid[:, j, :], in0=psb[pu][:], in1=tf, op=ALU.mult),
                      r=[("ps", pu), ("tmpf", j % 2)], w=[("hid", j)])
            for m in range(8):
                po = 1 + m % 7
                for j in range(22):
                    mk.op("pe", lambda e, m=m, j=j, po=po: e.matmul(psb[po][:], lhsT=wfo[:, j, m * 128:(m + 1) * 128],
                                                                 rhs=hid[:, j, :], start=(j == 0), stop=(j == 21)),
                          r=[("hid", j), ("W", "ffn_out", j, 0)], w=[("ps", po)])
                xoi = xo[m % 2]
                mk.op("dve", lambda e, m=m, po=po, xoi=xoi: e.tensor_tensor(out=xoi, in0=psb[po][:], in1=xb[:, m, :], op=ALU.add),
                      r=[("ps", po), ("xb", 0, m)], w=[("xo", m % 2)])
                mk.op("sp", lambda e, m=m, b=b, xoi=xoi: e.dma_start(out=xT[m, :, b * TB:(b + 1) * TB], in_=xoi),
                      r=[("xo", m % 2)], w=[("xT", b)], dma=True)


    def final_phase():
        phase()
        xb2 = [A.alloc([128, 8, TB], F32) for i in range(2)]
        sq = [A.alloc([128, TB], BF16) for i in range(2)]
        rstd2 = [A.alloc([128, TB], F32) for i in range(2)]
        yt = [A.alloc([128, D], F32) for i in range(4)]
        outs = [dbg_o]
        load_xblock(xb2[0], 0, 0)

        def block(b, par, xb, rstd):
            if b + 1 < NB:
                load_xblock(xb2[1 - par], b + 1, 1 - par)
            rms_block(xb, None, sq, rstd, 0, par)
            for c in range(8):
                mk.op("dve", lambda e, c=c: e.scalar_tensor_tensor(out=xb[:, c, :], in0=xb[:, c, :], scalar=pv[:, PV_FINAL + c:PV_FINAL + c + 1],
                                                               in1=rstd, op0=ALU.mult, op1=ALU.mult),
                      r=[("xb", par, c), ("rstd", par), "pv"], w=[("xb", par, c)])
            for tt in range(4):
                i = (b * 4 + tt) % 4
                for half in range(2):
                    pb = 1 + ((b * 4 + tt) * 2 + half) % 7
                    for c4 in range(4):
                        c = half * 4 + c4
                        mk.op("pe", lambda e, c=c, c4=c4, tt=tt, pb=pb: e.transpose(
                            out=psb[pb][:, c4 * 128:(c4 + 1) * 128], in_=xb[:, c, tt * 128:(tt + 1) * 128], identity=ident),
                            r=[("xb", par, c), "cs"], w=[("ps", pb)])
                    evac(("act", "dve")[half], yt[i][:, half * 512:(half + 1) * 512], psb[pb][:], r=[("ps", pb)], w=[("yt", i, half)])
                t = b * 4 + tt
                o = mk.op("sp", lambda e, t=t, i=i: e.dma_start(out=y_out[t * 128:(t + 1) * 128, :], in_=yt[i]),
                          r=[("yt", i, 0), ("yt", i, 1)], w=[("yout", t)], dma=True)
                outs.append(o)

        for b in range(NB):
            block(b, b % 2, xb2[b % 2], rstd2[b % 2])
        return outs

    for l in range(DEPTH):
        if STAGE >= 1:
            inproj_phase(l)
            attention_phase(l)
            brs = ["att"]
            if STAGE >= 2:
                gla_pass(l, 0)
                gla_pass(l, 1)
                brs = ["gla", "att"]
            if STAGE >= 3:
                ssd_pass(l, 0)
                ssd_pass(l, 1)
                brs = ["ssd", "gla", "att"]
            merge_phase(l, brs)
        if STAGE >= 0:
            ffn_phase(l)
    outs = final_phase()
    mk.emit(final_waits=outs)
    return nc


PV_NORM1 = 0
PV_NORM2 = 16
PV_FINAL = 32
PV_QW = 40
PV_KW = 42
PV_LINK = 44
PV_GLAW = 46
NGC = 1792
PV_CONV = 64
NSC = 1536
NSR = 2128
NPV = 256
NCST = 392


def _pack_pvec(inp):
    pv = np.zeros((128, NPV), np.float32)
    for l in range(DEPTH):
        pv[:, PV_NORM1 + 8 * l:PV_NORM1 + 8 * l + 8] = inp["norm1_w"][l].reshape(8, 128).T
        pv[:, PV_NORM2 + 8 * l:PV_NORM2 + 8 * l + 8] = inp["norm2_w"][l].reshape(8, 128).T
    pv[:, PV_FINAL:PV_FINAL + 8] = inp["final_norm_w"].reshape(8, 128).T
    for l in range(DEPTH):
        for j in range(16):
            cb = PV_CONV + (l * 16 + j) * 6
            pv[:, cb:cb + 5] = inp["conv_w"][l][:, j * 128:(j + 1) * 128].T
            pv[:, cb + 5] = inp["conv_b"][l][j * 128:(j + 1) * 128]
        pv[:, PV_GLAW + l] = inp["gla_norm_w"][l]
        pv[:, PV_QW + l] = np.tile(inp["att_q_norm_w"][l], 2)
        pv[:, PV_KW + l] = np.tile(inp["att_k_norm_w"][l], 2)
    return pv


def _consts():
    c = np.zeros((128, NCST), np.float32)
    c[:, 0:128] = np.eye(128, dtype=np.float32)
    c[:, 128] = EPS
    c[:, 129] = 64 * EPS
    c[:, 130] = 1.0
    for m in range(128):
        if (m % 32) < 16:
            c[m + 16, 136 + m] = -1.0
        else:
            c[m - 16, 136 + m] = 1.0
    for m in range(128):
        c[(m // 64) * 64:(m // 64 + 1) * 64, 264 + m] = 1.0
    return c


def _rope_tables(core):
    t = np.arange(T)
    seg = t // SEGL
    pos = t % SEGL + np.where((seg == 1) & (core < 4), SEGL, 0)
    row = (pos // 64).astype(np.float32)
    col = (pos % 64).astype(np.float32)
    inv = (np.float32(10000.0) ** (-np.arange(16, dtype=np.float32) / np.float32(16))).astype(np.float32)
    p = np.arange(128)
    d = p % 64
    sec = d // 32
    f = d % 16
    axis = np.where(sec[:, None] == 0, row[None, :], col[None, :]).astype(np.float32)
    ang = (axis * inv[f][:, None]).astype(np.float32)
    return np.cos(ang).astype(np.float32), np.sin(ang).astype(np.float32)


def _ssd_consts():
    g = np.zeros((128, NSC), np.float32)
    t = np.arange(128)[:, None]
    x = np.arange(128)[None, :]
    g[:, 0:128] = (t <= x)
    g[:, 128:256] = (t > x)
    g[:, 256:384] = (t >= x)
    g[:, 384:512] = (t < x)
    g[:, 512:1024] = np.tile((x >= t).astype(np.float32), (1, 4))
    g[:, 1024:1536] = np.tile((x <= t).astype(np.float32), (1, 4))
    return g


def _ssd_rows(inp):
    r = np.zeros((DEPTH, 128, NSR), np.float32)
    for l in range(DEPTH):
        row = np.concatenate([inp["ssd_dt_bias_f"][l], inp["ssd_dt_bias_b"][l], inp["ssd_a_log_f"][l], inp["ssd_a_log_b"][l],
                              inp["ssd_d"][l], inp["ssd_norm_w"][l], np.repeat(inp["ssd_d"][l], 64)]).astype(np.float32)
        r[l, :, :] = row[None, :]
    return r


def _gla_consts():
    g = np.zeros((128, NGC), np.float32)
    t = np.arange(128)[:, None]
    l_ = np.arange(128)[None, :]
    same = (t // 64) == (l_ // 64)
    for dr in range(2):
        if dr == 0:
            U = same & (t <= l_)
            R = same & ((t % 64) <= 32)
            mask = same & (l_ >= t)
        else:
            U = same & (t >= l_)
            R = same & ((t % 64) >= 31)
            mask = same & (l_ <= t)
        J = same
        o0 = dr * 896
        g[:, o0:o0 + 128] = -(U.astype(np.float32) - R.astype(np.float32)) / 16.0
        g[:, o0 + 128:o0 + 256] = -U.astype(np.float32) / 16.0
        g[:, o0 + 256:o0 + 384] = -(J.astype(np.float32) - U.astype(np.float32)) / 16.0
        g[:, o0 + 384:o0 + 896] = np.tile(mask.astype(np.float32), (1, 4))
    return g


_NC_CACHE = {}


def kernel(**inputs):
    inp = {k: np.asarray(v) for k, v in inputs.items()}
    xp = inp["x_prompt"]
    xs = inp["x_sample"]
    if "nc" not in _NC_CACHE:
        _NC_CACHE["nc"] = build_program()
    nc = _NC_CACHE["nc"]
    pv = _pack_pvec(inp)
    cst = _consts()
    gcs = _gla_consts()
    scs = _ssd_consts()
    srw = _ssd_rows(inp)
    in_maps = []
    for c in range(NCORES):
        if c < 4:
            xc = np.concatenate([xs[c], xp[c]], axis=0)
        else:
            j = 4 + 3 * (c - 4)
            xc = np.concatenate([xp[j], xp[j + 1], xp[j + 2]], axis=0)
        pvc = pv.copy()
        pvc[:, PV_LINK] = 1.0 if c < 4 else 0.0
        rc, rs_ = _rope_tables(c)
        in_maps.append({
            "x": np.ascontiguousarray(xc, dtype=np.float32),
            "w_ffn_in": inp["w_ffn_in"], "w_ffn_out": inp["w_ffn_out"],
            "w_in": inp["w_in"], "w_br_attn": inp["w_br_attn"], "w_br_gla": inp["w_br_gla"],
            "w_br_ssd": inp["w_br_ssd"], "w_out": inp["w_out"],
            "ropec": rc, "ropes": rs_, "gcst": gcs, "scst": scs, "srow": srw,
            "gla_w2_f": inp["gla_w2_f"], "gla_w2_b": inp["gla_w2_b"], "gla_b_f": inp["gla_b_f"], "gla_b_b": inp["gla_b_b"],
            "pvec": pvc, "cst": cst,
        })
    res = run_bass_kernel_spmd(nc, in_maps, core_ids=list(range(NCORES)))
    yp = np.zeros_like(xp)
    ys = np.zeros_like(xs)
    for c in range(NCORES):
        y = np.asarray(res.results[c]["y"]).reshape(T, D)
        if c < 4:
            ys[c] = y[:2 * SEGL]
            yp[c] = y[2 * SEGL:]
        else:
            j = 4 + 3 * (c - 4)
            for s in range(3):
                yp[j + s] = y[s * SEGL:(s + 1) * SEGL]
    return (yp, ys)
```
